# Optimizing a Trainium2 kernel written in Bass

```python
import math
import jax
import jax.numpy as jnp
from jax import lax
import numpy as np

D_MODEL = 1024
BATCH = 32
SEQ = 256
DEPTH = 4
DEC_BATCH = 8
DEC_SEQ = 1024
PAST_LEN = 512

GRID_W = 64
EPS = 1e-6
N_EVEN = (DEPTH + 1) // 2
N_ODD = DEPTH // 2
SSD_HEADS = 16
SSD_HEAD_DIM = 64
D_SSD = SSD_HEADS * SSD_HEAD_DIM
SSD_GROUPS = 2
SSD_STATE = 64
SSD_CONV_W = 4
SSD_CHUNK = 128
SSD_CONV_CH = D_SSD + 2 * SSD_GROUPS * SSD_STATE
ATT_HEADS = 8
ATT_HALF_DIM = 64
ATT_V_DIM = 2 * ATT_HALF_DIM
D_ATT = ATT_HEADS * ATT_V_DIM
Q_BLOCK = 128
ROPE_THETA = 10000.0
AXIS_ROT_DIM = ATT_HALF_DIM // 2
D_RNN = 1024
RNN_BLOCKS = 16
RNN_BLOCK_W = D_RNN // RNN_BLOCKS
RNN_CONV_W = 4
RG_C = 8.0
D_FF = 2816
FFN_CONV_W = 3
D_IN_EVEN = D_SSD + SSD_CONV_CH + 2 * SSD_HEADS + 3 * D_ATT
D_IN_ODD = 2 * D_RNN

kernel_name = 'hybrid_ssd_diffattn_rglru_dit_step'


def rmsnorm(x, g):
    xf = x.astype(jnp.float32)
    y = xf * lax.rsqrt(jnp.mean(xf * xf, axis=-1, keepdims=True) + EPS)
    return (y * g.astype(jnp.float32)).astype(x.dtype)


def modulate(h, shift, scale):
    return h * (1.0 + scale) + shift


def ada_params(cvec, w_mod, b_mod):
    m = jax.nn.silu(cvec) @ w_mod + b_mod
    return jnp.split(m[:, None, :], 6, axis=-1)


def dwconv_centred(x, w, b):
    width = w.shape[0]
    left = width // 2
    right = width - 1 - left
    n = x.shape[1]
    xp = jnp.pad(x, ((0, 0), (left, right), (0, 0)))
    y = xp[:, 0:n] * w[0]
    for k in range(1, width):
        y = y + xp[:, k:k + n] * w[k]
    return y + b


def axial_rope_tables(n, dtype):
    rows = n // GRID_W
    row = jnp.repeat(jnp.arange(rows, dtype=jnp.float32), GRID_W)
    col = jnp.tile(jnp.arange(GRID_W, dtype=jnp.float32), rows)
    freqs = ROPE_THETA ** (-jnp.arange(0, AXIS_ROT_DIM, 2, dtype=jnp.float32) / AXIS_ROT_DIM)

    def tab(pos):
        ang = (pos[:, None] * freqs)[:, None, None, :]
        return jnp.cos(ang).astype(dtype), jnp.sin(ang).astype(dtype)

    cos_r, sin_r = tab(row)
    cos_c, sin_c = tab(col)
    return (cos_r, sin_r, cos_c, sin_c)


def rope_half(x, cos, sin):
    x1, x2 = jnp.split(x, 2, axis=-1)
    return jnp.concatenate([x1 * cos - x2 * sin, x1 * sin + x2 * cos], axis=-1)


def apply_axial_rope(x, tables):
    cos_r, sin_r, cos_c, sin_c = tables
    x_row, x_col = jnp.split(x, 2, axis=-1)
    return jnp.concatenate([rope_half(x_row, cos_r, sin_r), rope_half(x_col, cos_c, sin_c)], axis=-1)


def diff_attention(q, k, v, lam):
    b, nq, nh = q.shape[0], q.shape[1], q.shape[2]
    nb = nq // Q_BLOCK
    scale = ATT_HALF_DIM ** -0.5
    qb = jnp.moveaxis(q.reshape(b, nb, Q_BLOCK, nh, 2, ATT_HALF_DIM), 1, 0)

    def block(qi):
        s = jnp.einsum('bqhcd,bkhcd->bhcqk', qi, k, preferred_element_type=jnp.float32) * scale
        p = jax.nn.softmax(s, axis=-1)
        pd = (p[:, :, 0] - lam * p[:, :, 1]).astype(v.dtype)
        return jnp.einsum('bhqk,bkhe->bqhe', pd, v)

    o = lax.map(block, qb)
    return jnp.moveaxis(o, 0, 1).reshape(b, nq, nh, v.shape[-1])


def ssd_chunk_scan(x, dt, a, bm, cm, h0):
    f32 = jnp.float32
    b, n, nh, p = x.shape
    g, ns = bm.shape[2], bm.shape[3]
    hg = nh // g
    nc = n // SSD_CHUNK
    xc = x.reshape(b, nc, SSD_CHUNK, g, hg, p).astype(f32)
    dtc = dt.reshape(b, nc, SSD_CHUNK, g, hg)
    bc = bm.reshape(b, nc, SSD_CHUNK, g, ns).astype(f32)
    cc = cm.reshape(b, nc, SSD_CHUNK, g, ns).astype(f32)
    acs = jnp.cumsum(dtc * a.reshape(g, hg), axis=2)
    seg = acs[:, :, :, None] - acs[:, :, None, :]
    tril = jnp.tril(jnp.ones((SSD_CHUNK, SSD_CHUNK), dtype=bool))[:, :, None, None]
    lmat = jnp.exp(jnp.where(tril, seg, -jnp.inf))
    xdt = xc * dtc[..., None]
    cb = jnp.einsum('bcqgn,bckgn->bcqkg', cc, bc)
    y_diag = jnp.einsum('bcqkg,bcqkgh,bckghp->bcqghp', cb, lmat, xdt)
    decay_end = jnp.exp(acs[:, :, -1:] - acs)
    states = jnp.einsum('bckgn,bckgh,bckghp->bcghpn', bc, decay_end, xdt)
    chunk_decay = jnp.exp(acs[:, :, -1])

    def step(h, inp):
        dec, st = inp
        return dec[..., None, None] * h + st, h

    h_fin, prev = lax.scan(step, h0.reshape(b, g, hg, p, ns).astype(f32),
                           (jnp.moveaxis(chunk_decay, 1, 0), jnp.moveaxis(states, 1, 0)))
    prev = jnp.moveaxis(prev, 0, 1)
    y_off = jnp.einsum('bcqgn,bcghpn,bcqgh->bcqghp', cc, prev, jnp.exp(acs))
    y = (y_diag + y_off).reshape(b, n, nh, p).astype(x.dtype)
    return y, h_fin.reshape(b, nh, p, ns).astype(h0.dtype)


def ssd_bidirectional(xs, dt_raw, bm, cm, a_log, dt_bias, h0f, h0b):
    f32 = jnp.float32

    def flip(t):
        return jnp.flip(t, axis=1)

    a = -jnp.exp(a_log.astype(f32))
    dt_f = jax.nn.softplus(dt_raw[..., :SSD_HEADS].astype(f32) + dt_bias[0].astype(f32))
    dt_b = jax.nn.softplus(dt_raw[..., SSD_HEADS:].astype(f32) + dt_bias[1].astype(f32))
    y_f, h_f = ssd_chunk_scan(xs, dt_f, a[0], bm, cm, h0f)
    y_b, h_b = ssd_chunk_scan(flip(xs), flip(dt_b), a[1], flip(bm), flip(cm), h0b)
    return y_f + flip(y_b), h_f, h_b


def linear_scan(a, u, h0):
    u = u.at[:, 0].add(a[:, 0] * h0)

    def combine(lo, hi):
        return (lo[0] * hi[0], hi[0] * lo[1] + hi[1])

    _, h = lax.associative_scan(combine, (a, u), axis=1)
    return h, h[:, -1]


def rglru_direction(xr, wa, ba, wx, bx, lam, h0, reverse):
    f32 = jnp.float32
    b, n, _ = xr.shape
    xb = xr.reshape(b, n, RNN_BLOCKS, RNN_BLOCK_W)
    r = jax.nn.sigmoid(jnp.einsum('blnk,nkj->blnj', xb, wa).reshape(b, n, D_RNN) + ba)
    i = jax.nn.sigmoid(jnp.einsum('blnk,nkj->blnj', xb, wx).reshape(b, n, D_RNN) + bx)
    log_a = -RG_C * r.astype(f32) * jax.nn.softplus(-lam.astype(f32))
    a = jnp.exp(log_a)
    u = jnp.sqrt(-jnp.expm1(2.0 * log_a)) * (i * xr).astype(f32)
    if reverse:
        a, u = jnp.flip(a, axis=1), jnp.flip(u, axis=1)
    hseq, hlast = linear_scan(a, u, h0.astype(f32))
    if reverse:
        hseq = jnp.flip(hseq, axis=1)
    return hseq.astype(xr.dtype), hlast.astype(h0.dtype)


def even_mixer(h, w_in, conv_w, conv_b, a_log, dt_bias, d_skip, norm_g, lam_p, sub_g, w_out,
               lam_init, rope, ctx):
    b, n, _ = h.shape
    gn = SSD_GROUPS * SSD_STATE
    c1 = D_SSD
    c2 = c1 + SSD_CONV_CH
    c3 = c2 + 2 * SSD_HEADS
    c4 = c3 + D_ATT
    c5 = c4 + D_ATT
    z, xbc, dt_raw, q, k, v = jnp.split(h @ w_in, [c1, c2, c3, c4, c5], axis=-1)
    xbc = jax.nn.silu(dwconv_centred(xbc, conv_w, conv_b))
    xs, bm, cm = jnp.split(xbc, [D_SSD, D_SSD + gn], axis=-1)
    xs = xs.reshape(b, n, SSD_HEADS, SSD_HEAD_DIM)
    bm = bm.reshape(b, n, SSD_GROUPS, SSD_STATE)
    cm = cm.reshape(b, n, SSD_GROUPS, SSD_STATE)
    if ctx is None:
        h0f = jnp.zeros((b, SSD_HEADS, SSD_HEAD_DIM, SSD_STATE), h.dtype)
        h0b = h0f
    else:
        h0f, h0b = ctx[2], ctx[3]
    y, hf, hb = ssd_bidirectional(xs, dt_raw, bm, cm, a_log, dt_bias, h0f, h0b)
    y = y + d_skip[:, None] * xs
    y = rmsnorm(y.reshape(b, n, D_SSD) * jax.nn.silu(z), norm_g)
    q = q.reshape(b, n, ATT_HEADS, 2, ATT_HALF_DIM)
    k = k.reshape(b, n, ATT_HEADS, 2, ATT_HALF_DIM)
    v = v.reshape(b, n, ATT_HEADS, ATT_V_DIM)
    lp = lam_p.astype(jnp.float32)
    lam = jnp.exp(jnp.sum(lp[0] * lp[1])) - jnp.exp(jnp.sum(lp[2] * lp[3])) + lam_init
    if ctx is None:
        o = diff_attention(q, k, v, lam)
    else:
        q = apply_axial_rope(q, rope)
        k = apply_axial_rope(k, rope)
        o = diff_attention(q, jnp.concatenate([ctx[0], k], axis=1),
                           jnp.concatenate([ctx[1], v], axis=1), lam)
    o = rmsnorm(o, sub_g) * (1.0 - lam_init)
    out = jnp.concatenate([y, o.reshape(b, n, D_ATT)], axis=-1) @ w_out
    return out, (k, v, hf, hb)


def odd_mixer(h, w_in, conv_w, conv_b, wa, ba, wx, bx, lam, w_out, ctx):
    b = h.shape[0]
    gate, xr = jnp.split(h @ w_in, 2, axis=-1)
    xr = dwconv_centred(xr, conv_w, conv_b)
    if ctx is None:
        h0f = jnp.zeros((b, D_RNN), h.dtype)
        h0b = h0f
    else:
        h0f, h0b = ctx
    yf, hf = rglru_direction(xr, wa[0], ba[0], wx[0], bx[0], lam[0], h0f, False)
    yb, hb = rglru_direction(xr, wa[1], ba[1], wx[1], bx[1], lam[1], h0b, True)
    y = (yf + yb) * jax.nn.gelu(gate)
    return y @ w_out, (hf, hb)


def conv_ffn(h, w_up, conv_w, conv_b, w_down):
    u = dwconv_centred(h @ w_up, conv_w, conv_b)
    val, g = jnp.split(u, 2, axis=-1)
    return (jax.nn.silu(g) * val) @ w_down


def setup_inputs(seed: int = 0) -> dict:
    key = jax.random.key(seed)
    keys = iter(jax.random.split(key, 48))
    f32 = jnp.float32
    D = D_MODEL
    ne, no = N_EVEN, N_ODD

    def nrm(shape, scale):
        return jax.random.normal(next(keys), shape, f32) * scale

    def unif(shape, lo, hi):
        return jax.random.uniform(next(keys), shape, f32, lo, hi)

    dt0 = jnp.exp(unif((ne, 2, SSD_HEADS), math.log(1e-3), math.log(1e-1)))
    a_lru = unif((no, 2, D_RNN), 0.9, 0.999) ** (1.0 / RG_C)
    return {
        'x_prompt': nrm((BATCH, SEQ, D), 1.0),
        'x_sample': nrm((DEC_BATCH, DEC_SEQ, D), 1.0),
        'c': nrm((DEC_BATCH, D), 1.0),
        'cache_attn_k': nrm((DEC_BATCH, ne, PAST_LEN, ATT_HEADS, 2, ATT_HALF_DIM), 1.0),
        'cache_attn_v': nrm((DEC_BATCH, ne, PAST_LEN, ATT_HEADS, ATT_V_DIM), 1.0),
        'state_ssd_fwd': nrm((DEC_BATCH, ne, SSD_HEADS, SSD_HEAD_DIM, SSD_STATE), 0.5),
        'state_ssd_bwd': nrm((DEC_BATCH, ne, SSD_HEADS, SSD_HEAD_DIM, SSD_STATE), 0.5),
        'state_lru_fwd': nrm((DEC_BATCH, no, D_RNN), 0.5),
        'state_lru_bwd': nrm((DEC_BATCH, no, D_RNN), 0.5),
        'c_ctx': nrm((D,), 1.0),
        'w_mod': nrm((DEPTH, D, 6 * D), 0.5 * D ** -0.5),
        'b_mod': nrm((DEPTH, 6 * D), 0.02),
        'norm_mix_g': 1.0 + nrm((DEPTH, D), 0.02),
        'norm_ffn_g': 1.0 + nrm((DEPTH, D), 0.02),
        'ssd_attn_w_in': nrm((ne, D, D_IN_EVEN), D ** -0.5),
        'ssd_conv_w': nrm((ne, SSD_CONV_W, SSD_CONV_CH), SSD_CONV_W ** -0.5),
        'ssd_conv_b': nrm((ne, SSD_CONV_CH), 0.02),
        'ssd_a_log': jnp.log(unif((ne, 2, SSD_HEADS), 1.0, 16.0)),
        'ssd_dt_bias': dt0 + jnp.log(-jnp.expm1(-dt0)),
        'ssd_d': 1.0 + nrm((ne, SSD_HEADS), 0.02),
        'ssd_norm_g': 1.0 + nrm((ne, D_SSD), 0.02),
        'diff_lambda': nrm((ne, 4, ATT_HALF_DIM), 0.1),
        'diff_norm_g': 1.0 + nrm((ne, ATT_V_DIM), 0.02),
        'ssd_attn_w_out': nrm((ne, D_SSD + D_ATT, D), (D_SSD + D_ATT) ** -0.5),
        'lru_w_in': nrm((no, D, D_IN_ODD), D ** -0.5),
        'lru_conv_w': nrm((no, RNN_CONV_W, D_RNN), RNN_CONV_W ** -0.5),
        'lru_conv_b': nrm((no, D_RNN), 0.02),
        'lru_wa': nrm((no, 2, RNN_BLOCKS, RNN_BLOCK_W, RNN_BLOCK_W), RNN_BLOCK_W ** -0.5),
        'lru_ba': nrm((no, 2, D_RNN), 0.02),
        'lru_wx': nrm((no, 2, RNN_BLOCKS, RNN_BLOCK_W, RNN_BLOCK_W), RNN_BLOCK_W ** -0.5),
        'lru_bx': nrm((no, 2, D_RNN), 0.02),
        'lru_lambda': jnp.log(a_lru) - jnp.log1p(-a_lru),
        'lru_w_out': nrm((no, D_RNN, D), D_RNN ** -0.5),
        'ffn_w_up': nrm((DEPTH, D, 2 * D_FF), D ** -0.5),
        'ffn_conv_w': nrm((DEPTH, FFN_CONV_W, 2 * D_FF), FFN_CONV_W ** -0.5),
        'ffn_conv_b': nrm((DEPTH, 2 * D_FF), 0.02),
        'ffn_w_down': nrm((DEPTH, D_FF, D), D_FF ** -0.5),
        'final_norm_g': 1.0 + nrm((D,), 0.02),
    }


def reference(x_prompt, x_sample, c, cache_attn_k, cache_attn_v, state_ssd_fwd, state_ssd_bwd,
              state_lru_fwd, state_lru_bwd, c_ctx, w_mod, b_mod, norm_mix_g, norm_ffn_g,
              ssd_attn_w_in, ssd_conv_w, ssd_conv_b, ssd_a_log, ssd_dt_bias, ssd_d, ssd_norm_g,
              diff_lambda, diff_norm_g, ssd_attn_w_out, lru_w_in, lru_conv_w, lru_conv_b,
              lru_wa, lru_ba, lru_wx, lru_bx, lru_lambda, lru_w_out, ffn_w_up, ffn_conv_w,
              ffn_conv_b, ffn_w_down, final_norm_g):
    rope = axial_rope_tables(x_sample.shape[1], x_sample.dtype)
    xp, xs = x_prompt, x_sample
    new_k, new_v, new_sf, new_sb, new_lf, new_lb = [], [], [], [], [], []
    for l in range(DEPTH):
        sp_m, sc_m, gp_m, sp_f, sc_f, gp_f = ada_params(c_ctx[None, :], w_mod[l], b_mod[l])
        ss_m, scs_m, gs_m, ss_f, scs_f, gs_f = ada_params(c, w_mod[l], b_mod[l])
        hp = modulate(rmsnorm(xp, norm_mix_g[l]), sp_m, sc_m)
        hs = modulate(rmsnorm(xs, norm_mix_g[l]), ss_m, scs_m)
        j = l // 2
        if l % 2 == 0:
            lam_init = 0.8 - 0.6 * math.exp(-0.3 * l)
            ew = (ssd_attn_w_in[j], ssd_conv_w[j], ssd_conv_b[j], ssd_a_log[j], ssd_dt_bias[j],
                  ssd_d[j], ssd_norm_g[j], diff_lambda[j], diff_norm_g[j], ssd_attn_w_out[j])
            out_p, (kp, vp, sfp, sbp) = even_mixer(hp, *ew, lam_init, None, None)
            out_s, _ = even_mixer(hs, *ew, lam_init, rope,
                                  (cache_attn_k[:, j], cache_attn_v[:, j],
                                   state_ssd_fwd[:, j], state_ssd_bwd[:, j]))
            new_k.append(kp)
            new_v.append(vp)
            new_sf.append(sfp)
            new_sb.append(sbp)
        else:
            ow = (lru_w_in[j], lru_conv_w[j], lru_conv_b[j], lru_wa[j], lru_ba[j], lru_wx[j],
                  lru_bx[j], lru_lambda[j], lru_w_out[j])
            out_p, (lfp, lbp) = odd_mixer(hp, *ow, None)
            out_s, _ = odd_mixer(hs, *ow, (state_lru_fwd[:, j], state_lru_bwd[:, j]))
            new_lf.append(lfp)
            new_lb.append(lbp)
        xp = xp + gp_m * out_p
        xs = xs + gs_m * out_s
        fw = (ffn_w_up[l], ffn_conv_w[l], ffn_conv_b[l], ffn_w_down[l])
        xp = xp + gp_f * conv_ffn(modulate(rmsnorm(xp, norm_ffn_g[l]), sp_f, sc_f), *fw)
        xs = xs + gs_f * conv_ffn(modulate(rmsnorm(xs, norm_ffn_g[l]), ss_f, scs_f), *fw)
    y_prompt = rmsnorm(xp, final_norm_g)
    y_sample = rmsnorm(xs, final_norm_g)
    return (y_prompt, y_sample, jnp.stack(new_k, axis=1), jnp.stack(new_v, axis=1),
            jnp.stack(new_sf, axis=1), jnp.stack(new_sb, axis=1),
            jnp.stack(new_lf, axis=1), jnp.stack(new_lb, axis=1))
```

```python
import math
import os
from contextlib import ExitStack
import numpy as np
import concourse.bass as bass
import concourse.mybir as mybir
from concourse.bass_utils import run_bass_kernel_spmd

F32 = mybir.dt.float32
BF16 = mybir.dt.bfloat16
AF = mybir.ActivationFunctionType
ALU = mybir.AluOpType

D = 1024
DEPTH = 4
NP_SEQ = 4
LP = 256
LS = 1024
NTOK = 2048
PAST = 512
EPS = 1e-6
D_FF = 2816
NJ = 22
C_XBC0, C_DT0, C_Q0, C_K0, C_V0 = 1024, 2304, 2336, 3360, 4384
ATT_SCALE = 64 ** -0.5


class V:
    __slots__ = ("ap", "keys")

    def __init__(self, ap, keys):
        self.ap = ap
        self.keys = tuple(keys)


class Tile:
    def __init__(self, ap, key):
        self.ap = ap
        self.key = key

    def __getitem__(self, idx):
        return V(self.ap[idx], (self.key,))

    def v(self):
        return V(self.ap, (self.key,))


def _ap(x):
    return x.ap if isinstance(x, V) else x


class KB:
    ENGS = ("pe", "act", "dve", "pool", "sp")

    def __init__(self, nc, es):
        self.nc = nc
        self.es = es
        self.eng = {"pe": nc.tensor, "act": nc.scalar, "dve": nc.vector, "pool": nc.gpsimd, "sp": nc.sync}
        self.ops = []
        self.count = {e: 0 for e in self.ENGS}
        self.writers = {}
        self.readers = {}
        self.seen = {e: {} for e in self.ENGS}
        self.chan_n = {}
        self.milestones = {e: set() for e in self.ENGS}
        self.pending_bar = {e: {} for e in self.ENGS}
        self.osize = {e: [] for e in self.ENGS}

    def op(self, eng, fn, reads=(), writes=(), chan=None, osize=1 << 20):
        idx = self.count[eng]
        self.count[eng] += 1
        self.osize[eng].append(osize)
        need = {}

        def add(src, val):
            if src == ("e", eng) and chan is None:
                if not (val >= idx - 4 and self.osize[eng][val] < 512):
                    return
            if src[0] == "c":
                val = self.chan_n[src[1]]
            if need.get(src, -1) < val:
                need[src] = val

        for k in reads:
            for s, v in self.writers.get(k, {}).items():
                add(s, v)
        for k in writes:
            for s, v in self.writers.get(k, {}).items():
                add(s, v)
            for s, v in self.readers.get(k, {}).items():
                add(s, v)
        for s, v in self.pending_bar[eng].items():
            add(s, v)
        self.pending_bar[eng] = {}
        if chan is not None and self.chan_n.get(chan, 0) > 0:
            add(("c", chan), self.chan_n[chan])
        deps = []
        for s, v in need.items():
            if self.seen[eng].get(s, -1) >= v:
                continue
            self.seen[eng][s] = v
            deps.append((s, v))
            if s[0] == "e":
                self.milestones[s[1]].add(v)
        if chan is not None:
            self.chan_n[chan] = self.chan_n.get(chan, 0) + 1
            me, myv = ("c", chan), self.chan_n[chan]
        else:
            me, myv = ("e", eng), idx
        for k in writes:
            self.writers[k] = {me: myv}
            self.readers[k] = {}
        for k in reads:
            if k not in writes:
                self.readers.setdefault(k, {})[me] = myv
        self.ops.append((eng, idx, fn, deps, chan))

    def barrier(self):
        comp = ("pe", "act", "dve")
        for e in comp + ("sp",):
            for o in comp:
                if o != e and self.count[o] > 0:
                    self.pending_bar[e][("e", o)] = self.count[o] - 1
            for c, n in self.chan_n.items():
                if not str(c).startswith("w") and n > 0:
                    self.pending_bar[e][("c", c)] = n

    def emit(self):
        nc = self.nc
        sems = {e: self.es.enter_context(nc.semaphore("s_" + e)) for e in self.ENGS}
        csems = {c: self.es.enter_context(nc.semaphore("c_%s" % str(c))) for c in self.chan_n}
        ranks = {}
        for e in self.ENGS:
            for r, i in enumerate(sorted(self.milestones[e])):
                ranks[(e, i)] = r + 1
        for eng, idx, fn, deps, chan in self.ops:
            E = self.eng[eng]
            for s, v in deps:
                if s[0] == "e":
                    E.wait_ge(sems[s[1]], ranks[(s[1], v)])
                else:
                    E.wait_ge(csems[s[1]], 16 * v)
            ins = fn()
            if chan is not None:
                ins.then_inc(csems[chan], 16)
            elif idx in self.milestones[eng]:
                ins.then_inc(sems[eng], 1)
        for c, n in self.chan_n.items():
            nc.sync.wait_ge(csems[c], 16 * n)

    @staticmethod
    def _fs(x):
        ap = _ap(x)
        n = 1
        for d in list(ap.shape)[1:]:
            n *= int(d)
        return n

    @staticmethod
    def _keys(*xs):
        ks = []
        for x in xs:
            if isinstance(x, V):
                ks.extend(x.keys)
        return ks

    def mm(self, out, lhsT, rhs, start=True, stop=True):
        self.op("pe", lambda: self.nc.tensor.matmul(_ap(out), lhsT=_ap(lhsT), rhs=_ap(rhs), start=start, stop=stop),
                reads=self._keys(lhsT, rhs), writes=self._keys(out))

    def tr(self, out, in_, ident):
        self.op("pe", lambda: self.nc.tensor.transpose(_ap(out), _ap(in_), _ap(ident)),
                reads=self._keys(in_, ident), writes=self._keys(out))

    def act(self, out, in_, func, bias=0.0, scale=1.0, accum=None):
        def f():
            kw = {}
            if accum is not None:
                kw["accum_out"] = _ap(accum)
            return self.nc.scalar.activation(out=_ap(out), in_=_ap(in_), func=func, bias=_ap(bias), scale=_ap(scale), **kw)
        self.op("act", f, reads=self._keys(in_, bias, scale), writes=self._keys(out, accum),
                osize=(1 if accum is not None else self._fs(out)))

    def tt(self, out, in0, in1, op, eng="dve"):
        E = self.eng[eng]
        self.op(eng, lambda: E.tensor_tensor(out=_ap(out), in0=_ap(in0), in1=_ap(in1), op=op),
                reads=self._keys(in0, in1), writes=self._keys(out), osize=self._fs(out))

    def ts(self, out, in0, s1, s2, op0, op1=None, eng="dve"):
        E = self.eng[eng]
        if op1 is None:
            f = lambda: E.tensor_scalar(out=_ap(out), in0=_ap(in0), scalar1=_ap(s1), scalar2=None, op0=op0)
        else:
            f = lambda: E.tensor_scalar(out=_ap(out), in0=_ap(in0), scalar1=_ap(s1), scalar2=_ap(s2), op0=op0, op1=op1)
        self.op(eng, f, reads=self._keys(in0, s1, s2), writes=self._keys(out), osize=self._fs(out))

    def stt(self, out, in0, scalar, in1, op0, op1, eng="dve"):
        E = self.eng[eng]
        self.op(eng, lambda: E.scalar_tensor_tensor(out=_ap(out), in0=_ap(in0), scalar=_ap(scalar), in1=_ap(in1), op0=op0, op1=op1),
                reads=self._keys(in0, scalar, in1), writes=self._keys(out), osize=self._fs(out))

    def copy(self, out, in_, eng="dve"):
        if eng == "act":
            return self.act(out, in_, AF.Copy)
        E = self.eng[eng]
        self.op(eng, lambda: E.tensor_copy(out=_ap(out), in_=_ap(in_)), reads=self._keys(in_), writes=self._keys(out), osize=self._fs(out))

    def memset(self, out, val, eng="dve"):
        E = self.eng[eng]
        self.op(eng, lambda: E.memset(_ap(out), val), writes=self._keys(out), osize=self._fs(out))

    def recip(self, out, in_):
        self.op("dve", lambda: self.nc.vector.reciprocal(out=_ap(out), in_=_ap(in_)), reads=self._keys(in_), writes=self._keys(out), osize=self._fs(out))

    def scan(self, out, d0, d1, init):
        self.op("dve", lambda: self.nc.vector.tensor_tensor_scan(out=_ap(out), data0=_ap(d0), data1=_ap(d1), initial=_ap(init),
                                                                 op0=ALU.mult, op1=ALU.add),
                reads=self._keys(d0, d1, init), writes=self._keys(out))

    def dma(self, out, in_, chan, q="sp"):
        E = self.eng[q]
        self.op(q, lambda: E.dma_start(out=_ap(out), in_=_ap(in_)), reads=self._keys(in_), writes=self._keys(out), chan=chan)


def build_program(nlayers=DEPTH, debug=False):
    nc = bass.Bass("TRN2", target_bir_lowering=False)
    es = ExitStack()
    k = KB(nc, es)

    def din(name, shape):
        return nc.dram_tensor(name, list(shape), F32, kind="ExternalInput").ap()

    def dout(name, shape):
        return nc.dram_tensor(name, list(shape), F32, kind="ExternalOutput").ap()

    xin = din("xin", [NTOK, D])
    cvec = din("cvec", [2, D])
    cache_k = din("cache_k", [2, PAST, D])
    cache_v = din("cache_v", [2, PAST, D])
    ssd_f0 = din("ssd_f0", [2, 16, 64, 64])
    ssd_b0 = din("ssd_b0", [2, 16, 64, 64])
    lru_f0 = din("lru_f0", [2, D])
    lru_b0 = din("lru_b0", [2, D])
    w_mod = din("w_mod", [4, D, 6 * D]); b_mod = din("b_mod", [4, 6 * D])
    norm_mix_g = din("norm_mix_g", [4, D]); norm_ffn_g = din("norm_ffn_g", [4, D])
    w_in_e = din("ssd_attn_w_in", [2, D, 5408]); conv_w_e = din("ssd_conv_w", [2, 4, 1280]); conv_b_e = din("ssd_conv_b", [2, 1280])
    a_log = din("ssd_a_log", [2, 32]); dt_bias = din("ssd_dt_bias", [2, 32]); ssd_d = din("ssd_d", [2, 16])
    ssd_norm_g = din("ssd_norm_g", [2, D]); diff_lambda = din("diff_lambda", [2, 256]); diff_norm_g = din("diff_norm_g", [2, 128])
    w_out_e = din("ssd_attn_w_out", [2, 2048, D])
    lru_w_in = din("lru_w_in", [2, D, 2048]); lru_conv_w = din("lru_conv_w", [2, 4, D]); lru_conv_b = din("lru_conv_b", [2, D])
    lru_wa = din("lru_wa", [2, 2, 16, 64, 64]); lru_ba = din("lru_ba", [2, 2, D])
    lru_wx = din("lru_wx", [2, 2, 16, 64, 64]); lru_bx = din("lru_bx", [2, 2, D])
    lru_lambda = din("lru_lambda", [2, 2, D]); lru_w_out = din("lru_w_out", [2, D, D])
    ffn_w_up = din("ffn_w_up", [4, D, 2 * D_FF]); ffn_conv_w = din("ffn_conv_w", [4, 3, 2 * D_FF]); ffn_conv_b = din("ffn_conv_b", [4, 2 * D_FF])
    ffn_w_down = din("ffn_w_down", [4, D_FF, D]); final_norm_g = din("final_norm_g", [D])
    consts = din("consts", [10, 128, 128])
    ropetab = din("ropetab", [2, 128, LS])

    y_out = dout("y_out", [NTOK, D])
    nk_out = dout("nk_out", [NP_SEQ, 2, LP, D])
    nv_out = dout("nv_out", [NP_SEQ, 2, LP, D])
    sf_out = dout("sf_out", [NP_SEQ, 2, 16, 64, 64])
    sb_out = dout("sb_out", [NP_SEQ, 2, 16, 64, 64])
    lf_out = dout("lf_out", [NP_SEQ, 2, D])
    lb_out = dout("lb_out", [NP_SEQ, 2, D])

    dbgst = {}

    def dbg(name, v, shape):
        if not debug:
            return
        o = nc.dram_tensor("dbg_" + name, list(shape), F32, kind="ExternalOutput").ap()
        if "t" not in dbgst:
            dbgst["t"] = sbt("dbgst", [128, 1024], F32)
        st_ = dbgst["t"]
        n_ = shape[1]
        k.copy(st_[:, 0:n_], v)
        k.dma(o, st_[:, 0:n_], chan="dbg")

    def sbt(name, shape, dt):
        return Tile(es.enter_context(nc.sbuf_tensor(name, list(shape), dt))[:], name)

    x = sbt("x", [128, 8, NTOK], F32)
    cst = sbt("cst", [128, 10, 128], F32)
    cstb = sbt("cstb", [128, 10, 128], BF16)
    NCOL = 1150 if debug else 2300
    cols = sbt("cols", [128, NCOL], F32)
    mod = sbt("mod", [128, 4, 48, 2], F32)
    wslots = [sbt("wslot%d" % i, [128, 4096], BF16) for i in range(3)]
    ARENA_W = 25400
    arena = es.enter_context(nc.sbuf_tensor("arena", [128, ARENA_W], F32))[:]
    big = [Tile(es.enter_context(nc.psum_tensor("big%d" % i, [128, 1024], F32))[:], "big%d" % i) for i in range(4)]
    ps = [Tile(big[i // 2].ap[:, (i % 2) * 512:(i % 2 + 1) * 512], "ps%d" % i) for i in range(8)]
    stage = sbt("stage", [128, 128], F32)

    astate = {"off": 0, "n": 0}

    def alloc(shape, dt, name=None):
        n = 1
        for s in shape[1:]:
            n *= s
        words = n if dt == F32 else (n + 1) // 2
        o = astate["off"]
        assert o + words <= ARENA_W, ("arena overflow", o, words)
        astate["off"] = o + words
        astate["n"] += 1
        ap = arena[:, o:o + words]
        if dt != F32:
            ap = ap.bitcast(dt)[:, 0:n]
        if len(shape) == 3:
            ap = ap.rearrange("p (a b) -> p a b", a=shape[1])
        elif len(shape) == 4:
            ap = ap.rearrange("p (a b c) -> p a b c", a=shape[1], b=shape[2])
        if shape[0] != 128:
            ap = ap[0:shape[0]]
        return Tile(ap, "%s_%d" % (name or "a", astate["n"]))

    def arena_mark():
        return astate["off"]

    def arena_reset(m):
        astate["off"] = m
        k.barrier()

    wstate = {"i": 0}

    def wload(pieces, kc, cols_total):
        i = wstate["i"] % 3
        wstate["i"] += 1
        slot = wslots[i]
        view = slot.ap[:, 0:kc * cols_total].rearrange("p (a b) -> p a b", a=kc)
        for (src, off, c) in pieces:
            k.dma(V(view[:, :, off:off + c], (slot.key,)), src.rearrange("(a p) n -> p a n", p=128), chan="w%d" % i, q="pool")
        return Tile(view, slot.key)

    class WSeq:
        def __init__(self, specs, ahead=2):
            self.specs = specs
            self.tiles = {}
            self.nxt = 0
            self.ahead = ahead

        def get(self, i):
            while self.nxt < len(self.specs) and self.nxt <= i + self.ahead:
                self.tiles[self.nxt] = wload(*self.specs[self.nxt])
                self.nxt += 1
            return self.tiles[i]

    k.dma(cst.v(), consts.rearrange("a p n -> p a n"), chan="ld")
    k.copy(cstb.v(), cst.v())
    ident, identb = cst[:, 0, :], cstb[:, 0, :]
    Rb = cstb[:, 1, :]
    Tdir = [cst[:, 2, :], cst[:, 3, :]]
    maskb = [cstb[:, 4, :], cstb[:, 5, :]]
    onesb = cstb[:, 6, :]
    ones = cst[:, 6, :]

    colstate = {"n": 0}

    def load_cols(src_rows, nrows):
        off = colstate["n"]
        done = 0
        while done < nrows:
            r = min(128, nrows - done)
            k.dma(stage[0:r, :], src_rows[done:done + r, :], chan="ld")
            k.tr(ps[7][:, 0:r], stage[0:r, :], V(cst.ap[0:r, 0, 0:r], (cst.key,)))
            k.copy(cols[:, off + done:off + done + r], ps[7][:, 0:r])
            done += r
        colstate["n"] += nrows
        assert colstate["n"] <= NCOL
        return off

    def rows(ap1d_or_2d, n):
        return ap1d_or_2d.rearrange("(r c) -> r c", c=128)

    xm = arena_mark()
    xst = [alloc([128, D], F32, "xst") for _ in range(2)]
    for t in range(NTOK // 128):
        st = xst[t % 2]
        k.dma(st.v(), xin[t * 128:(t + 1) * 128, :], chan="xl%d" % (t % 2))
        for half in range(2):
            pt = ps[half]
            for c4 in range(4):
                c = half * 4 + c4
                k.tr(pt[:, c4 * 128:(c4 + 1) * 128], st[:, c * 128:(c + 1) * 128], ident)
            k.copy(V(x.ap[:, half * 4:half * 4 + 4, t * 128:(t + 1) * 128], (x.key,)),
                   V(pt.ap.rearrange("p (a b) -> p a b", a=4), (pt.key,)), eng=("act" if half else "dve"))
    arena_reset(xm)

    cm = arena_mark()
    cT = alloc([128, 16], F32, "cT")
    cTb = alloc([128, 16], BF16, "cTb")
    k.dma(stage[0:16, :], cvec.rearrange("v (c q) -> (v c) q", q=128), chan="ld")
    k.tr(ps[7][:, 0:16], stage[0:16, :], V(cst.ap[0:16, 0, 0:16], (cst.key,)))
    k.act(cT.v(), ps[7][:, 0:16], AF.Silu)
    k.copy(cTb.v(), cT.v())
    cTb3 = cTb.ap.rearrange("p (v c) -> p c v", v=2)
    for l in range(nlayers):
        bcol = load_cols(rows(b_mod[l], 48), 48)
        specs = [([(w_mod[l][:, g * 512:(g + 1) * 512], 0, 512)], 8, 512) for g in range(12)]
        wq = WSeq(specs)
        for g in range(12):
            w = wq.get(g)
            for s4 in range(4):
                ch = g * 4 + s4
                for kc in range(8):
                    k.mm(ps[6][:, ch * 2:ch * 2 + 2], w[:, kc, s4 * 128:(s4 + 1) * 128], V(cTb3[:, kc, :], (cTb.key,)),
                         start=(kc == 0), stop=(kc == 7))
        k.tt(V(mod.ap[:, l, :, :], (mod.key,)), V(ps[6].ap[:, 0:96].rearrange("p (a b) -> p a b", b=2), (ps[6].key,)),
             V(cols.ap[:, bcol:bcol + 48].unsqueeze(2).broadcast_to([128, 48, 2]), (cols.key,)), ALU.add)
    arena_reset(cm)

    PASSES = [dict(t0=0, nseq=NP_SEQ, L=LP, v=0), dict(t0=1024, nseq=1, L=LS, v=1)]

    def modcol(l, which, c, v):
        return V(mod.ap[:, l, which * 8 + c, v:v + 1], (mod.key,))

    def phase_norm(l, gcol, which_shift, which_scale, P, h, final=False, hf=None):
        m = arena_mark()
        AB = alloc([128, 8, 2], F32, "AB")
        sq = [alloc([128, 512], BF16, "sq") for _ in range(2)]
        lnv = alloc([128, 512], F32, "lnv")
        rstd = alloc([128, 512], F32, "rstd")
        tmp = [alloc([128, 512], F32, "tmp") for _ in range(2)]
        for c in range(8):
            if final:
                k.copy(AB[:, c, 0:1], cols[:, gcol + c:gcol + c + 1])
                k.memset(AB[:, c, 1:2], 0.0)
            else:
                k.ts(AB[:, c, 0:1], modcol(l, which_scale, c, P["v"]), 1.0, cols[:, gcol + c:gcol + c + 1], ALU.add, ALU.mult)
                k.copy(AB[:, c, 1:2], modcol(l, which_shift, c, P["v"]))
        for blk in range(2):
            tok = slice(P["t0"] + blk * 512, P["t0"] + blk * 512 + 512)
            for c in range(8):
                s = sq[c % 2]
                k.act(s.v(), x[:, c, tok], AF.Square)
                k.mm(ps[0].v(), onesb, s.v(), start=(c == 0), stop=(c == 7))
            k.act(lnv.v(), ps[0].v(), AF.Ln, bias=EPS, scale=1.0 / D)
            k.act(rstd.v(), lnv.v(), AF.Exp, scale=-0.5)
            for c in range(8):
                t = tmp[c % 2]
                k.tt(t.v(), x[:, c, tok], rstd.v(), ALU.mult)
                dst = (hf if final else h)[:, c, blk * 512:(blk + 1) * 512]
                k.ts(dst, t.v(), AB[:, c, 0:1], AB[:, c, 1:2], ALU.mult, ALU.add)
        astate["off"] = m
        k.barrier()

    def conv_fm(src_ps_list, P, stg, width, left, wcol0, wstride, bcol, acc):
        nseq, L = P["nseq"], P["L"]
        for blk in range(2):
            if nseq == 1:
                dst = V(stg.ap[:, 0, left + blk * 512:left + blk * 512 + 512], (stg.key,))
                src = src_ps_list[blk].v()
            else:
                dst = V(stg.ap[:, 2 * blk:2 * blk + 2, left:left + L], (stg.key,))
                src = V(src_ps_list[blk].ap.rearrange("p (a b) -> p a b", a=2), (src_ps_list[blk].key,))
            k.act(dst, src, AF.Copy)
        k.act(acc.v(), V(stg.ap[:, :, 0:L], (stg.key,)), AF.Identity, bias=cols[:, bcol:bcol + 1], scale=cols[:, wcol0:wcol0 + 1])
        for j in range(1, width):
            wc = wcol0 + j * wstride
            k.stt(acc.v(), V(stg.ap[:, :, j:j + L], (stg.key,)), cols[:, wc:wc + 1], acc.v(), ALU.mult, ALU.add)

    def resid_add(l, which_gate, P, d, blk, pst):
        tok = slice(P["t0"] + blk * 512, P["t0"] + blk * 512 + 512)
        k.stt(x[:, d, tok], pst.v(), modcol(l, which_gate, d, P["v"]), x[:, d, tok], ALU.mult, ALU.add)

    def phase_ffn(l, P, cw, cb):
        m0 = arena_mark()
        h = alloc([128, 8, 1024], BF16, "h")
        phase_norm(l, gcols_ffn[l], 3, 4, P, h)
        nseq, L = P["nseq"], P["L"]
        actT = [alloc([128, 1024], BF16, "act") for _ in range(NJ)]
        stg = [[alloc([128, nseq, L + 2], F32, "stg") for _ in range(2)] for _ in range(2)]
        accv = [alloc([128, nseq, L], F32, "accv") for _ in range(2)]
        accg = [alloc([128, nseq, L], F32, "accg") for _ in range(2)]
        for sp_ in stg:
            for s in sp_:
                k.memset(s.v(), 0.0)
        specs = []
        for jj in range(11):
            j0 = jj * 2
            specs.append(([(ffn_w_up[l][:, j0 * 128:(j0 + 2) * 128], 0, 256),
                           (ffn_w_up[l][:, D_FF + j0 * 128:D_FF + (j0 + 2) * 128], 256, 256)], 8, 512))
        for d in range(8):
            specs.append(([(ffn_w_down[l][:, d * 128:(d + 1) * 128], 0, 128)], NJ, 128))
        wq = WSeq(specs)
        for j in range(NJ):
            w = wq.get(j // 2)
            jo = (j % 2) * 128
            par = j % 2
            for half, (coff, stgt, acc) in enumerate(((jo, stg[par][0], accv[par]), (256 + jo, stg[par][1], accg[par]))):
                pl = [ps[par * 4 + half * 2], ps[par * 4 + half * 2 + 1]]
                for blk in range(2):
                    for kc in range(8):
                        k.mm(pl[blk].v(), w[:, kc, coff:coff + 128], h[:, kc, blk * 512:(blk + 1) * 512], start=(kc == 0), stop=(kc == 7))
                fcol = j + (NJ if half else 0)
                conv_fm(pl, P, stgt, 3, 1, cw + fcol, 2 * NJ, cb + fcol, acc)
            k.act(accg[par].v(), accg[par].v(), AF.Silu)
            k.tt(V(actT[j].ap.rearrange("p (a b) -> p a b", a=nseq), (actT[j].key,)), accg[par].v(), accv[par].v(), ALU.mult)
        for d in range(8):
            w = wq.get(11 + d)
            for blk in range(2):
                pt = ps[4 + (d * 2 + blk) % 2]
                for j in range(NJ):
                    k.mm(pt.v(), w[:, j, :], actT[j][:, blk * 512:(blk + 1) * 512], start=(j == 0), stop=(j == NJ - 1))
                resid_add(l, 5, P, d, blk, pt)
        arena_reset(m0)

    def phase_even(l, P, ec):
        j = l // 2
        lam_init = 0.8 - 0.6 * math.exp(-0.3 * l)
        m0 = arena_mark()
        h = alloc([128, 8, 1024], BF16, "h")
        phase_norm(l, gcols_mix[l], 0, 1, P, h)
        yo = alloc([128, 8, 1024], BF16, "yo")
        nseq, L, isB = P["nseq"], P["L"], (P["v"] == 1)
        B0, B1, B2, B3 = big
        P3bf = V(B3.ap.bitcast(BF16), (B3.key,))

        def ccol(name, i=0):
            return cols[:, ec[name] + i:ec[name] + i + 1]

        def out_proj(kbase):
            specs = [([(w_out_e[j][kbase * 128:kbase * 128 + 1024, dg * 256:(dg + 1) * 256], 0, 256)], 8, 256) for dg in range(4)]
            wq = WSeq(specs)
            for d in range(8):
                w = wq.get(d // 2)
                for blk in range(2):
                    pt = V(B0.ap[:, blk * 512:(blk + 1) * 512], (B0.key,))
                    for kc in range(8):
                        k.mm(pt, w[:, kc, (d % 2) * 128:(d % 2 + 1) * 128], V(yo.ap[:, kc, blk * 512:(blk + 1) * 512], (yo.key,)),
                             start=(kc == 0), stop=(kc == 7))
                    tok = slice(P["t0"] + blk * 512, P["t0"] + blk * 512 + 512)
                    k.stt(x[:, d, tok], pt, modcol(l, 2, d, P["v"]), x[:, d, tok], ALU.mult, ALU.add)

        ms = arena_mark()
        xbc = [alloc([128, 256], BF16, "xbc") for _ in range(10)]
        stg = [alloc([128, 262], F32, "stg") for _ in range(2)]
        acc = [alloc([128, 256], F32, "acc") for _ in range(2)]
        xsT = [alloc([128, 1024], BF16, "xsT") for _ in range(2)]
        BT = [alloc([128, 128], BF16, "BT") for _ in range(2)]
        sz = [alloc([128, 1024], BF16, "sz") for _ in range(2)]
        dtr = alloc([128, 2, 32], F32, "dtr"); dt = alloc([128, 2, 32], F32, "dt"); dta = alloc([128, 2, 32], F32, "dta")
        Scol = alloc([128, 2, 32], F32, "Scol"); eU = alloc([128, 2, 32], F32, "eU"); dend = alloc([128, 2, 32], F32, "dend")
        dec = alloc([128, 2, 32], F32, "dec"); wdd = alloc([128, 2, 32], F32, "wdd"); Utot = alloc([128, 2, 32], F32, "Utot")
        xdt = [alloc([128, 1024], BF16, "xdt") for _ in range(2)]
        xdd = xdt
        xD = alloc([128, 1024], BF16, "xD")
        CBT = alloc([128, 2, 128], BF16, "CBT")
        segT = alloc([128, 8, 128], F32, "segT")
        LT = alloc([128, 8, 128], BF16, "LT")
        MT = [[alloc([128, 8, 128], BF16, "MT") for _ in range(2)] for _ in range(2)]
        ysb = alloc([128, 1024], F32, "ysb")
        tmpy = Tile(segT.ap.rearrange("p a b -> p (a b)"), segT.key)
        gn = alloc([128, 1024], BF16, "gn")
        junk = gn
        ssq = alloc([128, 2], F32, "ssq")
        Hs = [alloc([128, 512], F32, "Hs") for _ in range(2)]
        Hb16 = [[alloc([128, 512], BF16, "Hb16") for _ in range(2)] for _ in range(2)]
        Hent_b = [alloc([128, 512], BF16, "Hentb") for _ in range(8)] if isB else None
        stF = alloc([128, 4, 2, 64], F32, "stF")
        Dbc = V(cols.ap[:, ec["d"]:ec["d"] + 16].unsqueeze(2).broadcast_to([128, 16, 64]), (cols.key,))

        def v3(t, a):
            return V(t.ap.rearrange("p (a b) -> p a b", a=a), (t.key,))

        def group_prep(gi, need_z):
            g0 = gi * 256
            seq0 = (g0 // L) * L
            lo, hi = max(seq0, g0 - 2), min(seq0 + L, g0 + 257)
            n_in = hi - lo
            so = lo - (g0 - 2)
            specs = []
            if need_z:
                specs += [([(w_in_e[j][:, 0:512], 0, 512)], 8, 512), ([(w_in_e[j][:, 512:1024], 0, 512)], 8, 512)]
            specs += [([(w_in_e[j][:, 1024:1536], 0, 512)], 8, 512), ([(w_in_e[j][:, 1536:2048], 0, 512)], 8, 512),
                      ([(w_in_e[j][:, 2048:2336], 0, 288)], 8, 288)]
            wq = WSeq(specs)
            wi = 0
            if need_z:
                for half in range(2):
                    w = wq.get(wi); wi += 1
                    for tt in range(2):
                        pt = V(B3.ap[:, (tt % 2) * 512:(tt % 2) * 512 + 512], (B3.key,))
                        for kc in range(8):
                            k.mm(pt, h[:, kc, g0 + tt * 128:g0 + (tt + 1) * 128], w[:, kc, :], start=(kc == 0), stop=(kc == 7))
                        k.act(sz[tt][:, half * 512:(half + 1) * 512], pt, AF.Silu)
            pend_silu = []
            for c in range(10):
                if c % 4 == 0:
                    w = wq.get(wi); wi += 1
                st_, ac_ = stg[c % 2], acc[c % 2]
                pt = V(B0.ap[:, (c % 2) * 512:(c % 2) * 512 + n_in], (B0.key,))
                for kc in range(8):
                    k.mm(pt, w[:, kc, (c % 4) * 128:(c % 4 + 1) * 128], h[:, kc, lo:hi], start=(kc == 0), stop=(kc == 7))
                k.memset(st_.v(), 0.0)
                k.act(st_[:, so:so + n_in], pt, AF.Copy)
                cw, cb = ec["cw"] + c, ec["cb"] + c
                k.act(ac_.v(), st_[:, 0:256], AF.Identity, bias=cols[:, cb:cb + 1], scale=cols[:, cw:cw + 1])
                if pend_silu:
                    pend_silu.pop()()
                for kk in range(1, 4):
                    k.stt(ac_.v(), st_[:, kk:kk + 256], cols[:, cw + 10 * kk:cw + 10 * kk + 1], ac_.v(), ALU.mult, ALU.add)
                pend_silu.append(lambda c=c, ac_=ac_: k.act(xbc[c].v(), ac_.v(), AF.Silu))
            pend_silu.pop()()
            if int(os.environ.get("PREP_LEVEL", "9")) < 3:
                return
            pdt = V(B1.ap[:, 0:64].rearrange("p (a b) -> p a b", a=2), (B1.key,))
            pS = V(B1.ap[:, 64:128].rearrange("p (a b) -> p a b", a=2), (B1.key,))
            pU = V(B1.ap[:, 128:192].rearrange("p (a b) -> p a b", a=2), (B1.key,))
            steps = []
            def s1():
                for tt in range(2):
                    for kc in range(8):
                        k.mm(V(B1.ap[:, tt * 32:(tt + 1) * 32], (B1.key,)), h[:, kc, g0 + tt * 128:g0 + (tt + 1) * 128], w[:, kc, 256:288],
                             start=(kc == 0), stop=(kc == 7))
            steps.append(s1)
            steps.append(lambda: k.tt(dtr.v(), pdt, V(cols.ap[:, ec["dtb"]:ec["dtb"] + 32].unsqueeze(1).broadcast_to([128, 2, 32]), (cols.key,)), ALU.add))
            steps.append(lambda: k.act(dtr.v(), dtr.v(), AF.Exp))
            steps.append(lambda: k.act(dt.v(), dtr.v(), AF.Ln, bias=1.0))
            steps.append(lambda: k.tt(dta.v(), dt.v(), V(cols.ap[:, ec["abc"]:ec["abc"] + 32].unsqueeze(1).broadcast_to([128, 2, 32]), (cols.key,)), ALU.mult))
            def s6():
                for tt in range(2):
                    for dr in range(2):
                        k.mm(V(B1.ap[:, 64 + tt * 32 + dr * 16:64 + tt * 32 + dr * 16 + 16], (B1.key,)), Tdir[dr], dta[:, tt, dr * 16:(dr + 1) * 16])
                    k.mm(V(B1.ap[:, 128 + tt * 32:128 + (tt + 1) * 32], (B1.key,)), ones, dta[:, tt, :])
            steps.append(s6)
            steps.append(lambda: k.copy(Scol.v(), pS))
            steps.append(lambda: k.copy(Utot.v(), pU))
            steps.append(lambda: k.act(eU.v(), Scol.v(), AF.Exp))
            steps.append(lambda: k.act(dec.v(), Utot.v(), AF.Exp))
            steps.append(lambda: k.tt(dend.v(), Utot.v(), Scol.v(), ALU.subtract))
            steps.append(lambda: k.act(dend.v(), dend.v(), AF.Exp))
            steps.append(lambda: k.tt(wdd.v(), dend.v(), dt.v(), ALU.mult))
            for st_i, st_f in enumerate(steps):
                if st_i < int(os.environ.get("PREP_STEPS", "99")):
                    st_f()
            if int(os.environ.get("PREP_LEVEL", "9")) < 4:
                return
            for tt in range(2):
                for c in range(8):
                    k.tr(V(P3bf.ap[:, c * 128:(c + 1) * 128], (B3.key,)), xbc[c][:, tt * 128:(tt + 1) * 128], identb)
                k.tr(V(P3bf.ap[:, 1024:1152], (B3.key,)), xbc[8][:, tt * 128:(tt + 1) * 128], identb)
                k.act(xsT[tt].v(), V(P3bf.ap[:, 0:1024], (B3.key,)), AF.Copy)
                k.act(BT[tt].v(), V(P3bf.ap[:, 1024:1152], (B3.key,)), AF.Copy)

        def bc16(t, tt, dr):
            return V(t.ap[:, tt, dr * 16:(dr + 1) * 16].unsqueeze(2).broadcast_to([128, 16, 64]), (t.key,))

        SSD_LEVEL = int(os.environ.get("SSD_LEVEL", "3"))

        def chunk_states(tt, dr, pst):
            if SSD_LEVEL < 2:
                return
            k.tt(v3(xdd[dr], 16), v3(xsT[tt], 16), bc16(wdd, tt, dr), ALU.mult, eng="dve")
            for g in range(2):
                k.mm(V(pst.ap[g * 64:(g + 1) * 64, :], pst.keys), BT[tt][:, g * 64:(g + 1) * 64], xdd[dr][:, g * 512:(g + 1) * 512])

        def state_step(Ht, tt, dr, pst, have):
            if SSD_LEVEL < 2:
                return
            if not have:
                k.copy(Ht.v(), pst)
                return
            for g in range(2):
                hv = V(Ht.ap[g * 64:(g + 1) * 64, :].rearrange("p (a b) -> p a b", a=8), (Ht.key,))
                dv = V(dec.ap[g * 64:(g + 1) * 64, tt, dr * 16 + g * 8:dr * 16 + g * 8 + 8].unsqueeze(2).broadcast_to([64, 8, 64]), (dec.key,))
                k.tt(hv, hv, dv, ALU.mult)
            k.tt(Ht.v(), Ht.v(), pst, ALU.add)

        def chunk_y(gi, tt, ent):
            tok = slice(gi * 256 + tt * 128, gi * 256 + (tt + 1) * 128)
            tl = slice(tt * 128, (tt + 1) * 128)
            if SSD_LEVEL < 3:
                return
            YS = int(os.environ.get("Y_STEPS", "99"))
            for g in range(2):
                k.mm(V(B3.ap[:, g * 512:g * 512 + 128], (B3.key,)), xbc[8][g * 64:(g + 1) * 64, tl], xbc[9][g * 64:(g + 1) * 64, tl])
            for g in range(2):
                k.copy(CBT[:, g, :], V(B3.ap[:, g * 512:g * 512 + 128], (B3.key,)))
            DB = debug and (not isB) and gi == 0 and tt == 0
            if DB:
                dbg("CBT", V(CBT.ap.rearrange("p a b -> p (a b)"), (CBT.key,)), [128, 256])
                dbg("xsT", xsT[tt].v(), [128, 1024])
                dbg("dt", V(dt.ap.rearrange("p a b -> p (a b)"), (dt.key,)), [128, 64])
                dbg("Scol", V(Scol.ap.rearrange("p a b -> p (a b)"), (Scol.key,)), [128, 64])
            if YS < 2:
                return
            for dr in range(2):
                k.tt(v3(xdt[dr], 16), v3(xsT[tt], 16), bc16(dt, tt, dr), ALU.mult, eng="dve")
            k.tt(v3(xD, 16), v3(xsT[tt], 16), Dbc, ALU.mult, eng="dve")
            if YS < 3:
                return
            for g in range(2):
                for dr in range(2):
                    pb = big[dr]
                    pbv = V(pb.ap.rearrange("p (a b) -> p a b", a=8), (pb.key,))
                    for h8 in range(8):
                        hd = dr * 16 + g * 8 + h8
                        k.mm(V(pb.ap[:, h8 * 128:(h8 + 1) * 128], (pb.key,)), V(dta.ap[:, tt, hd:hd + 1].broadcast_to([128, 128]), (dta.key,)),
                             Tdir[dr], start=True, stop=False)
                        k.mm(V(pb.ap[:, h8 * 128:(h8 + 1) * 128], (pb.key,)), identb, maskb[dr], start=False, stop=True)
                    hd0 = dr * 16 + g * 8
                    for bk in range(2):
                        k.tt(segT[:, bk * 4:(bk + 1) * 4, :], V(pbv.ap[:, bk * 4:(bk + 1) * 4, :], pbv.keys),
                             V(Scol.ap[:, tt, hd0 + bk * 4:hd0 + bk * 4 + 4].unsqueeze(2).broadcast_to([128, 4, 128]), (Scol.key,)), ALU.subtract)
                    k.act(LT.v(), segT.v(), AF.Exp)
                    k.tt(MT[g][dr].v(), LT.v(), V(CBT.ap[:, g, :].unsqueeze(1).broadcast_to([128, 8, 128]), (CBT.key,)), ALU.mult, eng="dve")
                    if DB and g == 0:
                        dbg("seg%d" % dr, V(segT.ap.rearrange("p a b -> p (a b)"), (segT.key,)), [128, 1024])
                        dbg("MT%d" % dr, V(MT[g][dr].ap.rearrange("p a b -> p (a b)"), (MT[g][dr].key,)), [128, 1024])
                if YS < 4:
                    continue
                for h8 in range(8):
                    hsl = slice((g * 8 + h8) * 64, (g * 8 + h8 + 1) * 64)
                    yv = V(B2.ap[:, hsl], (B2.key,))
                    k.mm(yv, identb, xD[:, hsl], start=True, stop=False)
                    k.mm(yv, MT[g][0][:, h8, :], xdt[0][:, hsl], start=False, stop=False)
                    k.mm(yv, MT[g][1][:, h8, :], xdt[1][:, hsl], start=False, stop=True)
            if YS < 5:
                return
            for bk in range(2):
                k.act(ysb[:, bk * 512:(bk + 1) * 512], B2[:, bk * 512:(bk + 1) * 512], AF.Copy)
            if YS < 6:
                return
            if DB:
                dbg("ydiag", ysb.v(), [128, 1024])
            for dr in range(2):
                if ent[dr] is None:
                    continue
                for g in range(2):
                    k.mm(V(B3.ap[:, g * 512:(g + 1) * 512], (B3.key,)), xbc[9][g * 64:(g + 1) * 64, tl], ent[dr][g * 64:(g + 1) * 64, :])
                for bk in range(2):
                    k.tt(V(tmpy.ap[:, bk * 512:(bk + 1) * 512].rearrange("p (a b) -> p a b", a=8), (tmpy.key,)),
                         V(B3.ap[:, bk * 512:(bk + 1) * 512].rearrange("p (a b) -> p a b", a=8), (B3.key,)),
                         V(eU.ap[:, tt, dr * 16 + bk * 8:dr * 16 + bk * 8 + 8].unsqueeze(2).broadcast_to([128, 8, 64]), (eU.key,)), ALU.mult)
                k.tt(ysb.v(), ysb.v(), tmpy.v(), ALU.add, eng="dve")
            if DB:
                dbg("ysb", ysb.v(), [128, 1024])
            if YS < 7:
                return
            k.tt(ysb.v(), ysb.v(), sz[tt].v(), ALU.mult)
            k.memset(ssq[:, 0:1], 0.0)
            k.act(junk.v(), ysb.v(), AF.Square, accum=ssq[:, 0:1])
            k.act(ssq[:, 1:2], ssq[:, 0:1], AF.Ln, bias=EPS, scale=1.0 / 1024)
            k.act(ssq[:, 1:2], ssq[:, 1:2], AF.Exp, scale=-0.5)
            k.act(gn.v(), ysb.v(), AF.Copy, scale=ssq[:, 1:2])
            for c in range(8):
                k.tr(V(P3bf.ap[:, c * 128:(c + 1) * 128], (B3.key,)), gn[:, c * 128:(c + 1) * 128], identb)
            k.tt(V(yo.ap[:, 0:8, tok], (yo.key,)), V(P3bf.ap[:, 0:1024].rearrange("p (a b) -> p a b", a=8), (B3.key,)),
                 V(cols.ap[:, ec["ng"]:ec["ng"] + 8].unsqueeze(2).broadcast_to([128, 8, 128]), (cols.key,)), ALU.mult)

        def write_state(Ht, dst):
            if SSD_LEVEL < 2:
                return
            for pr in range(4):
                k.tr(V(B3.ap[:, pr * 128:(pr + 1) * 128], (B3.key,)), Ht[:, pr * 128:(pr + 1) * 128], ident)
            k.copy(V(stF.ap.rearrange("p a b c -> p (a b c)"), (stF.key,)), V(B3.ap[:, 0:512], (B3.key,)))
            dv_ = dst.rearrange("(g pr h2) p n -> g (h2 p) pr n", g=2, pr=4)
            for g in range(2):
                k.dma(dv_[g], V(stF.ap[:, :, g, :], (stF.key,)), chan="st")

        pstates = [V(B3.ap[:, 0:512], (B3.key,)), V(B3.ap[:, 512:1024], (B3.key,))]
        SKIP_SSD = bool(os.environ.get("SKIP_SSD")); SKIP_ATT = bool(os.environ.get("SKIP_ATT"))
        if SKIP_SSD:
            pass
        elif not isB:
            for s in range(nseq):
                group_prep(s, True)
                chunk_states(0, 0, pstates[0]); state_step(Hs[0], 0, 0, pstates[0], False)
                k.copy(Hb16[0][1].v(), Hs[0].v(), eng="act")
                chunk_states(1, 1, pstates[1]); state_step(Hs[1], 1, 1, pstates[1], False)
                k.copy(Hb16[1][0].v(), Hs[1].v(), eng="act")
                chunk_states(1, 0, pstates[0]); state_step(Hs[0], 1, 0, pstates[0], True)
                write_state(Hs[0], sf_out[s, j])
                chunk_states(0, 1, pstates[1]); state_step(Hs[1], 0, 1, pstates[1], True)
                write_state(Hs[1], sb_out[s, j])
                chunk_y(s, 0, [None, Hb16[1][0]])
                chunk_y(s, 1, [Hb16[0][1], None])
        else:
            for dr, src in enumerate((ssd_f0, ssd_b0)):
                sv_ = src[j].rearrange("(g pr h2) p n -> g (h2 p) pr n", g=2, pr=4)
                for g in range(2):
                    k.dma(V(stF.ap[:, :, g, :], (stF.key,)), sv_[g], chan="ld")
                for pr in range(4):
                    k.tr(V(B3.ap[:, pr * 128:(pr + 1) * 128], (B3.key,)), V(stF.ap[:, pr, :, :].rearrange("p a b -> p (a b)"), (stF.key,)), ident)
                k.copy(Hs[dr].v(), V(B3.ap[:, 0:512], (B3.key,)))
            for gi in (3, 2, 1, 0):
                group_prep(gi, False)
                for tt in (1, 0):
                    k.copy(Hent_b[gi * 2 + tt].v(), Hs[1].v(), eng="act")
                    if gi * 2 + tt > 0:
                        chunk_states(tt, 1, pstates[tt]); state_step(Hs[1], tt, 1, pstates[tt], True)
            for gi in range(4):
                group_prep(gi, True)
                for tt in range(2):
                    k.copy(Hb16[0][tt].v(), Hs[0].v(), eng="act")
                    if gi * 2 + tt < 7:
                        chunk_states(tt, 0, pstates[tt]); state_step(Hs[0], tt, 0, pstates[tt], True)
                for tt in range(2):
                    chunk_y(gi, tt, [Hb16[0][tt], Hent_b[gi * 2 + tt]])
        if not SKIP_SSD:
            out_proj(0)
        arena_reset(ms)

        nkt = 12 if isB else 8
        nk = nkt * 128
        koff = 4 if isB else 0
        qT = [alloc([128, 1024], BF16, "qT") for _ in range(4)]
        kT = [alloc([128, nk], BF16, "kT") for _ in range(4)]
        vaug = alloc([128, nkt, 4, 130], BF16, "vaug")
        PT = [alloc([128, 512], BF16, "PT") for _ in range(3)]
        raw = [alloc([128, 512], BF16, "raw") for _ in range(2)]
        t1 = alloc([128, 512], F32, "t1"); t2 = alloc([128, 512], F32, "t2")
        ost = alloc([128, 512], F32, "ost")
        o_t = alloc([128, 4, 128], F32, "o_t"); o_n = alloc([128, 4, 128], BF16, "o_n")
        ojunk = Tile(t1.ap.rearrange("p (a b) -> p a b", a=4), t1.key)
        Osb = [alloc([128, 4, 130], F32, "Osb") for _ in range(2)]
        rs = alloc([128, 2, 4], F32, "rs"); rs2 = alloc([128, 2, 4], F32, "rs2"); sso = alloc([128, 2, 4], F32, "sso")
        if isB:
            rope = alloc([128, 2, 1024], F32, "rope")
            k.dma(rope.v(), ropetab.rearrange("a p n -> p a n"), chan="ld")
            ckst = alloc([128, 4, 512], F32, "ckst")
        for hg in range(0 if SKIP_ATT else 2):
            specs = [([(w_in_e[j][:, c0 + hg * 512:c0 + (hg + 1) * 512], 0, 512)], 8, 512) for c0 in (C_Q0, C_K0, C_V0)]
            wq = WSeq(specs)
            wQ, wK, wV = wq.get(0), wq.get(1), wq.get(2)
            k.memset(V(vaug.ap[:, :, :, 128:130], (vaug.key,)), 1.0)
            if isB:
                k.dma(ckst.v(), cache_k[j][:, hg * 512:(hg + 1) * 512].rearrange("(a p) n -> p a n", p=128), chan="ld")
                for kt in range(4):
                    k.dma(V(vaug.ap[:, kt, :, 0:128], (vaug.key,)),
                          cache_v[j][kt * 128:(kt + 1) * 128, hg * 512:(hg + 1) * 512].rearrange("p (a b) -> p a b", a=4), chan="cv", q="pool")
                    for hh in range(4):
                        k.tr(V(B3.ap[:, hh * 128:(hh + 1) * 128], (B3.key,)), ckst[:, kt, hh * 128:(hh + 1) * 128], ident)
                    for hh in range(4):
                        k.copy(kT[hh][:, kt * 128:(kt + 1) * 128], V(B3.ap[:, hh * 128:(hh + 1) * 128], (B3.key,)), eng="act")
            for which, (wt, dstT, doff) in enumerate(((wQ, qT, 0), (wK, kT, koff * 128))):
                for hh in range(4):
                    for blk in range(2):
                        pt = V(B0.ap[:, blk * 512:(blk + 1) * 512], (B0.key,))
                        for kc in range(8):
                            k.mm(pt, wt[:, kc, hh * 128:(hh + 1) * 128], h[:, kc, blk * 512:(blk + 1) * 512], start=(kc == 0), stop=(kc == 7))
                        dst = dstT[hh][:, doff + blk * 512:doff + (blk + 1) * 512]
                        if not isB:
                            k.act(dst, pt, AF.Copy)
                        else:
                            rw = raw[blk]
                            k.act(rw.v(), pt, AF.Copy)
                            p2 = V(B1.ap[:, blk * 512:(blk + 1) * 512], (B1.key,))
                            k.mm(p2, Rb, rw.v())
                            k.tt(t1.v(), rw.v(), rope[:, 0, blk * 512:(blk + 1) * 512], ALU.mult, eng="dve")
                            k.tt(t2.v(), p2, rope[:, 1, blk * 512:(blk + 1) * 512], ALU.mult)
                            k.tt(dst, t1.v(), t2.v(), ALU.add)
            for t in range(8):
                pt = V(B2.ap[:, (t % 2) * 512:(t % 2) * 512 + 512], (B2.key,))
                for kc in range(8):
                    k.mm(pt, h[:, kc, t * 128:(t + 1) * 128], wV[:, kc, :], start=(kc == 0), stop=(kc == 7))
                k.act(V(vaug.ap[:, koff + t, :, 0:128], (vaug.key,)), V(pt.ap.rearrange("p (a b) -> p a b", a=4), pt.keys), AF.Copy)
                if not isB:
                    s, tl = t // 2, (t % 2) * 128
                    k.copy(ost.v(), pt, eng="act")
                    k.dma(nv_out[s, j, tl:tl + 128, hg * 512:(hg + 1) * 512], ost.v(), chan="stv")
                    pk = V(B3.ap[:, (t % 2) * 512:(t % 2) * 512 + 512], (B3.key,))
                    for kc in range(8):
                        k.mm(pk, h[:, kc, t * 128:(t + 1) * 128], wK[:, kc, :], start=(kc == 0), stop=(kc == 7))
                    k.copy(ost.v(), pk)
                    k.dma(nk_out[s, j, tl:tl + 128, hg * 512:(hg + 1) * 512], ost.v(), chan="stk")
            if isB:
                qblocks = [(0, 512, list(range(12))), (512, 512, list(range(12)))]
            else:
                qblocks = [(s * 256, 256, [2 * s, 2 * s + 1]) for s in range(4)]
            pti = 0
            Sbank = [Tile(B0.ap[:, 0:512], "B0_lo"), Tile(B0.ap[:, 512:1024], "B0_hi")]
            k.barrier()
            oslots = [V(B1.ap[:, 0:129], (B1.key,)), V(B1.ap[:, 512:641], (B1.key,)),
                      V(B2.ap[:, 0:129], (B2.key,)), V(B2.ap[:, 512:641], (B2.key,))]
            for hh in range(4):
                for (q0, nq, kts) in qblocks:
                    nqt = nq // 128
                    for c in range(2):
                        def s_mm(ki, kt):
                            S = Sbank[ki % 2]
                            k.mm(V(S.ap[:, 0:nq], (S.key,)), kT[hh][c * 64:(c + 1) * 64, kt * 128:(kt + 1) * 128], qT[hh][c * 64:(c + 1) * 64, q0:q0 + nq])
                        s_mm(0, kts[0])
                        for ki, kt in enumerate(kts):
                            S = Sbank[ki % 2]
                            pt_ = PT[pti % 3]; pti += 1
                            k.act(pt_[:, 0:nq], V(S.ap[:, 0:nq], (S.key,)), AF.Exp, scale=ATT_SCALE)
                            if ki + 1 < len(kts):
                                s_mm(ki + 1, kts[ki + 1])
                            for qt in range(nqt):
                                k.mm(oslots[qt], pt_[:, qt * 128:(qt + 1) * 128],
                                     V(vaug.ap[:, kt, hh, 0:129], (vaug.key,)), start=(ki == 0), stop=(ki == len(kts) - 1))
                        for qt in range(nqt):
                            k.copy(Osb[c][:, qt, 0:129], oslots[qt])
                    for c in range(2):
                        k.recip(rs[:, c, 0:nqt], V(Osb[c].ap[:, 0:nqt, 128], (Osb[c].key,)))
                    k.act(rs2[:, 0, 0:nqt], rs[:, 0, 0:nqt], AF.Copy)
                    k.act(rs2[:, 1, 0:nqt], rs[:, 1, 0:nqt], AF.Copy, scale=ccol("nlam"))
                    o3 = V(o_t.ap[:, 0:nqt, :], (o_t.key,))
                    k.tt(o3, Osb[0][:, 0:nqt, 0:128], V(rs2.ap[:, 0, 0:nqt].unsqueeze(2).broadcast_to([128, nqt, 128]), (rs2.key,)), ALU.mult)
                    k.tt(V(ojunk.ap[:, 0:nqt, :], (ojunk.key,)), Osb[1][:, 0:nqt, 0:128],
                         V(rs2.ap[:, 1, 0:nqt].unsqueeze(2).broadcast_to([128, nqt, 128]), (rs2.key,)), ALU.mult)
                    k.tt(o3, o3, V(ojunk.ap[:, 0:nqt, :], (ojunk.key,)), ALU.add)
                    k.tt(V(ojunk.ap[:, 0:nqt, :], (ojunk.key,)), o3, o3, ALU.mult)
                    k.op("dve", (lambda nqt=nqt: nc.vector.reduce_sum(out=sso.ap[:, 0, 0:nqt], in_=ojunk.ap[:, 0:nqt, :], axis=mybir.AxisListType.X)),
                         reads=[ojunk.key], writes=[sso.key], osize=nqt)
                    k.act(sso[:, 1, 0:nqt], sso[:, 0, 0:nqt], AF.Ln, bias=EPS, scale=1.0 / 128)
                    k.act(sso[:, 1, 0:nqt], sso[:, 1, 0:nqt], AF.Exp, scale=-0.5)
                    k.tt(V(o_n.ap[:, 0:nqt, :], (o_n.key,)), o3, V(sso.ap[:, 1, 0:nqt].unsqueeze(2).broadcast_to([128, nqt, 128]), (sso.key,)), ALU.mult)
                    for qt in range(nqt):
                        k.tr(V(P3bf.ap[:, qt * 128:(qt + 1) * 128], (B3.key,)), o_n[:, qt, :], identb)
                    k.ts(V(yo.ap[:, hg * 4 + hh, q0:q0 + nq].rearrange("p (a b) -> p a b", a=nqt), (yo.key,)),
                         V(P3bf.ap[:, 0:nq].rearrange("p (a b) -> p a b", a=nqt), (B3.key,)), ccol("sgl"), None, ALU.mult)
            k.barrier()
        if not SKIP_ATT:
            out_proj(8)
        arena_reset(m0)

    def phase_odd(l, P, pc):
        j = l // 2
        m0 = arena_mark()
        h = alloc([128, 8, 1024], BF16, "h")
        phase_norm(l, gcols_mix[l], 0, 1, P, h)
        nseq, L = P["nseq"], P["L"]
        gg = [alloc([128, 1024], BF16, "gg") for _ in range(8)]
        xr = [alloc([128, 1024], BF16, "xr") for _ in range(8)]
        stg = alloc([128, nseq, L + 3], F32, "stg")
        k.memset(stg.v(), 0.0)
        bd = alloc([128, 4, 8, 128], BF16, "bd")
        k.memset(bd.v(), 0.0)
        for g, (src, dr) in enumerate(((lru_wa, 0), (lru_wx, 0), (lru_wa, 1), (lru_wx, 1))):
            sv = src[j, dr].rearrange("(c two) kk jj -> two kk c jj", two=2)
            for half in range(2):
                k.dma(V(bd.ap[half * 64:(half + 1) * 64, g, :, half * 64:(half + 1) * 64], (bd.key,)), sv[half], chan="bd", q="pool")
        tAll = alloc([128, 1024], F32, "tAll")
        tA = [Tile(tAll.ap[:, i * 512:(i + 1) * 512], tAll.key) for i in range(2)]
        specs = [([(lru_w_in[j][:, g * 512:(g + 1) * 512], 0, 512)], 8, 512) for g in range(4)]
        specs += [([(lru_w_out[j][:, g * 512:(g + 1) * 512], 0, 512)], 8, 512) for g in range(2)]
        wq = WSeq(specs)
        acc = Tile(tAll.ap.rearrange("p (a b) -> p a b", a=nseq), tAll.key)
        for c in range(8):
            w = wq.get(c // 4)
            for blk in range(2):
                pt = ps[blk]
                for kc in range(8):
                    k.mm(pt.v(), w[:, kc, (c % 4) * 128:(c % 4 + 1) * 128], h[:, kc, blk * 512:(blk + 1) * 512], start=(kc == 0), stop=(kc == 7))
                a = tA[blk]
                k.act(a.v(), pt.v(), AF.Square)
                k.ts(a.v(), a.v(), 0.044715, 1.0, ALU.mult, ALU.add)
                k.tt(a.v(), a.v(), pt.v(), ALU.mult)
                k.act(a.v(), a.v(), AF.Sigmoid, scale=2.0 * 0.7978845608028654)
                k.tt(gg[c][:, blk * 512:(blk + 1) * 512], a.v(), pt.v(), ALU.mult)
        for c in range(8):
            w = wq.get(2 + c // 4)
            pl = [ps[2], ps[3]]
            for blk in range(2):
                for kc in range(8):
                    k.mm(pl[blk].v(), w[:, kc, (c % 4) * 128:(c % 4 + 1) * 128], h[:, kc, blk * 512:(blk + 1) * 512], start=(kc == 0), stop=(kc == 7))
            conv_fm(pl, P, stg, 4, 2, pc["cw"] + c, 8, pc["cb"] + c, acc)
            k.copy(V(xr[c].ap.rearrange("p (a b) -> p a b", a=nseq), (xr[c].key,)), acc.v())
            if c == 0 and l == 1:
                dbg("xr0_%d" % P["v"], xr[0].v(), [128, 1024])
                dbg("gg0_%d" % P["v"], gg[0].v(), [128, 1024])
                dbg("h0_%d" % P["v"], h[:, 0, :], [128, 1024])
        rr = alloc([128, 1024], F32, "rr"); ii = alloc([128, 1024], F32, "ii")
        aa = alloc([128, 1024], F32, "aa"); uu = alloc([128, 1024], F32, "uu")
        hhb = [[alloc([128, 1024], F32, "hh") for _ in range(2)] for _ in range(2)]
        lst = alloc([128, 8, 2, NP_SEQ], F32, "lst")
        h0 = alloc([128, 2, 8], F32, "h0")
        USE_H0 = (P["v"] == 1) and not os.environ.get("NOH0")
        if USE_H0:
            for dr, src in enumerate((lru_f0, lru_b0)):
                k.dma(stage[0:8, :], rows(src[j], 8), chan="ld")
                k.tr(ps[7][:, 0:8], stage[0:8, :], V(cst.ap[0:8, 0, 0:8], (cst.key,)))
                k.copy(h0[:, dr, :], ps[7][:, 0:8])
            if debug:
                od = nc.dram_tensor("dbg_h0s", [128, 16], F32, kind="ExternalOutput").ap()
                k.dma(od, V(h0.ap.rearrange("p a b -> p (a b)"), (h0.key,)), chan="dbg")
        aaD = [aa, Tile(stg.ap.rearrange("p a b -> p (a b)")[:, 0:1024], stg.key)]
        uuD = [uu, Tile(tAll.ap, tAll.key)]

        def finish(c):
            hh = hhb[c % 2]
            k.tt(rr.v(), hh[0].v(), hh[1].v(), ALU.add)
            if P["v"] == 0:
                for s_ in range(nseq):
                    for dr_, col_ in ((0, (s_ + 1) * L - 1), (1, s_ * L)):
                        o_ap = lst.ap[:, c, dr_, s_:s_ + 1]
                        i_ap = hh[dr_].ap[:, col_:col_ + 1]
                        k.op("act", (lambda o_ap=o_ap, i_ap=i_ap: nc.scalar.activation(out=o_ap, in_=i_ap, func=AF.Copy)),
                             reads=[hh[dr_].key, rr.key], writes=[lst.key], osize=1)
            k.tt(h[:, c, :], rr.v(), gg[c].v(), ALU.mult)

        for c in range(8):
            hh = hhb[c % 2]
            for dr in range(2):
                for gi, dst in ((0, rr), (1, ii)):
                    g = dr * 2 + gi
                    bcolx = (pc["ba"] if gi == 0 else pc["bx"]) + dr * 8 + c
                    for blk in range(2):
                        pt = ps[(g * 2 + blk) % 4]
                        k.mm(pt.v(), V(bd.ap[:, g, c, :], (bd.key,)), xr[c][:, blk * 512:(blk + 1) * 512])
                        k.act(dst[:, blk * 512:(blk + 1) * 512], pt.v(), AF.Sigmoid, bias=cols[:, bcolx:bcolx + 1])
                lc = pc["nc8"] + dr * 8 + c
                k.act(aaD[dr].v(), rr.v(), AF.Exp, scale=cols[:, lc:lc + 1])
                k.act(uuD[dr].v(), aaD[dr].v(), AF.Square)
                k.act(uuD[dr].v(), uuD[dr].v(), AF.Identity, bias=1.0, scale=-1.0)
                k.act(uuD[dr].v(), uuD[dr].v(), AF.Sqrt)
                k.tt(uuD[dr].v(), uuD[dr].v(), ii.v(), ALU.mult)
                k.tt(uuD[dr].v(), uuD[dr].v(), xr[c].v(), ALU.mult)
                if c == 0 and l == 1 and dr == 0:
                    dbg("rr_%d" % P["v"], rr.v(), [128, 1024]); dbg("ii_%d" % P["v"], ii.v(), [128, 1024])
                    dbg("aa_%d" % P["v"], aaD[dr].v(), [128, 1024]); dbg("uu_%d" % P["v"], uuD[dr].v(), [128, 1024])
                for s in range(nseq):
                    sl = slice(s * L, (s + 1) * L)
                    first = s * L if dr == 0 else (s + 1) * L - 1
                    if USE_H0:
                        k.act(uuD[dr][:, first:first + 1], aaD[dr][:, first:first + 1], AF.Identity, bias=uuD[dr][:, first:first + 1], scale=h0[:, dr, c:c + 1])
                    if dr == 0:
                        k.scan(hh[0][:, sl], aaD[dr][:, sl], uuD[dr][:, sl], 0.0)
                    else:
                        rs = slice((s + 1) * L - 1, s * L - 1 if s > 0 else None, -1)
                        k.scan(V(hh[1].ap[:, rs], (hh[1].key,)), V(aaD[dr].ap[:, rs], (aaD[dr].key,)), V(uuD[dr].ap[:, rs], (uuD[dr].key,)), 0.0)
            if c >= 1:
                finish(c - 1)
        finish(7)
        if P["v"] == 0:
            k.tr(ps[7][0:64, 0:128], V(lst.ap.rearrange("p a b c -> p (a b c)"), (lst.key,)), ident)
            lrow = alloc([64, 128], F32, "lrow")
            k.copy(lrow.v(), ps[7][0:64, 0:128])
            if debug:
                od = nc.dram_tensor("dbg_lst", [128, 64], F32, kind="ExternalOutput").ap()
                k.dma(od, V(lst.ap.rearrange("p a b c -> p (a b c)"), (lst.key,)), chan="dbg")
                od2 = nc.dram_tensor("dbg_lrow", [64, 128], F32, kind="ExternalOutput").ap()
                k.dma(od2, lrow.v(), chan="dbg")
            for dr, dst in enumerate((lf_out, lb_out)):
                for c in range(8):
                    r0 = c * 8 + dr * 4
                    k.dma(dst[:, j, c * 128:(c + 1) * 128], lrow[r0:r0 + 4, :], chan="st")
        if l == 1:
            dbg("yo0_%d" % P["v"], h[:, 0, :], [128, 1024])
            dbg("yo5_%d" % P["v"], h[:, 5, :], [128, 1024])
        for d in range(8):
            w = wq.get(4 + d // 4)
            for blk in range(2):
                pt = ps[4 + (d * 2 + blk) % 2]
                for kc in range(8):
                    k.mm(pt.v(), w[:, kc, (d % 4) * 128:(d % 4 + 1) * 128], h[:, kc, blk * 512:(blk + 1) * 512], start=(kc == 0), stop=(kc == 7))
                resid_add(l, 2, P, d, blk, pt)
        if l == 1:
            dbg("xm0_%d" % P["v"], x[:, 0, P["t0"]:P["t0"] + 1024], [128, 1024])
        arena_reset(m0)

    gcols_mix, gcols_ffn, fcw, fcb = [], [], [], []
    ocols = {}
    ecols = {}
    for l in range(nlayers):
        gcols_mix.append(load_cols(rows(norm_mix_g[l], 8), 8))
        gcols_ffn.append(load_cols(rows(norm_ffn_g[l], 8), 8))
        fcw.append(load_cols(ffn_conv_w[l].rearrange("w (r c) -> (w r) c", c=128), 3 * 2 * NJ))
        fcb.append(load_cols(rows(ffn_conv_b[l], 2 * NJ), 2 * NJ))
        if l % 2 == 0:
            j = l // 2
            ec = {}
            ec["cw"] = load_cols(conv_w_e[j].rearrange("w (r c) -> (w r) c", c=128), 40)
            ec["cb"] = load_cols(rows(conv_b_e[j], 10), 10)
            ec["ng"] = load_cols(rows(ssd_norm_g[j], 8), 8)
            dn = load_cols(rows(diff_norm_g[j], 1), 1)
            base = colstate["n"]
            colstate["n"] += 32 + 32 + 16 + 256 + 32 + 8
            assert colstate["n"] <= NCOL
            ec["dtb"], alg, ec["d"], lp = base, base + 32, base + 64, base + 80
            ec["abc"] = base + 336
            sc = base + 368
            k.dma(cols[:, ec["dtb"]:ec["dtb"] + 32], dt_bias[j:j + 1, :].partition_broadcast(128), chan="ld")
            k.dma(cols[:, alg:alg + 32], a_log[j:j + 1, :].partition_broadcast(128), chan="ld")
            k.dma(cols[:, ec["d"]:ec["d"] + 16], ssd_d[j:j + 1, :].partition_broadcast(128), chan="ld")
            k.dma(cols[:, lp:lp + 256], diff_lambda[j:j + 1, :].partition_broadcast(128), chan="ld")
            k.act(cols[:, ec["abc"]:ec["abc"] + 32], cols[:, alg:alg + 32], AF.Exp)
            k.ts(cols[:, ec["abc"]:ec["abc"] + 32], cols[:, ec["abc"]:ec["abc"] + 32], -1.0, None, ALU.mult)
            lam_init = 0.8 - 0.6 * math.exp(-0.3 * l)
            k.tt(cols[:, lp:lp + 64], cols[:, lp:lp + 64], cols[:, lp + 64:lp + 128], ALU.mult)
            k.tt(cols[:, lp + 128:lp + 192], cols[:, lp + 128:lp + 192], cols[:, lp + 192:lp + 256], ALU.mult)
            k.memset(cols[:, sc:sc + 2], 0.0)
            k.act(cols[:, lp + 64:lp + 128], cols[:, lp:lp + 64], AF.Copy, accum=cols[:, sc:sc + 1])
            k.act(cols[:, lp + 192:lp + 256], cols[:, lp + 128:lp + 192], AF.Copy, accum=cols[:, sc + 1:sc + 2])
            k.act(cols[:, sc + 2:sc + 4], cols[:, sc:sc + 2], AF.Exp)
            k.tt(cols[:, sc + 4:sc + 5], cols[:, sc + 2:sc + 3], cols[:, sc + 3:sc + 4], ALU.subtract)
            k.act(cols[:, sc + 5:sc + 6], cols[:, sc + 4:sc + 5], AF.Identity, bias=-lam_init, scale=-1.0)
            k.act(cols[:, sc + 6:sc + 7], cols[:, dn:dn + 1], AF.Copy, scale=(1.0 - lam_init))
            ec["nlam"], ec["sgl"] = sc + 5, sc + 6
            ecols[l] = ec
        if l % 2 == 1:
            j = l // 2
            pc = {}
            pc["cw"] = load_cols(lru_conv_w[j].rearrange("w (r c) -> (w r) c", c=128), 32)
            pc["cb"] = load_cols(rows(lru_conv_b[j], 8), 8)
            pc["ba"] = load_cols(lru_ba[j].rearrange("d (r c) -> (d r) c", c=128), 16)
            pc["bx"] = load_cols(lru_bx[j].rearrange("d (r c) -> (d r) c", c=128), 16)
            lam = load_cols(lru_lambda[j].rearrange("d (r c) -> (d r) c", c=128), 16)
            pc["nc8"] = colstate["n"]
            colstate["n"] += 16
            dst = cols[:, pc["nc8"]:pc["nc8"] + 16]
            k.act(dst, cols[:, lam:lam + 16], AF.Exp, scale=-1.0)
            k.act(dst, dst, AF.Ln, bias=1.0)
            k.ts(dst, dst, -8.0, None, ALU.mult)
            ocols[l] = pc
    gfin = load_cols(rows(final_norm_g, 8), 8)

    for l in range(nlayers):
        for P in PASSES:
            if l % 2 == 1:
                phase_odd(l, P, ocols[l])
            else:
                if not os.environ.get("DISABLE_EVEN"):
                    phase_even(l, P, ecols[l])
            phase_ffn(l, P, fcw[l], fcb[l])

    for P in PASSES:
        m0 = arena_mark()
        hf = alloc([128, 8, 1024], F32, "hf")
        phase_norm(0, gfin, 0, 0, P, None, final=True, hf=hf)
        ost = [alloc([128, D], F32, "ost") for _ in range(2)]
        for t in range(8):
            o = ost[t % 2]
            for half in range(2):
                pt = ps[2 + half]
                for c4 in range(4):
                    c = half * 4 + c4
                    k.tr(pt[:, c4 * 128:(c4 + 1) * 128], hf[:, c, t * 128:(t + 1) * 128], ident)
                k.copy(o[:, half * 512:(half + 1) * 512], pt.v(), eng=("act" if half else "dve"))
            k.dma(y_out[P["t0"] + t * 128:P["t0"] + (t + 1) * 128, :], o.v(), chan="st")
        arena_reset(m0)

    k.emit()
    return nc, es


def host_consts():
    c = np.zeros((10, 128, 128), np.float32)
    c[0] = np.eye(128)
    R = np.zeros((128, 128), np.float32)
    for m in range(128):
        d = m % 32
        if d < 16:
            R[m + 16, m] = -1.0
        else:
            R[m - 16, m] = 1.0
    c[1] = R
    jj, qq = np.meshgrid(np.arange(128), np.arange(128), indexing="ij")
    c[2] = (jj <= qq)
    c[3] = (jj >= qq)
    c[4] = np.where(qq >= jj, 0.0, -30000.0)
    c[5] = np.where(qq <= jj, 0.0, -30000.0)
    c[6] = 1.0
    t = np.arange(LS)
    row = (t // 64).astype(np.float32)
    col = (t % 64).astype(np.float32)
    freqs = (10000.0 ** (-np.arange(0, 32, 2, dtype=np.float32) / 32.0)).astype(np.float32)
    tab = np.zeros((2, 128, LS), np.float32)
    for p in range(128):
        d = p % 64
        pos = row if d < 32 else col
        f = freqs[(d % 32) % 16]
        ang = (pos * f).astype(np.float32)
        tab[0, p] = np.cos(ang)
        tab[1, p] = np.sin(ang)
    return c, tab


_CACHE = {}


def make_in_maps(inputs):
    consts, tab = host_consts()
    maps = []
    for i in range(8):
        m = {}
        m["xin"] = np.ascontiguousarray(np.concatenate(
            [inputs["x_prompt"][4 * i:4 * i + 4].reshape(4 * LP, D), inputs["x_sample"][i]], axis=0))
        m["cvec"] = np.ascontiguousarray(np.stack([inputs["c_ctx"], inputs["c"][i]], axis=0))
        m["cache_k"] = np.ascontiguousarray(inputs["cache_attn_k"][i].reshape(2, PAST, D))
        m["cache_v"] = np.ascontiguousarray(inputs["cache_attn_v"][i].reshape(2, PAST, D))
        m["ssd_f0"] = np.ascontiguousarray(inputs["state_ssd_fwd"][i])
        m["ssd_b0"] = np.ascontiguousarray(inputs["state_ssd_bwd"][i])
        m["lru_f0"] = np.ascontiguousarray(inputs["state_lru_fwd"][i])
        m["lru_b0"] = np.ascontiguousarray(inputs["state_lru_bwd"][i])
        for nm in ("w_mod", "b_mod", "norm_mix_g", "norm_ffn_g", "ssd_attn_w_in", "ssd_conv_w", "ssd_conv_b",
                   "ssd_norm_g", "diff_norm_g", "ssd_attn_w_out", "lru_w_in", "lru_conv_w", "lru_conv_b", "lru_wa",
                   "lru_ba", "lru_wx", "lru_bx", "lru_lambda", "lru_w_out", "ffn_w_up", "ffn_conv_w", "ffn_conv_b",
                   "ffn_w_down", "final_norm_g"):
            m[nm] = np.ascontiguousarray(inputs[nm])
        m["ssd_a_log"] = np.ascontiguousarray(inputs["ssd_a_log"].reshape(2, 32))
        m["ssd_dt_bias"] = np.ascontiguousarray(inputs["ssd_dt_bias"].reshape(2, 32))
        m["ssd_d"] = np.ascontiguousarray(inputs["ssd_d"])
        m["diff_lambda"] = np.ascontiguousarray(inputs["diff_lambda"].reshape(2, 256))
        m["consts"] = consts
        m["ropetab"] = tab
        maps.append(m)
    return maps


def kernel(**inputs):
    inputs = {k_: np.asarray(v, dtype=np.float32) for k_, v in inputs.items()}
    if "nc" not in _CACHE:
        _CACHE["nc"] = build_program()
    nc, _es = _CACHE["nc"]
    maps = make_in_maps(inputs)
    res = run_bass_kernel_spmd(nc, maps, core_ids=list(range(8)))
    R = res.results
    y = np.stack([r["y_out"] for r in R])
    y_prompt = y[:, :1024].reshape(32, LP, D)
    y_sample = y[:, 1024:].reshape(8, LS, D)
    nk = np.concatenate([r["nk_out"] for r in R], axis=0).reshape(32, 2, LP, 8, 2, 64)
    nv = np.concatenate([r["nv_out"] for r in R], axis=0).reshape(32, 2, LP, 8, 128)
    sf = np.concatenate([r["sf_out"] for r in R], axis=0)
    sb = np.concatenate([r["sb_out"] for r in R], axis=0)
    lf = np.concatenate([r["lf_out"] for r in R], axis=0)
    lb = np.concatenate([r["lb_out"] for r in R], axis=0)
    return (y_prompt, y_sample, nk, nv, sf, sb, lf, lb)
```

```python
import math
import os
from contextlib import ExitStack
import numpy as np
import concourse.bass as bass
import concourse.mybir as mybir
from concourse.bass_utils import run_bass_kernel_spmd

F32 = mybir.dt.float32
BF16 = mybir.dt.bfloat16
AF = mybir.ActivationFunctionType
ALU = mybir.AluOpType

D = 1024
DEPTH = 4
NP_SEQ = 4
LP = 256
LS = 1024
NTOK = 2048
PAST = 512
EPS = 1e-6
D_FF = 2816
NJ = 22
C_XBC0, C_DT0, C_Q0, C_K0, C_V0 = 1024, 2304, 2336, 3360, 4384
ATT_SCALE = 64 ** -0.5


class V:
    __slots__ = ("ap", "keys")

    def __init__(self, ap, keys):
        self.ap = ap
        self.keys = tuple(keys)


class Tile:
    def __init__(self, ap, key):
        self.ap = ap
        self.key = key

    def __getitem__(self, idx):
        return V(self.ap[idx], (self.key,))

    def v(self):
        return V(self.ap, (self.key,))


def _ap(x):
    return x.ap if isinstance(x, V) else x


class KB:
    ENGS = ("pe", "act", "dve", "pool", "sp")

    def __init__(self, nc, es):
        self.nc = nc
        self.es = es
        self.eng = {"pe": nc.tensor, "act": nc.scalar, "dve": nc.vector, "pool": nc.gpsimd, "sp": nc.sync}
        self.ops = []
        self.count = {e: 0 for e in self.ENGS}
        self.writers = {}
        self.readers = {}
        self.seen = {e: {} for e in self.ENGS}
        self.chan_n = {}
        self.milestones = {e: set() for e in self.ENGS}
        self.pending_bar = {e: {} for e in self.ENGS}
        self.osize = {e: [] for e in self.ENGS}

    def op(self, eng, fn, reads=(), writes=(), chan=None, osize=1 << 20):
        idx = self.count[eng]
        self.count[eng] += 1
        self.osize[eng].append(osize)
        need = {}

        def add(src, val):
            if src == ("e", eng) and chan is None:
                if not (val >= idx - 4 and self.osize[eng][val] < 512):
                    return
            if src[0] == "c":
                val = self.chan_n[src[1]]
            if need.get(src, -1) < val:
                need[src] = val

        for k in reads:
            for s, v in self.writers.get(k, {}).items():
                add(s, v)
        for k in writes:
            for s, v in self.writers.get(k, {}).items():
                add(s, v)
            for s, v in self.readers.get(k, {}).items():
                add(s, v)
        for s, v in self.pending_bar[eng].items():
            add(s, v)
        self.pending_bar[eng] = {}
        if chan is not None and self.chan_n.get(chan, 0) > 0:
            add(("c", chan), self.chan_n[chan])
        deps = []
        for s, v in need.items():
            if self.seen[eng].get(s, -1) >= v:
                continue
            self.seen[eng][s] = v
            deps.append((s, v))
            if s[0] == "e":
                self.milestones[s[1]].add(v)
        if chan is not None:
            self.chan_n[chan] = self.chan_n.get(chan, 0) + 1
            me, myv = ("c", chan), self.chan_n[chan]
        else:
            me, myv = ("e", eng), idx
        for k in writes:
            self.writers[k] = {me: myv}
            self.readers[k] = {}
        for k in reads:
            if k not in writes:
                self.readers.setdefault(k, {})[me] = myv
        self.ops.append((eng, idx, fn, deps, chan))

    def barrier(self):
        comp = ("pe", "act", "dve")
        for e in comp + ("sp",):
            for o in comp:
                if o != e and self.count[o] > 0:
                    self.pending_bar[e][("e", o)] = self.count[o] - 1
            for c, n in self.chan_n.items():
                if not str(c).startswith("w") and n > 0:
                    self.pending_bar[e][("c", c)] = n

    def emit(self):
        nc = self.nc
        sems = {e: self.es.enter_context(nc.semaphore("s_" + e)) for e in self.ENGS}
        csems = {c: self.es.enter_context(nc.semaphore("c_%s" % str(c))) for c in self.chan_n}
        ranks = {}
        for e in self.ENGS:
            for r, i in enumerate(sorted(self.milestones[e])):
                ranks[(e, i)] = r + 1
        for eng, idx, fn, deps, chan in self.ops:
            E = self.eng[eng]
            for s, v in deps:
                if s[0] == "e":
                    E.wait_ge(sems[s[1]], ranks[(s[1], v)])
                else:
                    E.wait_ge(csems[s[1]], 16 * v)
            ins = fn()
            if chan is not None:
                ins.then_inc(csems[chan], 16)
            elif idx in self.milestones[eng]:
                ins.then_inc(sems[eng], 1)
        for c, n in self.chan_n.items():
            nc.sync.wait_ge(csems[c], 16 * n)

    @staticmethod
    def _fs(x):
        ap = _ap(x)
        n = 1
        for d in list(ap.shape)[1:]:
            n *= int(d)
        return n

    @staticmethod
    def _keys(*xs):
        ks = []
        for x in xs:
            if isinstance(x, V):
                ks.extend(x.keys)
        return ks

    def mm(self, out, lhsT, rhs, start=True, stop=True):
        self.op("pe", lambda: self.nc.tensor.matmul(_ap(out), lhsT=_ap(lhsT), rhs=_ap(rhs), start=start, stop=stop),
                reads=self._keys(lhsT, rhs), writes=self._keys(out))

    def tr(self, out, in_, ident):
        self.op("pe", lambda: self.nc.tensor.transpose(_ap(out), _ap(in_), _ap(ident)),
                reads=self._keys(in_, ident), writes=self._keys(out))

    def act(self, out, in_, func, bias=0.0, scale=1.0, accum=None):
        def f():
            kw = {}
            if accum is not None:
                kw["accum_out"] = _ap(accum)
            return self.nc.scalar.activation(out=_ap(out), in_=_ap(in_), func=func, bias=_ap(bias), scale=_ap(scale), **kw)
        self.op("act", f, reads=self._keys(in_, bias, scale), writes=self._keys(out, accum),
                osize=(1 if accum is not None else self._fs(out)))

    def tt(self, out, in0, in1, op, eng="dve"):
        E = self.eng[eng]
        self.op(eng, lambda: E.tensor_tensor(out=_ap(out), in0=_ap(in0), in1=_ap(in1), op=op),
                reads=self._keys(in0, in1), writes=self._keys(out), osize=self._fs(out))

    def ts(self, out, in0, s1, s2, op0, op1=None, eng="dve"):
        E = self.eng[eng]
        if op1 is None:
            f = lambda: E.tensor_scalar(out=_ap(out), in0=_ap(in0), scalar1=_ap(s1), scalar2=None, op0=op0)
        else:
            f = lambda: E.tensor_scalar(out=_ap(out), in0=_ap(in0), scalar1=_ap(s1), scalar2=_ap(s2), op0=op0, op1=op1)
        self.op(eng, f, reads=self._keys(in0, s1, s2), writes=self._keys(out), osize=self._fs(out))

    def stt(self, out, in0, scalar, in1, op0, op1, eng="dve"):
        E = self.eng[eng]
        self.op(eng, lambda: E.scalar_tensor_tensor(out=_ap(out), in0=_ap(in0), scalar=_ap(scalar), in1=_ap(in1), op0=op0, op1=op1),
                reads=self._keys(in0, scalar, in1), writes=self._keys(out), osize=self._fs(out))

    def copy(self, out, in_, eng="dve"):
        if eng == "act":
            return self.act(out, in_, AF.Copy)
        E = self.eng[eng]
        self.op(eng, lambda: E.tensor_copy(out=_ap(out), in_=_ap(in_)), reads=self._keys(in_), writes=self._keys(out), osize=self._fs(out))

    def memset(self, out, val, eng="dve"):
        E = self.eng[eng]
        self.op(eng, lambda: E.memset(_ap(out), val), writes=self._keys(out), osize=self._fs(out))

    def recip(self, out, in_):
        self.op("dve", lambda: self.nc.vector.reciprocal(out=_ap(out), in_=_ap(in_)), reads=self._keys(in_), writes=self._keys(out), osize=self._fs(out))

    def scan(self, out, d0, d1, init):
        self.op("dve", lambda: self.nc.vector.tensor_tensor_scan(out=_ap(out), data0=_ap(d0), data1=_ap(d1), initial=_ap(init),
                                                                 op0=ALU.mult, op1=ALU.add),
                reads=self._keys(d0, d1, init), writes=self._keys(out))

    def dma(self, out, in_, chan, q="sp"):
        E = self.eng[q]
        self.op(q, lambda: E.dma_start(out=_ap(out), in_=_ap(in_)), reads=self._keys(in_), writes=self._keys(out), chan=chan)


def build_program(nlayers=DEPTH, debug=False):
    nc = bass.Bass("TRN2", target_bir_lowering=False)
    es = ExitStack()
    k = KB(nc, es)

    def din(name, shape):
        return nc.dram_tensor(name, list(shape), F32, kind="ExternalInput").ap()

    def dout(name, shape):
        return nc.dram_tensor(name, list(shape), F32, kind="ExternalOutput").ap()

    xin = din("xin", [NTOK, D])
    cvec = din("cvec", [2, D])
    cache_k = din("cache_k", [2, PAST, D])
    cache_v = din("cache_v", [2, PAST, D])
    ssd_f0 = din("ssd_f0", [2, 16, 64, 64])
    ssd_b0 = din("ssd_b0", [2, 16, 64, 64])
    lru_f0 = din("lru_f0", [2, D])
    lru_b0 = din("lru_b0", [2, D])
    w_mod = din("w_mod", [4, D, 6 * D]); b_mod = din("b_mod", [4, 6 * D])
    norm_mix_g = din("norm_mix_g", [4, D]); norm_ffn_g = din("norm_ffn_g", [4, D])
    w_in_e = din("ssd_attn_w_in", [2, D, 5408]); conv_w_e = din("ssd_conv_w", [2, 4, 1280]); conv_b_e = din("ssd_conv_b", [2, 1280])
    a_log = din("ssd_a_log", [2, 32]); dt_bias = din("ssd_dt_bias", [2, 32]); ssd_d = din("ssd_d", [2, 16])
    ssd_norm_g = din("ssd_norm_g", [2, D]); diff_lambda = din("diff_lambda", [2, 256]); diff_norm_g = din("diff_norm_g", [2, 128])
    w_out_e = din("ssd_attn_w_out", [2, 2048, D])
    lru_w_in = din("lru_w_in", [2, D, 2048]); lru_conv_w = din("lru_conv_w", [2, 4, D]); lru_conv_b = din("lru_conv_b", [2, D])
    lru_wa = din("lru_wa", [2, 2, 16, 64, 64]); lru_ba = din("lru_ba", [2, 2, D])
    lru_wx = din("lru_wx", [2, 2, 16, 64, 64]); lru_bx = din("lru_bx", [2, 2, D])
    lru_lambda = din("lru_lambda", [2, 2, D]); lru_w_out = din("lru_w_out", [2, D, D])
    ffn_w_up = din("ffn_w_up", [4, D, 2 * D_FF]); ffn_conv_w = din("ffn_conv_w", [4, 3, 2 * D_FF]); ffn_conv_b = din("ffn_conv_b", [4, 2 * D_FF])
    ffn_w_down = din("ffn_w_down", [4, D_FF, D]); final_norm_g = din("final_norm_g", [D])
    consts = din("consts", [10, 128, 128])
    ropetab = din("ropetab", [2, 128, LS])

    y_out = dout("y_out", [NTOK, D])
    nk_out = dout("nk_out", [NP_SEQ, 2, LP, D])
    nv_out = dout("nv_out", [NP_SEQ, 2, LP, D])
    sf_out = dout("sf_out", [NP_SEQ, 2, 16, 64, 64])
    sb_out = dout("sb_out", [NP_SEQ, 2, 16, 64, 64])
    lf_out = dout("lf_out", [NP_SEQ, 2, D])
    lb_out = dout("lb_out", [NP_SEQ, 2, D])

    dbgst = {}

    def dbg(name, v, shape):
        if not debug:
            return
        o = nc.dram_tensor("dbg_" + name, list(shape), F32, kind="ExternalOutput").ap()
        if "t" not in dbgst:
            dbgst["t"] = sbt("dbgst", [128, 1024], F32)
        st_ = dbgst["t"]
        n_ = shape[1]
        k.copy(st_[:, 0:n_], v)
        k.dma(o, st_[:, 0:n_], chan="dbg")

    def sbt(name, shape, dt):
        return Tile(es.enter_context(nc.sbuf_tensor(name, list(shape), dt))[:], name)

    x = sbt("x", [128, 8, NTOK], F32)
    cst = sbt("cst", [128, 10, 128], F32)
    cstb = sbt("cstb", [128, 10, 128], BF16)
    NCOL = 1150 if debug else 2300
    cols = sbt("cols", [128, NCOL], F32)
    mod = sbt("mod", [128, 4, 48, 2], F32)
    wslots = [sbt("wslot%d" % i, [128, 4096], BF16) for i in range(3)]
    ARENA_W = 25400
    arena = es.enter_context(nc.sbuf_tensor("arena", [128, ARENA_W], F32))[:]
    big = [Tile(es.enter_context(nc.psum_tensor("big%d" % i, [128, 1024], F32))[:], "big%d" % i) for i in range(4)]
    ps = [Tile(big[i // 2].ap[:, (i % 2) * 512:(i % 2 + 1) * 512], "ps%d" % i) for i in range(8)]
    stage = sbt("stage", [128, 128], F32)

    astate = {"off": 0, "n": 0}

    def alloc(shape, dt, name=None):
        n = 1
        for s in shape[1:]:
            n *= s
        words = n if dt == F32 else (n + 1) // 2
        o = astate["off"]
        assert o + words <= ARENA_W, ("arena overflow", o, words)
        astate["off"] = o + words
        astate["n"] += 1
        ap = arena[:, o:o + words]
        if dt != F32:
            ap = ap.bitcast(dt)[:, 0:n]
        if len(shape) == 3:
            ap = ap.rearrange("p (a b) -> p a b", a=shape[1])
        elif len(shape) == 4:
            ap = ap.rearrange("p (a b c) -> p a b c", a=shape[1], b=shape[2])
        if shape[0] != 128:
            ap = ap[0:shape[0]]
        return Tile(ap, "%s_%d" % (name or "a", astate["n"]))

    def arena_mark():
        return astate["off"]

    def arena_reset(m):
        astate["off"] = m
        k.barrier()

    wstate = {"i": 0}

    def wload(pieces, kc, cols_total):
        i = wstate["i"] % 3
        wstate["i"] += 1
        slot = wslots[i]
        view = slot.ap[:, 0:kc * cols_total].rearrange("p (a b) -> p a b", a=kc)
        for (src, off, c) in pieces:
            k.dma(V(view[:, :, off:off + c], (slot.key,)), src.rearrange("(a p) n -> p a n", p=128), chan="w%d" % i, q="pool")
        return Tile(view, slot.key)

    class WSeq:
        def __init__(self, specs, ahead=2):
            self.specs = specs
            self.tiles = {}
            self.nxt = 0
            self.ahead = ahead

        def get(self, i):
            while self.nxt < len(self.specs) and self.nxt <= i + self.ahead:
                self.tiles[self.nxt] = wload(*self.specs[self.nxt])
                self.nxt += 1
            return self.tiles[i]

    k.dma(cst.v(), consts.rearrange("a p n -> p a n"), chan="ld")
    k.copy(cstb.v(), cst.v())
    ident, identb = cst[:, 0, :], cstb[:, 0, :]
    Rb = cstb[:, 1, :]
    Tdir = [cst[:, 2, :], cst[:, 3, :]]
    maskb = [cstb[:, 4, :], cstb[:, 5, :]]
    onesb = cstb[:, 6, :]
    ones = cst[:, 6, :]

    colstate = {"n": 0}

    def load_cols(src_rows, nrows):
        off = colstate["n"]
        done = 0
        while done < nrows:
            r = min(128, nrows - done)
            k.dma(stage[0:r, :], src_rows[done:done + r, :], chan="ld")
            k.tr(ps[7][:, 0:r], stage[0:r, :], V(cst.ap[0:r, 0, 0:r], (cst.key,)))
            k.copy(cols[:, off + done:off + done + r], ps[7][:, 0:r])
            done += r
        colstate["n"] += nrows
        assert colstate["n"] <= NCOL
        return off

    def rows(ap1d_or_2d, n):
        return ap1d_or_2d.rearrange("(r c) -> r c", c=128)

    xm = arena_mark()
    xst = [alloc([128, D], F32, "xst") for _ in range(2)]
    for t in range(NTOK // 128):
        st = xst[t % 2]
        k.dma(st.v(), xin[t * 128:(t + 1) * 128, :], chan="xl%d" % (t % 2))
        for half in range(2):
            pt = ps[half]
            for c4 in range(4):
                c = half * 4 + c4
                k.tr(pt[:, c4 * 128:(c4 + 1) * 128], st[:, c * 128:(c + 1) * 128], ident)
            k.copy(V(x.ap[:, half * 4:half * 4 + 4, t * 128:(t + 1) * 128], (x.key,)),
                   V(pt.ap.rearrange("p (a b) -> p a b", a=4), (pt.key,)), eng=("act" if half else "dve"))
    arena_reset(xm)

    cm = arena_mark()
    cT = alloc([128, 16], F32, "cT")
    cTb = alloc([128, 16], BF16, "cTb")
    k.dma(stage[0:16, :], cvec.rearrange("v (c q) -> (v c) q", q=128), chan="ld")
    k.tr(ps[7][:, 0:16], stage[0:16, :], V(cst.ap[0:16, 0, 0:16], (cst.key,)))
    k.act(cT.v(), ps[7][:, 0:16], AF.Silu)
    k.copy(cTb.v(), cT.v())
    cTb3 = cTb.ap.rearrange("p (v c) -> p c v", v=2)
    for l in range(nlayers):
        bcol = load_cols(rows(b_mod[l], 48), 48)
        specs = [([(w_mod[l][:, g * 512:(g + 1) * 512], 0, 512)], 8, 512) for g in range(12)]
        wq = WSeq(specs)
        for g in range(12):
            w = wq.get(g)
            for s4 in range(4):
                ch = g * 4 + s4
                for kc in range(8):
                    k.mm(ps[6][:, ch * 2:ch * 2 + 2], w[:, kc, s4 * 128:(s4 + 1) * 128], V(cTb3[:, kc, :], (cTb.key,)),
                         start=(kc == 0), stop=(kc == 7))
        k.tt(V(mod.ap[:, l, :, :], (mod.key,)), V(ps[6].ap[:, 0:96].rearrange("p (a b) -> p a b", b=2), (ps[6].key,)),
             V(cols.ap[:, bcol:bcol + 48].unsqueeze(2).broadcast_to([128, 48, 2]), (cols.key,)), ALU.add)
    arena_reset(cm)

    PASSES = [dict(t0=0, nseq=NP_SEQ, L=LP, v=0), dict(t0=1024, nseq=1, L=LS, v=1)]

    def modcol(l, which, c, v):
        return V(mod.ap[:, l, which * 8 + c, v:v + 1], (mod.key,))

    def phase_norm(l, gcol, which_shift, which_scale, P, h, final=False, hf=None):
        m = arena_mark()
        AB = alloc([128, 8, 2], F32, "AB")
        sq = [alloc([128, 512], BF16, "sq") for _ in range(2)]
        lnv = alloc([128, 512], F32, "lnv")
        rstd = alloc([128, 512], F32, "rstd")
        tmp = [alloc([128, 512], F32, "tmp") for _ in range(2)]
        for c in range(8):
            if final:
                k.copy(AB[:, c, 0:1], cols[:, gcol + c:gcol + c + 1])
                k.memset(AB[:, c, 1:2], 0.0)
            else:
                k.ts(AB[:, c, 0:1], modcol(l, which_scale, c, P["v"]), 1.0, cols[:, gcol + c:gcol + c + 1], ALU.add, ALU.mult)
                k.copy(AB[:, c, 1:2], modcol(l, which_shift, c, P["v"]))
        for blk in range(2):
            tok = slice(P["t0"] + blk * 512, P["t0"] + blk * 512 + 512)
            for c in range(8):
                s = sq[c % 2]
                k.act(s.v(), x[:, c, tok], AF.Square)
                k.mm(ps[0].v(), onesb, s.v(), start=(c == 0), stop=(c == 7))
            k.act(lnv.v(), ps[0].v(), AF.Ln, bias=EPS, scale=1.0 / D)
            k.act(rstd.v(), lnv.v(), AF.Exp, scale=-0.5)
            for c in range(8):
                t = tmp[c % 2]
                k.tt(t.v(), x[:, c, tok], rstd.v(), ALU.mult)
                dst = (hf if final else h)[:, c, blk * 512:(blk + 1) * 512]
                k.ts(dst, t.v(), AB[:, c, 0:1], AB[:, c, 1:2], ALU.mult, ALU.add)
        astate["off"] = m
        k.barrier()

    def conv_fm(src_ps_list, P, stg, width, left, wcol0, wstride, bcol, acc):
        nseq, L = P["nseq"], P["L"]
        for blk in range(2):
            if nseq == 1:
                dst = V(stg.ap[:, 0, left + blk * 512:left + blk * 512 + 512], (stg.key,))
                src = src_ps_list[blk].v()
            else:
                dst = V(stg.ap[:, 2 * blk:2 * blk + 2, left:left + L], (stg.key,))
                src = V(src_ps_list[blk].ap.rearrange("p (a b) -> p a b", a=2), (src_ps_list[blk].key,))
            k.act(dst, src, AF.Copy)
        k.act(acc.v(), V(stg.ap[:, :, 0:L], (stg.key,)), AF.Identity, bias=cols[:, bcol:bcol + 1], scale=cols[:, wcol0:wcol0 + 1])
        for j in range(1, width):
            wc = wcol0 + j * wstride
            k.stt(acc.v(), V(stg.ap[:, :, j:j + L], (stg.key,)), cols[:, wc:wc + 1], acc.v(), ALU.mult, ALU.add)

    def resid_add(l, which_gate, P, d, blk, pst):
        tok = slice(P["t0"] + blk * 512, P["t0"] + blk * 512 + 512)
        k.stt(x[:, d, tok], pst.v(), modcol(l, which_gate, d, P["v"]), x[:, d, tok], ALU.mult, ALU.add)

    def phase_ffn(l, P, cw, cb):
        m0 = arena_mark()
        h = alloc([128, 8, 1024], BF16, "h")
        phase_norm(l, gcols_ffn[l], 3, 4, P, h)
        nseq, L = P["nseq"], P["L"]
        actT = [alloc([128, 1024], BF16, "act") for _ in range(NJ)]
        stg = [[alloc([128, nseq, L + 2], F32, "stg") for _ in range(2)] for _ in range(2)]
        accv = [alloc([128, nseq, L], F32, "accv") for _ in range(2)]
        accg = [alloc([128, nseq, L], F32, "accg") for _ in range(2)]
        for sp_ in stg:
            for s in sp_:
                k.memset(s.v(), 0.0)
        specs = []
        for jj in range(11):
            j0 = jj * 2
            specs.append(([(ffn_w_up[l][:, j0 * 128:(j0 + 2) * 128], 0, 256),
                           (ffn_w_up[l][:, D_FF + j0 * 128:D_FF + (j0 + 2) * 128], 256, 256)], 8, 512))
        for d in range(8):
            specs.append(([(ffn_w_down[l][:, d * 128:(d + 1) * 128], 0, 128)], NJ, 128))
        wq = WSeq(specs)
        for j in range(NJ):
            w = wq.get(j // 2)
            jo = (j % 2) * 128
            par = j % 2
            for half, (coff, stgt, acc) in enumerate(((jo, stg[par][0], accv[par]), (256 + jo, stg[par][1], accg[par]))):
                pl = [ps[par * 4 + half * 2], ps[par * 4 + half * 2 + 1]]
                for blk in range(2):
                    for kc in range(8):
                        k.mm(pl[blk].v(), w[:, kc, coff:coff + 128], h[:, kc, blk * 512:(blk + 1) * 512], start=(kc == 0), stop=(kc == 7))
                fcol = j + (NJ if half else 0)
                conv_fm(pl, P, stgt, 3, 1, cw + fcol, 2 * NJ, cb + fcol, acc)
            k.act(accg[par].v(), accg[par].v(), AF.Silu)
            k.tt(V(actT[j].ap.rearrange("p (a b) -> p a b", a=nseq), (actT[j].key,)), accg[par].v(), accv[par].v(), ALU.mult)
        for d in range(8):
            w = wq.get(11 + d)
            for blk in range(2):
                pt = ps[4 + (d * 2 + blk) % 2]
                for j in range(NJ):
                    k.mm(pt.v(), w[:, j, :], actT[j][:, blk * 512:(blk + 1) * 512], start=(j == 0), stop=(j == NJ - 1))
                resid_add(l, 5, P, d, blk, pt)
        arena_reset(m0)

    def phase_even(l, P, ec):
        j = l // 2
        lam_init = 0.8 - 0.6 * math.exp(-0.3 * l)
        m0 = arena_mark()
        h = alloc([128, 8, 1024], BF16, "h")
        phase_norm(l, gcols_mix[l], 0, 1, P, h)
        yo = alloc([128, 8, 1024], BF16, "yo")
        nseq, L, isB = P["nseq"], P["L"], (P["v"] == 1)
        B0, B1, B2, B3 = big
        P3bf = V(B3.ap.bitcast(BF16), (B3.key,))

        def ccol(name, i=0):
            return cols[:, ec[name] + i:ec[name] + i + 1]

        def out_proj(kbase, pts=None):
            specs = [([(w_out_e[j][kbase * 128:kbase * 128 + 1024, dg * 256:(dg + 1) * 256], 0, 256)], 8, 256) for dg in range(4)]
            wq = WSeq(specs)
            for d in range(8):
                w = wq.get(d // 2)
                for blk in range(2):
                    pt = pts[blk] if pts is not None else V(B0.ap[:, blk * 512:(blk + 1) * 512], (B0.key,))
                    for kc in range(8):
                        k.mm(pt, w[:, kc, (d % 2) * 128:(d % 2 + 1) * 128], V(yo.ap[:, kc, blk * 512:(blk + 1) * 512], (yo.key,)),
                             start=(kc == 0), stop=(kc == 7))
                    tok = slice(P["t0"] + blk * 512, P["t0"] + blk * 512 + 512)
                    k.stt(x[:, d, tok], pt, modcol(l, 2, d, P["v"]), x[:, d, tok], ALU.mult, ALU.add)

        ms = arena_mark()
        xbc = [alloc([128, 256], BF16, "xbc") for _ in range(10)]
        stg = [alloc([128, 262], F32, "stg") for _ in range(2)]
        acc = [alloc([128, 256], F32, "acc") for _ in range(2)]
        xsT = [alloc([128, 1024], BF16, "xsT") for _ in range(2)]
        BT = [alloc([128, 128], BF16, "BT") for _ in range(2)]
        sz = [alloc([128, 1024], BF16, "sz") for _ in range(2)]
        dtr = alloc([128, 2, 32], F32, "dtr"); dt = alloc([128, 2, 32], F32, "dt"); dta = alloc([128, 2, 32], F32, "dta")
        Scol = alloc([128, 2, 32], F32, "Scol"); eU = alloc([128, 2, 32], F32, "eU"); dend = alloc([128, 2, 32], F32, "dend")
        dec = alloc([128, 2, 32], F32, "dec"); wdd = alloc([128, 2, 32], F32, "wdd"); Utot = alloc([128, 2, 32], F32, "Utot")
        xdt = [alloc([128, 1024], BF16, "xdt") for _ in range(2)]
        xdd = xdt
        xD = alloc([128, 1024], BF16, "xD")
        CBT = alloc([128, 2, 128], BF16, "CBT")
        segT = alloc([128, 8, 128], F32, "segT")
        LTs = [alloc([128, 8, 128], BF16, "LT") for _ in range(2)]
        MT = [[alloc([128, 8, 128], BF16, "MT") for _ in range(2)] for _ in range(2)]
        ysb = alloc([128, 1024], F32, "ysb")
        tmpy = Tile(segT.ap.rearrange("p a b -> p (a b)"), segT.key)
        gn = alloc([128, 1024], BF16, "gn")
        junk = gn
        ssq = alloc([128, 2], F32, "ssq")
        Hs = [alloc([128, 512], F32, "Hs") for _ in range(2)]
        Hb16 = [[alloc([128, 512], BF16, "Hb16") for _ in range(2)] for _ in range(2)]
        Hent_b = [alloc([128, 512], BF16, "Hentb") for _ in range(8)] if isB else None
        stF = alloc([128, 4, 2, 64], F32, "stF")
        Dbc = V(cols.ap[:, ec["d"]:ec["d"] + 16].unsqueeze(2).broadcast_to([128, 16, 64]), (cols.key,))

        def v3(t, a):
            return V(t.ap.rearrange("p (a b) -> p a b", a=a), (t.key,))

        def group_prep(gi, need_z):
            g0 = gi * 256
            seq0 = (g0 // L) * L
            lo, hi = max(seq0, g0 - 2), min(seq0 + L, g0 + 257)
            n_in = hi - lo
            so = lo - (g0 - 2)
            specs = []
            if need_z:
                specs += [([(w_in_e[j][:, 0:512], 0, 512)], 8, 512), ([(w_in_e[j][:, 512:1024], 0, 512)], 8, 512)]
            specs += [([(w_in_e[j][:, 1024:1536], 0, 512)], 8, 512), ([(w_in_e[j][:, 1536:2048], 0, 512)], 8, 512),
                      ([(w_in_e[j][:, 2048:2336], 0, 288)], 8, 288)]
            wq = WSeq(specs)
            wi = 0
            if need_z:
                for half in range(2):
                    w = wq.get(wi); wi += 1
                    for tt in range(2):
                        pt = V(B3.ap[:, (tt % 2) * 512:(tt % 2) * 512 + 512], (B3.key,))
                        for kc in range(8):
                            k.mm(pt, h[:, kc, g0 + tt * 128:g0 + (tt + 1) * 128], w[:, kc, :], start=(kc == 0), stop=(kc == 7))
                        k.act(sz[tt][:, half * 512:(half + 1) * 512], pt, AF.Silu)
            pend_silu = []
            for c in range(10):
                if c % 4 == 0:
                    w = wq.get(wi); wi += 1
                st_, ac_ = stg[c % 2], acc[c % 2]
                pt = V(B0.ap[:, (c % 2) * 512:(c % 2) * 512 + n_in], (B0.key,))
                for kc in range(8):
                    k.mm(pt, w[:, kc, (c % 4) * 128:(c % 4 + 1) * 128], h[:, kc, lo:hi], start=(kc == 0), stop=(kc == 7))
                k.memset(st_.v(), 0.0)
                k.act(st_[:, so:so + n_in], pt, AF.Copy)
                cw, cb = ec["cw"] + c, ec["cb"] + c
                k.act(ac_.v(), st_[:, 0:256], AF.Identity, bias=cols[:, cb:cb + 1], scale=cols[:, cw:cw + 1])
                if pend_silu:
                    pend_silu.pop()()
                for kk in range(1, 4):
                    k.stt(ac_.v(), st_[:, kk:kk + 256], cols[:, cw + 10 * kk:cw + 10 * kk + 1], ac_.v(), ALU.mult, ALU.add)
                pend_silu.append(lambda c=c, ac_=ac_: k.act(xbc[c].v(), ac_.v(), AF.Silu))
            pend_silu.pop()()
            if int(os.environ.get("PREP_LEVEL", "9")) < 3:
                return
            pdt = V(B1.ap[:, 0:64].rearrange("p (a b) -> p a b", a=2), (B1.key,))
            pS = V(B1.ap[:, 64:128].rearrange("p (a b) -> p a b", a=2), (B1.key,))
            pU = V(B1.ap[:, 128:192].rearrange("p (a b) -> p a b", a=2), (B1.key,))
            steps = []
            def s1():
                for tt in range(2):
                    for kc in range(8):
                        k.mm(V(B1.ap[:, tt * 32:(tt + 1) * 32], (B1.key,)), h[:, kc, g0 + tt * 128:g0 + (tt + 1) * 128], w[:, kc, 256:288],
                             start=(kc == 0), stop=(kc == 7))
            steps.append(s1)
            steps.append(lambda: k.tt(dtr.v(), pdt, V(cols.ap[:, ec["dtb"]:ec["dtb"] + 32].unsqueeze(1).broadcast_to([128, 2, 32]), (cols.key,)), ALU.add))
            steps.append(lambda: k.act(dtr.v(), dtr.v(), AF.Exp))
            steps.append(lambda: k.act(dt.v(), dtr.v(), AF.Ln, bias=1.0))
            steps.append(lambda: k.tt(dta.v(), dt.v(), V(cols.ap[:, ec["abc"]:ec["abc"] + 32].unsqueeze(1).broadcast_to([128, 2, 32]), (cols.key,)), ALU.mult))
            def s6():
                for tt in range(2):
                    for dr in range(2):
                        k.mm(V(B1.ap[:, 64 + tt * 32 + dr * 16:64 + tt * 32 + dr * 16 + 16], (B1.key,)), Tdir[dr], dta[:, tt, dr * 16:(dr + 1) * 16])
                    k.mm(V(B1.ap[:, 128 + tt * 32:128 + (tt + 1) * 32], (B1.key,)), ones, dta[:, tt, :])
            steps.append(s6)
            steps.append(lambda: k.copy(Scol.v(), pS))
            steps.append(lambda: k.copy(Utot.v(), pU))
            steps.append(lambda: k.act(eU.v(), Scol.v(), AF.Exp))
            steps.append(lambda: k.act(dec.v(), Utot.v(), AF.Exp))
            steps.append(lambda: k.tt(dend.v(), Utot.v(), Scol.v(), ALU.subtract))
            steps.append(lambda: k.act(dend.v(), dend.v(), AF.Exp))
            steps.append(lambda: k.tt(wdd.v(), dend.v(), dt.v(), ALU.mult))
            for st_i, st_f in enumerate(steps):
                if st_i < int(os.environ.get("PREP_STEPS", "99")):
                    st_f()
            if int(os.environ.get("PREP_LEVEL", "9")) < 4:
                return
            for tt in range(2):
                for c in range(8):
                    k.tr(V(P3bf.ap[:, c * 128:(c + 1) * 128], (B3.key,)), xbc[c][:, tt * 128:(tt + 1) * 128], identb)
                k.tr(V(P3bf.ap[:, 1024:1152], (B3.key,)), xbc[8][:, tt * 128:(tt + 1) * 128], identb)
                k.act(xsT[tt].v(), V(P3bf.ap[:, 0:1024], (B3.key,)), AF.Copy)
                k.act(BT[tt].v(), V(P3bf.ap[:, 1024:1152], (B3.key,)), AF.Copy)

        def bc16(t, tt, dr):
            return V(t.ap[:, tt, dr * 16:(dr + 1) * 16].unsqueeze(2).broadcast_to([128, 16, 64]), (t.key,))

        SSD_LEVEL = int(os.environ.get("SSD_LEVEL", "3"))

        def chunk_states(tt, dr, pst):
            if SSD_LEVEL < 2:
                return
            k.tt(v3(xdd[dr], 16), v3(xsT[tt], 16), bc16(wdd, tt, dr), ALU.mult, eng="dve")
            for g in range(2):
                k.mm(V(pst.ap[g * 64:(g + 1) * 64, :], pst.keys), BT[tt][:, g * 64:(g + 1) * 64], xdd[dr][:, g * 512:(g + 1) * 512])

        def state_step(Ht, tt, dr, pst, have):
            if SSD_LEVEL < 2:
                return
            if not have:
                k.copy(Ht.v(), pst)
                return
            for g in range(2):
                hv = V(Ht.ap[g * 64:(g + 1) * 64, :].rearrange("p (a b) -> p a b", a=8), (Ht.key,))
                dv = V(dec.ap[g * 64:(g + 1) * 64, tt, dr * 16 + g * 8:dr * 16 + g * 8 + 8].unsqueeze(2).broadcast_to([64, 8, 64]), (dec.key,))
                k.tt(hv, hv, dv, ALU.mult)
            k.tt(Ht.v(), Ht.v(), pst, ALU.add)

        def chunk_y(gi, tt, ent):
            tok = slice(gi * 256 + tt * 128, gi * 256 + (tt + 1) * 128)
            tl = slice(tt * 128, (tt + 1) * 128)
            if SSD_LEVEL < 3:
                return
            YS = int(os.environ.get("Y_STEPS", "99"))
            for g in range(2):
                k.mm(V(B3.ap[:, g * 512:g * 512 + 128], (B3.key,)), xbc[8][g * 64:(g + 1) * 64, tl], xbc[9][g * 64:(g + 1) * 64, tl])
            for g in range(2):
                k.copy(CBT[:, g, :], V(B3.ap[:, g * 512:g * 512 + 128], (B3.key,)))
            DB = debug and (not isB) and gi == 0 and tt == 0
            if DB:
                dbg("CBT", V(CBT.ap.rearrange("p a b -> p (a b)"), (CBT.key,)), [128, 256])
                dbg("xsT", xsT[tt].v(), [128, 1024])
                dbg("dt", V(dt.ap.rearrange("p a b -> p (a b)"), (dt.key,)), [128, 64])
                dbg("Scol", V(Scol.ap.rearrange("p a b -> p (a b)"), (Scol.key,)), [128, 64])
            if YS < 2:
                return
            for dr in range(2):
                k.tt(v3(xdt[dr], 16), v3(xsT[tt], 16), bc16(dt, tt, dr), ALU.mult, eng="dve")
            k.tt(v3(xD, 16), v3(xsT[tt], 16), Dbc, ALU.mult, eng="dve")
            if YS < 3:
                return
            its = [(0, 0), (0, 1), (1, 0), (1, 1)]

            def stage_a(i):
                g, dr = its[i]
                pb = big[dr]
                pbv = V(pb.ap.rearrange("p (a b) -> p a b", a=8), (pb.key,))
                for h8 in range(8):
                    hd = dr * 16 + g * 8 + h8
                    k.mm(V(pb.ap[:, h8 * 128:(h8 + 1) * 128], (pb.key,)), V(dta.ap[:, tt, hd:hd + 1].broadcast_to([128, 128]), (dta.key,)),
                         Tdir[dr], start=True, stop=False)
                    k.mm(V(pb.ap[:, h8 * 128:(h8 + 1) * 128], (pb.key,)), identb, maskb[dr], start=False, stop=True)
                hd0 = dr * 16 + g * 8
                for bk in range(2):
                    k.tt(segT[:, bk * 4:(bk + 1) * 4, :], V(pbv.ap[:, bk * 4:(bk + 1) * 4, :], pbv.keys),
                         V(Scol.ap[:, tt, hd0 + bk * 4:hd0 + bk * 4 + 4].unsqueeze(2).broadcast_to([128, 4, 128]), (Scol.key,)), ALU.subtract)
                k.act(LTs[i % 2].v(), segT.v(), AF.Exp)

            def stage_b(i):
                g, dr = its[i]
                k.tt(MT[g][dr].v(), LTs[i % 2].v(), V(CBT.ap[:, g, :].unsqueeze(1).broadcast_to([128, 8, 128]), (CBT.key,)), ALU.mult, eng="dve")
                if DB and g == 0:
                    dbg("MT%d" % dr, V(MT[g][dr].ap.rearrange("p a b -> p (a b)"), (MT[g][dr].key,)), [128, 1024])

            stage_a(0)
            for i in range(4):
                if i + 1 < 4:
                    stage_a(i + 1)
                stage_b(i)
                g, dr = its[i]
                if dr == 0 or YS < 4:
                    continue
                for h8 in range(8):
                    hsl = slice((g * 8 + h8) * 64, (g * 8 + h8 + 1) * 64)
                    yv = V(B2.ap[:, hsl], (B2.key,))
                    k.mm(yv, identb, xD[:, hsl], start=True, stop=False)
                    k.mm(yv, MT[g][0][:, h8, :], xdt[0][:, hsl], start=False, stop=False)
                    k.mm(yv, MT[g][1][:, h8, :], xdt[1][:, hsl], start=False, stop=True)
            if YS < 5:
                return
            for bk in range(2):
                k.act(ysb[:, bk * 512:(bk + 1) * 512], B2[:, bk * 512:(bk + 1) * 512], AF.Copy)
            if YS < 6:
                return
            if DB:
                dbg("ydiag", ysb.v(), [128, 1024])
            for dr in range(2):
                if ent[dr] is None:
                    continue
                for g in range(2):
                    k.mm(V(B3.ap[:, g * 512:(g + 1) * 512], (B3.key,)), xbc[9][g * 64:(g + 1) * 64, tl], ent[dr][g * 64:(g + 1) * 64, :])
                for bk in range(2):
                    k.tt(V(tmpy.ap[:, bk * 512:(bk + 1) * 512].rearrange("p (a b) -> p a b", a=8), (tmpy.key,)),
                         V(B3.ap[:, bk * 512:(bk + 1) * 512].rearrange("p (a b) -> p a b", a=8), (B3.key,)),
                         V(eU.ap[:, tt, dr * 16 + bk * 8:dr * 16 + bk * 8 + 8].unsqueeze(2).broadcast_to([128, 8, 64]), (eU.key,)), ALU.mult)
                k.tt(ysb.v(), ysb.v(), tmpy.v(), ALU.add, eng="dve")
            if DB:
                dbg("ysb", ysb.v(), [128, 1024])
            if YS < 7:
                return
            k.tt(ysb.v(), ysb.v(), sz[tt].v(), ALU.mult)
            k.memset(ssq[:, 0:1], 0.0)
            k.act(junk.v(), ysb.v(), AF.Square, accum=ssq[:, 0:1])
            k.act(ssq[:, 1:2], ssq[:, 0:1], AF.Ln, bias=EPS, scale=1.0 / 1024)
            k.act(ssq[:, 1:2], ssq[:, 1:2], AF.Exp, scale=-0.5)
            k.act(gn.v(), ysb.v(), AF.Copy, scale=ssq[:, 1:2])
            for c in range(8):
                k.tr(V(P3bf.ap[:, c * 128:(c + 1) * 128], (B3.key,)), gn[:, c * 128:(c + 1) * 128], identb)
            k.tt(V(yo.ap[:, 0:8, tok], (yo.key,)), V(P3bf.ap[:, 0:1024].rearrange("p (a b) -> p a b", a=8), (B3.key,)),
                 V(cols.ap[:, ec["ng"]:ec["ng"] + 8].unsqueeze(2).broadcast_to([128, 8, 128]), (cols.key,)), ALU.mult)

        def write_state(Ht, dst):
            if SSD_LEVEL < 2:
                return
            for pr in range(4):
                k.tr(V(B3.ap[:, pr * 128:(pr + 1) * 128], (B3.key,)), Ht[:, pr * 128:(pr + 1) * 128], ident)
            k.copy(V(stF.ap.rearrange("p a b c -> p (a b c)"), (stF.key,)), V(B3.ap[:, 0:512], (B3.key,)))
            dv_ = dst.rearrange("(g pr h2) p n -> g (h2 p) pr n", g=2, pr=4)
            for g in range(2):
                k.dma(dv_[g], V(stF.ap[:, :, g, :], (stF.key,)), chan="st")

        pstates = [V(B3.ap[:, 0:512], (B3.key,)), V(B3.ap[:, 512:1024], (B3.key,))]
        SKIP_SSD = bool(os.environ.get("SKIP_SSD")); SKIP_ATT = bool(os.environ.get("SKIP_ATT"))
        if SKIP_SSD:
            pass
        elif not isB:
            for s in range(nseq):
                group_prep(s, True)
                chunk_states(0, 0, pstates[0]); state_step(Hs[0], 0, 0, pstates[0], False)
                k.copy(Hb16[0][1].v(), Hs[0].v(), eng="act")
                chunk_states(1, 1, pstates[1]); state_step(Hs[1], 1, 1, pstates[1], False)
                k.copy(Hb16[1][0].v(), Hs[1].v(), eng="act")
                chunk_states(1, 0, pstates[0]); state_step(Hs[0], 1, 0, pstates[0], True)
                write_state(Hs[0], sf_out[s, j])
                chunk_states(0, 1, pstates[1]); state_step(Hs[1], 0, 1, pstates[1], True)
                write_state(Hs[1], sb_out[s, j])
                chunk_y(s, 0, [None, Hb16[1][0]])
                chunk_y(s, 1, [Hb16[0][1], None])
        else:
            for dr, src in enumerate((ssd_f0, ssd_b0)):
                sv_ = src[j].rearrange("(g pr h2) p n -> g (h2 p) pr n", g=2, pr=4)
                for g in range(2):
                    k.dma(V(stF.ap[:, :, g, :], (stF.key,)), sv_[g], chan="ld")
                for pr in range(4):
                    k.tr(V(B3.ap[:, pr * 128:(pr + 1) * 128], (B3.key,)), V(stF.ap[:, pr, :, :].rearrange("p a b -> p (a b)"), (stF.key,)), ident)
                k.copy(Hs[dr].v(), V(B3.ap[:, 0:512], (B3.key,)))
            for gi in (3, 2, 1, 0):
                group_prep(gi, False)
                for tt in (1, 0):
                    k.copy(Hent_b[gi * 2 + tt].v(), Hs[1].v(), eng="act")
                    if gi * 2 + tt > 0:
                        chunk_states(tt, 1, pstates[tt]); state_step(Hs[1], tt, 1, pstates[tt], True)
            for gi in range(4):
                group_prep(gi, True)
                for tt in range(2):
                    k.copy(Hb16[0][tt].v(), Hs[0].v(), eng="act")
                    if gi * 2 + tt < 7:
                        chunk_states(tt, 0, pstates[tt]); state_step(Hs[0], tt, 0, pstates[tt], True)
                for tt in range(2):
                    chunk_y(gi, tt, [Hb16[0][tt], Hent_b[gi * 2 + tt]])
        if not SKIP_SSD:
            out_proj(0)
        arena_reset(ms)

        nkt = 12 if isB else 8
        nk = nkt * 128
        koff = 4 if isB else 0
        qT = [alloc([128, 1024], BF16, "qT") for _ in range(4)]
        kT = [alloc([128, nk], BF16, "kT") for _ in range(4)]
        vaug = alloc([128, nkt, 4, 130], BF16, "vaug")
        PT = [alloc([128, 512], BF16, "PT") for _ in range(3)]
        raw = [alloc([128, 512], BF16, "raw") for _ in range(2)]
        t1 = alloc([128, 512], F32, "t1"); t2 = alloc([128, 512], F32, "t2")
        ost = alloc([128, 512], F32, "ost")
        o_t = alloc([128, 4, 128], F32, "o_t"); o_n = alloc([128, 4, 128], BF16, "o_n")
        ojunk = Tile(t1.ap.rearrange("p (a b) -> p a b", a=4), t1.key)
        Osb = [alloc([128, 4, 130], F32, "Osb") for _ in range(2)]
        rs = alloc([128, 2, 4], F32, "rs"); rs2 = alloc([128, 2, 4], F32, "rs2"); sso = alloc([128, 2, 4], F32, "sso")
        if isB:
            rope = alloc([128, 2, 1024], F32, "rope")
            k.dma(rope.v(), ropetab.rearrange("a p n -> p a n"), chan="ld")
            ckst = alloc([128, 4, 512], F32, "ckst")
        Sbank = [Tile(B0.ap[:, 0:512], "B0_lo"), Tile(B0.ap[:, 512:1024], "B0_hi")]
        for hg in range(0 if SKIP_ATT else 2):
            specs = [([(w_in_e[j][:, c0 + hg * 512:c0 + (hg + 1) * 512], 0, 512)], 8, 512) for c0 in (C_Q0, C_K0, C_V0)]
            wq = WSeq(specs)
            wQ, wK, wV = wq.get(0), wq.get(1), wq.get(2)
            k.memset(V(vaug.ap[:, :, :, 128:130], (vaug.key,)), 1.0)
            if isB:
                k.dma(ckst.v(), cache_k[j][:, hg * 512:(hg + 1) * 512].rearrange("(a p) n -> p a n", p=128), chan="ld")
                for kt in range(4):
                    k.dma(V(vaug.ap[:, kt, :, 0:128], (vaug.key,)),
                          cache_v[j][kt * 128:(kt + 1) * 128, hg * 512:(hg + 1) * 512].rearrange("p (a b) -> p a b", a=4), chan="cv", q="pool")
                    for hh in range(4):
                        k.tr(V(B3.ap[:, hh * 128:(hh + 1) * 128], (B3.key,)), ckst[:, kt, hh * 128:(hh + 1) * 128], ident)
                    for hh in range(4):
                        k.copy(kT[hh][:, kt * 128:(kt + 1) * 128], V(B3.ap[:, hh * 128:(hh + 1) * 128], (B3.key,)), eng="act")
            for which, (wt, dstT, doff) in enumerate(((wQ, qT, 0), (wK, kT, koff * 128))):
                for hh in range(4):
                    for blk in range(2):
                        pt = Sbank[blk].v()
                        for kc in range(8):
                            k.mm(pt, wt[:, kc, hh * 128:(hh + 1) * 128], h[:, kc, blk * 512:(blk + 1) * 512], start=(kc == 0), stop=(kc == 7))
                        dst = dstT[hh][:, doff + blk * 512:doff + (blk + 1) * 512]
                        if not isB:
                            k.act(dst, pt, AF.Copy)
                        else:
                            rw = raw[blk]
                            k.act(rw.v(), pt, AF.Copy)
                            p2 = V(B1.ap[:, blk * 512:(blk + 1) * 512], (B1.key,))
                            k.mm(p2, Rb, rw.v())
                            k.tt(t1.v(), rw.v(), rope[:, 0, blk * 512:(blk + 1) * 512], ALU.mult, eng="dve")
                            k.tt(t2.v(), p2, rope[:, 1, blk * 512:(blk + 1) * 512], ALU.mult)
                            k.tt(dst, t1.v(), t2.v(), ALU.add)
            for t in range(8):
                pt = V(B2.ap[:, (t % 2) * 512:(t % 2) * 512 + 512], (B2.key,))
                for kc in range(8):
                    k.mm(pt, h[:, kc, t * 128:(t + 1) * 128], wV[:, kc, :], start=(kc == 0), stop=(kc == 7))
                k.act(V(vaug.ap[:, koff + t, :, 0:128], (vaug.key,)), V(pt.ap.rearrange("p (a b) -> p a b", a=4), pt.keys), AF.Copy)
                if not isB:
                    s, tl = t // 2, (t % 2) * 128
                    k.copy(ost.v(), pt, eng="act")
                    k.dma(nv_out[s, j, tl:tl + 128, hg * 512:(hg + 1) * 512], ost.v(), chan="stv")
                    pk = V(B3.ap[:, (t % 2) * 512:(t % 2) * 512 + 512], (B3.key,))
                    for kc in range(8):
                        k.mm(pk, h[:, kc, t * 128:(t + 1) * 128], wK[:, kc, :], start=(kc == 0), stop=(kc == 7))
                    k.copy(ost.v(), pk)
                    k.dma(nk_out[s, j, tl:tl + 128, hg * 512:(hg + 1) * 512], ost.v(), chan="stk")
            if isB:
                qblocks = [(0, 512, list(range(12))), (512, 512, list(range(12)))]
            else:
                qblocks = [(s * 256, 256, [2 * s, 2 * s + 1]) for s in range(4)]
            pti = 0
            oslots = [V(B1.ap[:, 0:129], (B1.key,)), V(B1.ap[:, 512:641], (B1.key,)),
                      V(B2.ap[:, 0:129], (B2.key,)), V(B2.ap[:, 512:641], (B2.key,))]
            for hh in range(4):
                for (q0, nq, kts) in qblocks:
                    nqt = nq // 128
                    for c in range(2):
                        def s_mm(ki, kt):
                            S = Sbank[ki % 2]
                            k.mm(V(S.ap[:, 0:nq], (S.key,)), kT[hh][c * 64:(c + 1) * 64, kt * 128:(kt + 1) * 128], qT[hh][c * 64:(c + 1) * 64, q0:q0 + nq])
                        s_mm(0, kts[0])
                        for ki, kt in enumerate(kts):
                            S = Sbank[ki % 2]
                            pt_ = PT[pti % 3]; pti += 1
                            k.act(pt_[:, 0:nq], V(S.ap[:, 0:nq], (S.key,)), AF.Exp, scale=ATT_SCALE)
                            if ki + 1 < len(kts):
                                s_mm(ki + 1, kts[ki + 1])
                            for qt in range(nqt):
                                k.mm(oslots[qt], pt_[:, qt * 128:(qt + 1) * 128],
                                     V(vaug.ap[:, kt, hh, 0:129], (vaug.key,)), start=(ki == 0), stop=(ki == len(kts) - 1))
                        for qt in range(nqt):
                            k.copy(Osb[c][:, qt, 0:129], oslots[qt])
                    for c in range(2):
                        k.recip(rs[:, c, 0:nqt], V(Osb[c].ap[:, 0:nqt, 128], (Osb[c].key,)))
                    k.act(rs2[:, 0, 0:nqt], rs[:, 0, 0:nqt], AF.Copy)
                    k.act(rs2[:, 1, 0:nqt], rs[:, 1, 0:nqt], AF.Copy, scale=ccol("nlam"))
                    o3 = V(o_t.ap[:, 0:nqt, :], (o_t.key,))
                    k.tt(o3, Osb[0][:, 0:nqt, 0:128], V(rs2.ap[:, 0, 0:nqt].unsqueeze(2).broadcast_to([128, nqt, 128]), (rs2.key,)), ALU.mult)
                    k.tt(V(ojunk.ap[:, 0:nqt, :], (ojunk.key,)), Osb[1][:, 0:nqt, 0:128],
                         V(rs2.ap[:, 1, 0:nqt].unsqueeze(2).broadcast_to([128, nqt, 128]), (rs2.key,)), ALU.mult)
                    k.tt(o3, o3, V(ojunk.ap[:, 0:nqt, :], (ojunk.key,)), ALU.add)
                    k.tt(V(ojunk.ap[:, 0:nqt, :], (ojunk.key,)), o3, o3, ALU.mult)
                    k.op("dve", (lambda nqt=nqt: nc.vector.reduce_sum(out=sso.ap[:, 0, 0:nqt], in_=ojunk.ap[:, 0:nqt, :], axis=mybir.AxisListType.X)),
                         reads=[ojunk.key], writes=[sso.key], osize=nqt)
                    k.act(sso[:, 1, 0:nqt], sso[:, 0, 0:nqt], AF.Ln, bias=EPS, scale=1.0 / 128)
                    k.act(sso[:, 1, 0:nqt], sso[:, 1, 0:nqt], AF.Exp, scale=-0.5)
                    k.tt(V(o_n.ap[:, 0:nqt, :], (o_n.key,)), o3, V(sso.ap[:, 1, 0:nqt].unsqueeze(2).broadcast_to([128, nqt, 128]), (sso.key,)), ALU.mult)
                    for qt in range(nqt):
                        k.tr(V(P3bf.ap[:, qt * 128:(qt + 1) * 128], (B3.key,)), o_n[:, qt, :], identb)
                    k.ts(V(yo.ap[:, hg * 4 + hh, q0:q0 + nq].rearrange("p (a b) -> p a b", a=nqt), (yo.key,)),
                         V(P3bf.ap[:, 0:nq].rearrange("p (a b) -> p a b", a=nqt), (B3.key,)), ccol("sgl"), None, ALU.mult)
        if not SKIP_ATT:
            out_proj(8, pts=[Sbank[0].v(), Sbank[1].v()])
        arena_reset(m0)

    def phase_odd(l, P, pc):
        j = l // 2
        m0 = arena_mark()
        h = alloc([128, 8, 1024], BF16, "h")
        phase_norm(l, gcols_mix[l], 0, 1, P, h)
        nseq, L = P["nseq"], P["L"]
        gg = [alloc([128, 1024], BF16, "gg") for _ in range(8)]
        xr = [alloc([128, 1024], BF16, "xr") for _ in range(8)]
        stg = alloc([128, nseq, L + 3], F32, "stg")
        k.memset(stg.v(), 0.0)
        bd = alloc([128, 4, 8, 128], BF16, "bd")
        k.memset(bd.v(), 0.0)
        for g, (src, dr) in enumerate(((lru_wa, 0), (lru_wx, 0), (lru_wa, 1), (lru_wx, 1))):
            sv = src[j, dr].rearrange("(c two) kk jj -> two kk c jj", two=2)
            for half in range(2):
                k.dma(V(bd.ap[half * 64:(half + 1) * 64, g, :, half * 64:(half + 1) * 64], (bd.key,)), sv[half], chan="bd", q="pool")
        tAll = alloc([128, 1024], F32, "tAll")
        tA = [Tile(tAll.ap[:, i * 512:(i + 1) * 512], tAll.key) for i in range(2)]
        specs = [([(lru_w_in[j][:, g * 512:(g + 1) * 512], 0, 512)], 8, 512) for g in range(4)]
        specs += [([(lru_w_out[j][:, g * 512:(g + 1) * 512], 0, 512)], 8, 512) for g in range(2)]
        wq = WSeq(specs)
        acc = Tile(tAll.ap.rearrange("p (a b) -> p a b", a=nseq), tAll.key)
        for c in range(8):
            w = wq.get(c // 4)
            for blk in range(2):
                pt = ps[blk]
                for kc in range(8):
                    k.mm(pt.v(), w[:, kc, (c % 4) * 128:(c % 4 + 1) * 128], h[:, kc, blk * 512:(blk + 1) * 512], start=(kc == 0), stop=(kc == 7))
                a = tA[blk]
                k.act(a.v(), pt.v(), AF.Square)
                k.ts(a.v(), a.v(), 0.044715, 1.0, ALU.mult, ALU.add)
                k.tt(a.v(), a.v(), pt.v(), ALU.mult)
                k.act(a.v(), a.v(), AF.Sigmoid, scale=2.0 * 0.7978845608028654)
                k.tt(gg[c][:, blk * 512:(blk + 1) * 512], a.v(), pt.v(), ALU.mult)
        for c in range(8):
            w = wq.get(2 + c // 4)
            pl = [ps[2], ps[3]]
            for blk in range(2):
                for kc in range(8):
                    k.mm(pl[blk].v(), w[:, kc, (c % 4) * 128:(c % 4 + 1) * 128], h[:, kc, blk * 512:(blk + 1) * 512], start=(kc == 0), stop=(kc == 7))
            conv_fm(pl, P, stg, 4, 2, pc["cw"] + c, 8, pc["cb"] + c, acc)
            k.copy(V(xr[c].ap.rearrange("p (a b) -> p a b", a=nseq), (xr[c].key,)), acc.v())
            if c == 0 and l == 1:
                dbg("xr0_%d" % P["v"], xr[0].v(), [128, 1024])
                dbg("gg0_%d" % P["v"], gg[0].v(), [128, 1024])
                dbg("h0_%d" % P["v"], h[:, 0, :], [128, 1024])
        rr = alloc([128, 1024], F32, "rr"); ii = alloc([128, 1024], F32, "ii")
        aa = alloc([128, 1024], F32, "aa"); uu = alloc([128, 1024], F32, "uu")
        hhb = [[alloc([128, 1024], F32, "hh") for _ in range(2)] for _ in range(2)]
        lst = alloc([128, 8, 2, NP_SEQ], F32, "lst")
        h0 = alloc([128, 2, 8], F32, "h0")
        USE_H0 = (P["v"] == 1) and not os.environ.get("NOH0")
        if USE_H0:
            for dr, src in enumerate((lru_f0, lru_b0)):
                k.dma(stage[0:8, :], rows(src[j], 8), chan="ld")
                k.tr(ps[7][:, 0:8], stage[0:8, :], V(cst.ap[0:8, 0, 0:8], (cst.key,)))
                k.copy(h0[:, dr, :], ps[7][:, 0:8])
            if debug:
                od = nc.dram_tensor("dbg_h0s", [128, 16], F32, kind="ExternalOutput").ap()
                k.dma(od, V(h0.ap.rearrange("p a b -> p (a b)"), (h0.key,)), chan="dbg")
        aaD = [aa, Tile(stg.ap.rearrange("p a b -> p (a b)")[:, 0:1024], stg.key)]
        uuD = [uu, Tile(tAll.ap, tAll.key)]

        def finish(c):
            hh = hhb[c % 2]
            k.tt(rr.v(), hh[0].v(), hh[1].v(), ALU.add)
            if P["v"] == 0:
                for s_ in range(nseq):
                    for dr_, col_ in ((0, (s_ + 1) * L - 1), (1, s_ * L)):
                        o_ap = lst.ap[:, c, dr_, s_:s_ + 1]
                        i_ap = hh[dr_].ap[:, col_:col_ + 1]
                        k.op("act", (lambda o_ap=o_ap, i_ap=i_ap: nc.scalar.activation(out=o_ap, in_=i_ap, func=AF.Copy)),
                             reads=[hh[dr_].key, rr.key], writes=[lst.key], osize=1)
            k.tt(h[:, c, :], rr.v(), gg[c].v(), ALU.mult)

        for c in range(8):
            hh = hhb[c % 2]
            for dr in range(2):
                for gi, dst in ((0, rr), (1, ii)):
                    g = dr * 2 + gi
                    bcolx = (pc["ba"] if gi == 0 else pc["bx"]) + dr * 8 + c
                    for blk in range(2):
                        pt = ps[(g * 2 + blk) % 4]
                        k.mm(pt.v(), V(bd.ap[:, g, c, :], (bd.key,)), xr[c][:, blk * 512:(blk + 1) * 512])
                        k.act(dst[:, blk * 512:(blk + 1) * 512], pt.v(), AF.Sigmoid, bias=cols[:, bcolx:bcolx + 1])
                lc = pc["nc8"] + dr * 8 + c
                k.act(aaD[dr].v(), rr.v(), AF.Exp, scale=cols[:, lc:lc + 1])
                k.act(uuD[dr].v(), aaD[dr].v(), AF.Square)
                k.act(uuD[dr].v(), uuD[dr].v(), AF.Identity, bias=1.0, scale=-1.0)
                k.act(uuD[dr].v(), uuD[dr].v(), AF.Sqrt)
                k.tt(uuD[dr].v(), uuD[dr].v(), ii.v(), ALU.mult)
                k.tt(uuD[dr].v(), uuD[dr].v(), xr[c].v(), ALU.mult)
                if c == 0 and l == 1 and dr == 0:
                    dbg("rr_%d" % P["v"], rr.v(), [128, 1024]); dbg("ii_%d" % P["v"], ii.v(), [128, 1024])
                    dbg("aa_%d" % P["v"], aaD[dr].v(), [128, 1024]); dbg("uu_%d" % P["v"], uuD[dr].v(), [128, 1024])
                for s in range(nseq):
                    sl = slice(s * L, (s + 1) * L)
                    first = s * L if dr == 0 else (s + 1) * L - 1
                    if USE_H0:
                        k.act(uuD[dr][:, first:first + 1], aaD[dr][:, first:first + 1], AF.Identity, bias=uuD[dr][:, first:first + 1], scale=h0[:, dr, c:c + 1])
                    if dr == 0:
                        k.scan(hh[0][:, sl], aaD[dr][:, sl], uuD[dr][:, sl], 0.0)
                    else:
                        rs = slice((s + 1) * L - 1, s * L - 1 if s > 0 else None, -1)
                        k.scan(V(hh[1].ap[:, rs], (hh[1].key,)), V(aaD[dr].ap[:, rs], (aaD[dr].key,)), V(uuD[dr].ap[:, rs], (uuD[dr].key,)), 0.0)
            if c >= 1:
                finish(c - 1)
        finish(7)
        if P["v"] == 0:
            k.tr(ps[7][0:64, 0:128], V(lst.ap.rearrange("p a b c -> p (a b c)"), (lst.key,)), ident)
            lrow = alloc([64, 128], F32, "lrow")
            k.copy(lrow.v(), ps[7][0:64, 0:128])
            if debug:
                od = nc.dram_tensor("dbg_lst", [128, 64], F32, kind="ExternalOutput").ap()
                k.dma(od, V(lst.ap.rearrange("p a b c -> p (a b c)"), (lst.key,)), chan="dbg")
                od2 = nc.dram_tensor("dbg_lrow", [64, 128], F32, kind="ExternalOutput").ap()
                k.dma(od2, lrow.v(), chan="dbg")
            for dr, dst in enumerate((lf_out, lb_out)):
                for c in range(8):
                    r0 = c * 8 + dr * 4
                    k.dma(dst[:, j, c * 128:(c + 1) * 128], lrow[r0:r0 + 4, :], chan="st")
        if l == 1:
            dbg("yo0_%d" % P["v"], h[:, 0, :], [128, 1024])
            dbg("yo5_%d" % P["v"], h[:, 5, :], [128, 1024])
        for d in range(8):
            w = wq.get(4 + d // 4)
            for blk in range(2):
                pt = ps[4 + (d * 2 + blk) % 2]
                for kc in range(8):
                    k.mm(pt.v(), w[:, kc, (d % 4) * 128:(d % 4 + 1) * 128], h[:, kc, blk * 512:(blk + 1) * 512], start=(kc == 0), stop=(kc == 7))
                resid_add(l, 2, P, d, blk, pt)
        if l == 1:
            dbg("xm0_%d" % P["v"], x[:, 0, P["t0"]:P["t0"] + 1024], [128, 1024])
        arena_reset(m0)

    gcols_mix, gcols_ffn, fcw, fcb = [], [], [], []
    ocols = {}
    ecols = {}
    for l in range(nlayers):
        gcols_mix.append(load_cols(rows(norm_mix_g[l], 8), 8))
        gcols_ffn.append(load_cols(rows(norm_ffn_g[l], 8), 8))
        fcw.append(load_cols(ffn_conv_w[l].rearrange("w (r c) -> (w r) c", c=128), 3 * 2 * NJ))
        fcb.append(load_cols(rows(ffn_conv_b[l], 2 * NJ), 2 * NJ))
        if l % 2 == 0:
            j = l // 2
            ec = {}
            ec["cw"] = load_cols(conv_w_e[j].rearrange("w (r c) -> (w r) c", c=128), 40)
            ec["cb"] = load_cols(rows(conv_b_e[j], 10), 10)
            ec["ng"] = load_cols(rows(ssd_norm_g[j], 8), 8)
            dn = load_cols(rows(diff_norm_g[j], 1), 1)
            base = colstate["n"]
            colstate["n"] += 32 + 32 + 16 + 256 + 32 + 8
            assert colstate["n"] <= NCOL
            ec["dtb"], alg, ec["d"], lp = base, base + 32, base + 64, base + 80
            ec["abc"] = base + 336
            sc = base + 368
            k.dma(cols[:, ec["dtb"]:ec["dtb"] + 32], dt_bias[j:j + 1, :].partition_broadcast(128), chan="ld")
            k.dma(cols[:, alg:alg + 32], a_log[j:j + 1, :].partition_broadcast(128), chan="ld")
            k.dma(cols[:, ec["d"]:ec["d"] + 16], ssd_d[j:j + 1, :].partition_broadcast(128), chan="ld")
            k.dma(cols[:, lp:lp + 256], diff_lambda[j:j + 1, :].partition_broadcast(128), chan="ld")
            k.act(cols[:, ec["abc"]:ec["abc"] + 32], cols[:, alg:alg + 32], AF.Exp)
            k.ts(cols[:, ec["abc"]:ec["abc"] + 32], cols[:, ec["abc"]:ec["abc"] + 32], -1.0, None, ALU.mult)
            lam_init = 0.8 - 0.6 * math.exp(-0.3 * l)
            k.tt(cols[:, lp:lp + 64], cols[:, lp:lp + 64], cols[:, lp + 64:lp + 128], ALU.mult)
            k.tt(cols[:, lp + 128:lp + 192], cols[:, lp + 128:lp + 192], cols[:, lp + 192:lp + 256], ALU.mult)
            k.memset(cols[:, sc:sc + 2], 0.0)
            k.act(cols[:, lp + 64:lp + 128], cols[:, lp:lp + 64], AF.Copy, accum=cols[:, sc:sc + 1])
            k.act(cols[:, lp + 192:lp + 256], cols[:, lp + 128:lp + 192], AF.Copy, accum=cols[:, sc + 1:sc + 2])
            k.act(cols[:, sc + 2:sc + 4], cols[:, sc:sc + 2], AF.Exp)
            k.tt(cols[:, sc + 4:sc + 5], cols[:, sc + 2:sc + 3], cols[:, sc + 3:sc + 4], ALU.subtract)
            k.act(cols[:, sc + 5:sc + 6], cols[:, sc + 4:sc + 5], AF.Identity, bias=-lam_init, scale=-1.0)
            k.act(cols[:, sc + 6:sc + 7], cols[:, dn:dn + 1], AF.Copy, scale=(1.0 - lam_init))
            ec["nlam"], ec["sgl"] = sc + 5, sc + 6
            ecols[l] = ec
        if l % 2 == 1:
            j = l // 2
            pc = {}
            pc["cw"] = load_cols(lru_conv_w[j].rearrange("w (r c) -> (w r) c", c=128), 32)
            pc["cb"] = load_cols(rows(lru_conv_b[j], 8), 8)
            pc["ba"] = load_cols(lru_ba[j].rearrange("d (r c) -> (d r) c", c=128), 16)
            pc["bx"] = load_cols(lru_bx[j].rearrange("d (r c) -> (d r) c", c=128), 16)
            lam = load_cols(lru_lambda[j].rearrange("d (r c) -> (d r) c", c=128), 16)
            pc["nc8"] = colstate["n"]
            colstate["n"] += 16
            dst = cols[:, pc["nc8"]:pc["nc8"] + 16]
            k.act(dst, cols[:, lam:lam + 16], AF.Exp, scale=-1.0)
            k.act(dst, dst, AF.Ln, bias=1.0)
            k.ts(dst, dst, -8.0, None, ALU.mult)
            ocols[l] = pc
    gfin = load_cols(rows(final_norm_g, 8), 8)

    for l in range(nlayers):
        for P in PASSES:
            if l % 2 == 1:
                phase_odd(l, P, ocols[l])
            else:
                if not os.environ.get("DISABLE_EVEN"):
                    phase_even(l, P, ecols[l])
            phase_ffn(l, P, fcw[l], fcb[l])

    for P in PASSES:
        m0 = arena_mark()
        hf = alloc([128, 8, 1024], F32, "hf")
        phase_norm(0, gfin, 0, 0, P, None, final=True, hf=hf)
        ost = [alloc([128, D], F32, "ost") for _ in range(2)]
        for t in range(8):
            o = ost[t % 2]
            for half in range(2):
                pt = ps[2 + half]
                for c4 in range(4):
                    c = half * 4 + c4
                    k.tr(pt[:, c4 * 128:(c4 + 1) * 128], hf[:, c, t * 128:(t + 1) * 128], ident)
                k.copy(o[:, half * 512:(half + 1) * 512], pt.v(), eng=("act" if half else "dve"))
            k.dma(y_out[P["t0"] + t * 128:P["t0"] + (t + 1) * 128, :], o.v(), chan="st")
        arena_reset(m0)

    k.emit()
    return nc, es


def host_consts():
    c = np.zeros((10, 128, 128), np.float32)
    c[0] = np.eye(128)
    R = np.zeros((128, 128), np.float32)
    for m in range(128):
        d = m % 32
        if d < 16:
            R[m + 16, m] = -1.0
        else:
            R[m - 16, m] = 1.0
    c[1] = R
    jj, qq = np.meshgrid(np.arange(128), np.arange(128), indexing="ij")
    c[2] = (jj <= qq)
    c[3] = (jj >= qq)
    c[4] = np.where(qq >= jj, 0.0, -30000.0)
    c[5] = np.where(qq <= jj, 0.0, -30000.0)
    c[6] = 1.0
    t = np.arange(LS)
    row = (t // 64).astype(np.float32)
    col = (t % 64).astype(np.float32)
    freqs = (10000.0 ** (-np.arange(0, 32, 2, dtype=np.float32) / 32.0)).astype(np.float32)
    tab = np.zeros((2, 128, LS), np.float32)
    for p in range(128):
        d = p % 64
        pos = row if d < 32 else col
        f = freqs[(d % 32) % 16]
        ang = (pos * f).astype(np.float32)
        tab[0, p] = np.cos(ang)
        tab[1, p] = np.sin(ang)
    return c, tab


_CACHE = {}


def make_in_maps(inputs):
    consts, tab = host_consts()
    maps = []
    for i in range(8):
        m = {}
        m["xin"] = np.ascontiguousarray(np.concatenate(
            [inputs["x_prompt"][4 * i:4 * i + 4].reshape(4 * LP, D), inputs["x_sample"][i]], axis=0))
        m["cvec"] = np.ascontiguousarray(np.stack([inputs["c_ctx"], inputs["c"][i]], axis=0))
        m["cache_k"] = np.ascontiguousarray(inputs["cache_attn_k"][i].reshape(2, PAST, D))
        m["cache_v"] = np.ascontiguousarray(inputs["cache_attn_v"][i].reshape(2, PAST, D))
        m["ssd_f0"] = np.ascontiguousarray(inputs["state_ssd_fwd"][i])
        m["ssd_b0"] = np.ascontiguousarray(inputs["state_ssd_bwd"][i])
        m["lru_f0"] = np.ascontiguousarray(inputs["state_lru_fwd"][i])
        m["lru_b0"] = np.ascontiguousarray(inputs["state_lru_bwd"][i])
        for nm in ("w_mod", "b_mod", "norm_mix_g", "norm_ffn_g", "ssd_attn_w_in", "ssd_conv_w", "ssd_conv_b",
                   "ssd_norm_g", "diff_norm_g", "ssd_attn_w_out", "lru_w_in", "lru_conv_w", "lru_conv_b", "lru_wa",
                   "lru_ba", "lru_wx", "lru_bx", "lru_lambda", "lru_w_out", "ffn_w_up", "ffn_conv_w", "ffn_conv_b",
                   "ffn_w_down", "final_norm_g"):
            m[nm] = np.ascontiguousarray(inputs[nm])
        m["ssd_a_log"] = np.ascontiguousarray(inputs["ssd_a_log"].reshape(2, 32))
        m["ssd_dt_bias"] = np.ascontiguousarray(inputs["ssd_dt_bias"].reshape(2, 32))
        m["ssd_d"] = np.ascontiguousarray(inputs["ssd_d"])
        m["diff_lambda"] = np.ascontiguousarray(inputs["diff_lambda"].reshape(2, 256))
        m["consts"] = consts
        m["ropetab"] = tab
        maps.append(m)
    return maps


def kernel(**inputs):
    inputs = {k_: np.asarray(v, dtype=np.float32) for k_, v in inputs.items()}
    if "nc" not in _CACHE:
        _CACHE["nc"] = build_program()
    nc, _es = _CACHE["nc"]
    maps = make_in_maps(inputs)
    res = run_bass_kernel_spmd(nc, maps, core_ids=list(range(8)))
    R = res.results
    y = np.stack([r["y_out"] for r in R])
    y_prompt = y[:, :1024].reshape(32, LP, D)
    y_sample = y[:, 1024:].reshape(8, LS, D)
    nk = np.concatenate([r["nk_out"] for r in R], axis=0).reshape(32, 2, LP, 8, 2, 64)
    nv = np.concatenate([r["nv_out"] for r in R], axis=0).reshape(32, 2, LP, 8, 128)
    sf = np.concatenate([r["sf_out"] for r in R], axis=0)
    sb = np.concatenate([r["sb_out"] for r in R], axis=0)
    lf = np.concatenate([r["lf_out"] for r in R], axis=0)
    lb = np.concatenate([r["lb_out"] for r in R], axis=0)
    return (y_prompt, y_sample, nk, nv, sf, sb, lf, lb)
```

```python
import math
import os
from contextlib import ExitStack
import numpy as np
import concourse.bass as bass
import concourse.mybir as mybir
from concourse.bass_utils import run_bass_kernel_spmd

F32 = mybir.dt.float32
BF16 = mybir.dt.bfloat16
AF = mybir.ActivationFunctionType
ALU = mybir.AluOpType

D = 1024
DEPTH = 4
NP_SEQ = 4
LP = 256
LS = 1024
NTOK = 2048
PAST = 512
EPS = 1e-6
D_FF = 2816
NJ = 22
C_XBC0, C_DT0, C_Q0, C_K0, C_V0 = 1024, 2304, 2336, 3360, 4384
ATT_SCALE = 64 ** -0.5


class V:
    __slots__ = ("ap", "keys")

    def __init__(self, ap, keys):
        self.ap = ap
        self.keys = tuple(keys)


class Tile:
    def __init__(self, ap, key):
        self.ap = ap
        self.key = key

    def __getitem__(self, idx):
        return V(self.ap[idx], (self.key,))

    def v(self):
        return V(self.ap, (self.key,))


def _ap(x):
    return x.ap if isinstance(x, V) else x


class KB:
    ENGS = ("pe", "act", "dve", "pool", "sp")

    def __init__(self, nc, es):
        self.nc = nc
        self.es = es
        self.eng = {"pe": nc.tensor, "act": nc.scalar, "dve": nc.vector, "pool": nc.gpsimd, "sp": nc.sync}
        self.ops = []
        self.count = {e: 0 for e in self.ENGS}
        self.writers = {}
        self.readers = {}
        self.seen = {e: {} for e in self.ENGS}
        self.chan_n = {}
        self.milestones = {e: set() for e in self.ENGS}
        self.pending_bar = {e: {} for e in self.ENGS}
        self.osize = {e: [] for e in self.ENGS}

    def op(self, eng, fn, reads=(), writes=(), chan=None, osize=1 << 20):
        idx = self.count[eng]
        self.count[eng] += 1
        self.osize[eng].append(osize)
        need = {}

        def add(src, val):
            if src == ("e", eng) and chan is None:
                if not (val >= idx - 4 and self.osize[eng][val] < 512):
                    return
            if src[0] == "c":
                val = self.chan_n[src[1]]
            if need.get(src, -1) < val:
                need[src] = val

        for k in reads:
            for s, v in self.writers.get(k, {}).items():
                add(s, v)
        for k in writes:
            for s, v in self.writers.get(k, {}).items():
                add(s, v)
            for s, v in self.readers.get(k, {}).items():
                add(s, v)
        for s, v in self.pending_bar[eng].items():
            add(s, v)
        self.pending_bar[eng] = {}
        if chan is not None and self.chan_n.get(chan, 0) > 0:
            add(("c", chan), self.chan_n[chan])
        deps = []
        for s, v in need.items():
            if self.seen[eng].get(s, -1) >= v:
                continue
            self.seen[eng][s] = v
            deps.append((s, v))
            if s[0] == "e":
                self.milestones[s[1]].add(v)
        if chan is not None:
            self.chan_n[chan] = self.chan_n.get(chan, 0) + 1
            me, myv = ("c", chan), self.chan_n[chan]
        else:
            me, myv = ("e", eng), idx
        for k in writes:
            self.writers[k] = {me: myv}
            self.readers[k] = {}
        for k in reads:
            if k not in writes:
                self.readers.setdefault(k, {})[me] = myv
        self.ops.append((eng, idx, fn, deps, chan))

    def barrier(self):
        comp = ("pe", "act", "dve")
        for e in comp + ("sp",):
            for o in comp:
                if o != e and self.count[o] > 0:
                    self.pending_bar[e][("e", o)] = self.count[o] - 1
            for c, n in self.chan_n.items():
                if not str(c).startswith("w") and n > 0:
                    self.pending_bar[e][("c", c)] = n

    def emit(self):
        nc = self.nc
        sems = {e: self.es.enter_context(nc.semaphore("s_" + e)) for e in self.ENGS}
        csems = {c: self.es.enter_context(nc.semaphore("c_%s" % str(c))) for c in self.chan_n}
        ranks = {}
        for e in self.ENGS:
            for r, i in enumerate(sorted(self.milestones[e])):
                ranks[(e, i)] = r + 1
        for eng, idx, fn, deps, chan in self.ops:
            E = self.eng[eng]
            for s, v in deps:
                if s[0] == "e":
                    E.wait_ge(sems[s[1]], ranks[(s[1], v)])
                else:
                    E.wait_ge(csems[s[1]], 16 * v)
            ins = fn()
            if chan is not None:
                ins.then_inc(csems[chan], 16)
            elif idx in self.milestones[eng]:
                ins.then_inc(sems[eng], 1)
        for c, n in self.chan_n.items():
            nc.sync.wait_ge(csems[c], 16 * n)

    @staticmethod
    def _fs(x):
        ap = _ap(x)
        n = 1
        for d in list(ap.shape)[1:]:
            n *= int(d)
        return n

    @staticmethod
    def _keys(*xs):
        ks = []
        for x in xs:
            if isinstance(x, V):
                ks.extend(x.keys)
        return ks

    def mm(self, out, lhsT, rhs, start=True, stop=True):
        self.op("pe", lambda: self.nc.tensor.matmul(_ap(out), lhsT=_ap(lhsT), rhs=_ap(rhs), start=start, stop=stop),
                reads=self._keys(lhsT, rhs), writes=self._keys(out))

    def tr(self, out, in_, ident):
        self.op("pe", lambda: self.nc.tensor.transpose(_ap(out), _ap(in_), _ap(ident)),
                reads=self._keys(in_, ident), writes=self._keys(out))

    def act(self, out, in_, func, bias=0.0, scale=1.0, accum=None):
        def f():
            kw = {}
            if accum is not None:
                kw["accum_out"] = _ap(accum)
            return self.nc.scalar.activation(out=_ap(out), in_=_ap(in_), func=func, bias=_ap(bias), scale=_ap(scale), **kw)
        self.op("act", f, reads=self._keys(in_, bias, scale), writes=self._keys(out, accum),
                osize=(1 if accum is not None else self._fs(out)))

    def tt(self, out, in0, in1, op, eng="dve"):
        E = self.eng[eng]
        self.op(eng, lambda: E.tensor_tensor(out=_ap(out), in0=_ap(in0), in1=_ap(in1), op=op),
                reads=self._keys(in0, in1), writes=self._keys(out), osize=self._fs(out))

    def ts(self, out, in0, s1, s2, op0, op1=None, eng="dve"):
        E = self.eng[eng]
        if op1 is None:
            f = lambda: E.tensor_scalar(out=_ap(out), in0=_ap(in0), scalar1=_ap(s1), scalar2=None, op0=op0)
        else:
            f = lambda: E.tensor_scalar(out=_ap(out), in0=_ap(in0), scalar1=_ap(s1), scalar2=_ap(s2), op0=op0, op1=op1)
        self.op(eng, f, reads=self._keys(in0, s1, s2), writes=self._keys(out), osize=self._fs(out))

    def stt(self, out, in0, scalar, in1, op0, op1, eng="dve"):
        E = self.eng[eng]
        self.op(eng, lambda: E.scalar_tensor_tensor(out=_ap(out), in0=_ap(in0), scalar=_ap(scalar), in1=_ap(in1), op0=op0, op1=op1),
                reads=self._keys(in0, scalar, in1), writes=self._keys(out), osize=self._fs(out))

    def copy(self, out, in_, eng="dve"):
        if eng == "act":
            return self.act(out, in_, AF.Copy)
        E = self.eng[eng]
        self.op(eng, lambda: E.tensor_copy(out=_ap(out), in_=_ap(in_)), reads=self._keys(in_), writes=self._keys(out), osize=self._fs(out))

    def memset(self, out, val, eng="dve"):
        E = self.eng[eng]
        self.op(eng, lambda: E.memset(_ap(out), val), writes=self._keys(out), osize=self._fs(out))

    def recip(self, out, in_):
        self.op("dve", lambda: self.nc.vector.reciprocal(out=_ap(out), in_=_ap(in_)), reads=self._keys(in_), writes=self._keys(out), osize=self._fs(out))

    def scan(self, out, d0, d1, init):
        self.op("dve", lambda: self.nc.vector.tensor_tensor_scan(out=_ap(out), data0=_ap(d0), data1=_ap(d1), initial=_ap(init),
                                                                 op0=ALU.mult, op1=ALU.add),
                reads=self._keys(d0, d1, init), writes=self._keys(out))

    def dma(self, out, in_, chan, q="sp"):
        E = self.eng[q]
        self.op(q, lambda: E.dma_start(out=_ap(out), in_=_ap(in_)), reads=self._keys(in_), writes=self._keys(out), chan=chan)


def build_program(nlayers=DEPTH, debug=False):
    nc = bass.Bass("TRN2", target_bir_lowering=False)
    es = ExitStack()
    k = KB(nc, es)

    def din(name, shape):
        return nc.dram_tensor(name, list(shape), F32, kind="ExternalInput").ap()

    def dout(name, shape):
        return nc.dram_tensor(name, list(shape), F32, kind="ExternalOutput").ap()

    xin = din("xin", [NTOK, D])
    cvec = din("cvec", [2, D])
    cache_k = din("cache_k", [2, PAST, D])
    cache_v = din("cache_v", [2, PAST, D])
    ssd_f0 = din("ssd_f0", [2, 16, 64, 64])
    ssd_b0 = din("ssd_b0", [2, 16, 64, 64])
    lru_f0 = din("lru_f0", [2, D])
    lru_b0 = din("lru_b0", [2, D])
    w_mod = din("w_mod", [4, D, 6 * D]); b_mod = din("b_mod", [4, 6 * D])
    norm_mix_g = din("norm_mix_g", [4, D]); norm_ffn_g = din("norm_ffn_g", [4, D])
    w_in_e = din("ssd_attn_w_in", [2, D, 5408]); conv_w_e = din("ssd_conv_w", [2, 4, 1280]); conv_b_e = din("ssd_conv_b", [2, 1280])
    a_log = din("ssd_a_log", [2, 32]); dt_bias = din("ssd_dt_bias", [2, 32]); ssd_d = din("ssd_d", [2, 16])
    ssd_norm_g = din("ssd_norm_g", [2, D]); diff_lambda = din("diff_lambda", [2, 256]); diff_norm_g = din("diff_norm_g", [2, 128])
    w_out_e = din("ssd_attn_w_out", [2, 2048, D])
    lru_w_in = din("lru_w_in", [2, D, 2048]); lru_conv_w = din("lru_conv_w", [2, 4, D]); lru_conv_b = din("lru_conv_b", [2, D])
    lru_wa = din("lru_wa", [2, 2, 16, 64, 64]); lru_ba = din("lru_ba", [2, 2, D])
    lru_wx = din("lru_wx", [2, 2, 16, 64, 64]); lru_bx = din("lru_bx", [2, 2, D])
    lru_lambda = din("lru_lambda", [2, 2, D]); lru_w_out = din("lru_w_out", [2, D, D])
    ffn_w_up = din("ffn_w_up", [4, D, 2 * D_FF]); ffn_conv_w = din("ffn_conv_w", [4, 3, 2 * D_FF]); ffn_conv_b = din("ffn_conv_b", [4, 2 * D_FF])
    ffn_w_down = din("ffn_w_down", [4, D_FF, D]); final_norm_g = din("final_norm_g", [D])
    consts = din("consts", [10, 128, 128])
    ropetab = din("ropetab", [2, 128, LS])

    y_out = dout("y_out", [NTOK, D])
    nk_out = dout("nk_out", [NP_SEQ, 2, LP, D])
    nv_out = dout("nv_out", [NP_SEQ, 2, LP, D])
    sf_out = dout("sf_out", [NP_SEQ, 2, 16, 64, 64])
    sb_out = dout("sb_out", [NP_SEQ, 2, 16, 64, 64])
    lf_out = dout("lf_out", [NP_SEQ, 2, D])
    lb_out = dout("lb_out", [NP_SEQ, 2, D])

    dbgst = {}

    def dbg(name, v, shape):
        if not debug:
            return
        o = nc.dram_tensor("dbg_" + name, list(shape), F32, kind="ExternalOutput").ap()
        if "t" not in dbgst:
            dbgst["t"] = sbt("dbgst", [128, 1024], F32)
        st_ = dbgst["t"]
        n_ = shape[1]
        k.copy(st_[:, 0:n_], v)
        k.dma(o, st_[:, 0:n_], chan="dbg")

    def sbt(name, shape, dt):
        return Tile(es.enter_context(nc.sbuf_tensor(name, list(shape), dt))[:], name)

    x = sbt("x", [128, 8, NTOK], F32)
    cst = sbt("cst", [128, 10, 128], F32)
    cstb = sbt("cstb", [128, 10, 128], BF16)
    NCOL = 1150 if debug else 2300
    cols = sbt("cols", [128, NCOL], F32)
    mod = sbt("mod", [128, 4, 48, 2], F32)
    wslots = [sbt("wslot%d" % i, [128, 4096], BF16) for i in range(3)]
    ARENA_W = 25400
    arena = es.enter_context(nc.sbuf_tensor("arena", [128, ARENA_W], F32))[:]
    big = [Tile(es.enter_context(nc.psum_tensor("big%d" % i, [128, 1024], F32))[:], "big%d" % i) for i in range(4)]
    ps = [Tile(big[i // 2].ap[:, (i % 2) * 512:(i % 2 + 1) * 512], "ps%d" % i) for i in range(8)]
    stage = sbt("stage", [128, 128], F32)

    astate = {"off": 0, "n": 0}

    def alloc(shape, dt, name=None):
        n = 1
        for s in shape[1:]:
            n *= s
        words = n if dt == F32 else (n + 1) // 2
        o = astate["off"]
        assert o + words <= ARENA_W, ("arena overflow", o, words)
        astate["off"] = o + words
        astate["n"] += 1
        ap = arena[:, o:o + words]
        if dt != F32:
            ap = ap.bitcast(dt)[:, 0:n]
        if len(shape) == 3:
            ap = ap.rearrange("p (a b) -> p a b", a=shape[1])
        elif len(shape) == 4:
            ap = ap.rearrange("p (a b c) -> p a b c", a=shape[1], b=shape[2])
        if shape[0] != 128:
            ap = ap[0:shape[0]]
        return Tile(ap, "%s_%d" % (name or "a", astate["n"]))

    def arena_mark():
        return astate["off"]

    def arena_reset(m):
        astate["off"] = m
        k.barrier()

    wstate = {"i": 0}

    def wload(pieces, kc, cols_total):
        i = wstate["i"] % 3
        wstate["i"] += 1
        slot = wslots[i]
        view = slot.ap[:, 0:kc * cols_total].rearrange("p (a b) -> p a b", a=kc)
        for (src, off, c) in pieces:
            k.dma(V(view[:, :, off:off + c], (slot.key,)), src.rearrange("(a p) n -> p a n", p=128), chan="w%d" % i, q="pool")
        return Tile(view, slot.key)

    class WSeq:
        def __init__(self, specs, ahead=2):
            self.specs = specs
            self.tiles = {}
            self.nxt = 0
            self.ahead = ahead

        def get(self, i):
            while self.nxt < len(self.specs) and self.nxt <= i + self.ahead:
                self.tiles[self.nxt] = wload(*self.specs[self.nxt])
                self.nxt += 1
            return self.tiles[i]

    k.dma(cst.v(), consts.rearrange("a p n -> p a n"), chan="ld")
    k.copy(cstb.v(), cst.v())
    ident, identb = cst[:, 0, :], cstb[:, 0, :]
    Rb = cstb[:, 1, :]
    Tdir = [cst[:, 2, :], cst[:, 3, :]]
    maskb = [cstb[:, 4, :], cstb[:, 5, :]]
    onesb = cstb[:, 6, :]
    ones = cst[:, 6, :]

    colstate = {"n": 0}

    def load_cols(src_rows, nrows):
        off = colstate["n"]
        done = 0
        while done < nrows:
            r = min(128, nrows - done)
            k.dma(stage[0:r, :], src_rows[done:done + r, :], chan="ld")
            k.tr(ps[7][:, 0:r], stage[0:r, :], V(cst.ap[0:r, 0, 0:r], (cst.key,)))
            k.copy(cols[:, off + done:off + done + r], ps[7][:, 0:r])
            done += r
        colstate["n"] += nrows
        assert colstate["n"] <= NCOL
        return off

    def rows(ap1d_or_2d, n):
        return ap1d_or_2d.rearrange("(r c) -> r c", c=128)

    xm = arena_mark()
    xst = [alloc([128, D], F32, "xst") for _ in range(2)]
    for t in range(NTOK // 128):
        st = xst[t % 2]
        k.dma(st.v(), xin[t * 128:(t + 1) * 128, :], chan="xl%d" % (t % 2))
        for half in range(2):
            pt = ps[half]
            for c4 in range(4):
                c = half * 4 + c4
                k.tr(pt[:, c4 * 128:(c4 + 1) * 128], st[:, c * 128:(c + 1) * 128], ident)
            k.copy(V(x.ap[:, half * 4:half * 4 + 4, t * 128:(t + 1) * 128], (x.key,)),
                   V(pt.ap.rearrange("p (a b) -> p a b", a=4), (pt.key,)), eng=("act" if half else "dve"))
    arena_reset(xm)

    cm = arena_mark()
    cT = alloc([128, 16], F32, "cT")
    cTb = alloc([128, 16], BF16, "cTb")
    k.dma(stage[0:16, :], cvec.rearrange("v (c q) -> (v c) q", q=128), chan="ld")
    k.tr(ps[7][:, 0:16], stage[0:16, :], V(cst.ap[0:16, 0, 0:16], (cst.key,)))
    k.act(cT.v(), ps[7][:, 0:16], AF.Silu)
    k.copy(cTb.v(), cT.v())
    cTb3 = cTb.ap.rearrange("p (v c) -> p c v", v=2)
    for l in range(nlayers):
        bcol = load_cols(rows(b_mod[l], 48), 48)
        specs = [([(w_mod[l][:, g * 512:(g + 1) * 512], 0, 512)], 8, 512) for g in range(12)]
        wq = WSeq(specs)
        for g in range(12):
            w = wq.get(g)
            for s4 in range(4):
                ch = g * 4 + s4
                for kc in range(8):
                    k.mm(ps[6][:, ch * 2:ch * 2 + 2], w[:, kc, s4 * 128:(s4 + 1) * 128], V(cTb3[:, kc, :], (cTb.key,)),
                         start=(kc == 0), stop=(kc == 7))
        k.tt(V(mod.ap[:, l, :, :], (mod.key,)), V(ps[6].ap[:, 0:96].rearrange("p (a b) -> p a b", b=2), (ps[6].key,)),
             V(cols.ap[:, bcol:bcol + 48].unsqueeze(2).broadcast_to([128, 48, 2]), (cols.key,)), ALU.add)
    arena_reset(cm)

    PASSES = [dict(t0=0, nseq=NP_SEQ, L=LP, v=0), dict(t0=1024, nseq=1, L=LS, v=1)]

    def modcol(l, which, c, v):
        return V(mod.ap[:, l, which * 8 + c, v:v + 1], (mod.key,))

    def phase_norm(l, gcol, which_shift, which_scale, P, h, final=False, hf=None):
        m = arena_mark()
        AB = alloc([128, 8, 2], F32, "AB")
        sq = [alloc([128, 512], BF16, "sq") for _ in range(2)]
        lnv = alloc([128, 512], F32, "lnv")
        rstd = alloc([128, 512], F32, "rstd")
        tmp = [alloc([128, 512], F32, "tmp") for _ in range(2)]
        for c in range(8):
            if final:
                k.copy(AB[:, c, 0:1], cols[:, gcol + c:gcol + c + 1])
                k.memset(AB[:, c, 1:2], 0.0)
            else:
                k.ts(AB[:, c, 0:1], modcol(l, which_scale, c, P["v"]), 1.0, cols[:, gcol + c:gcol + c + 1], ALU.add, ALU.mult)
                k.copy(AB[:, c, 1:2], modcol(l, which_shift, c, P["v"]))
        for blk in range(2):
            tok = slice(P["t0"] + blk * 512, P["t0"] + blk * 512 + 512)
            for c in range(8):
                s = sq[c % 2]
                k.act(s.v(), x[:, c, tok], AF.Square)
                k.mm(ps[0].v(), onesb, s.v(), start=(c == 0), stop=(c == 7))
            k.act(lnv.v(), ps[0].v(), AF.Ln, bias=EPS, scale=1.0 / D)
            k.act(rstd.v(), lnv.v(), AF.Exp, scale=-0.5)
            for c in range(8):
                t = tmp[c % 2]
                k.tt(t.v(), x[:, c, tok], rstd.v(), ALU.mult)
                dst = (hf if final else h)[:, c, blk * 512:(blk + 1) * 512]
                k.ts(dst, t.v(), AB[:, c, 0:1], AB[:, c, 1:2], ALU.mult, ALU.add)
        astate["off"] = m

    def conv_fm(src_ps_list, P, stg, width, left, wcol0, wstride, bcol, acc):
        nseq, L = P["nseq"], P["L"]
        for blk in range(2):
            if nseq == 1:
                dst = V(stg.ap[:, 0, left + blk * 512:left + blk * 512 + 512], (stg.key,))
                src = src_ps_list[blk].v()
            else:
                dst = V(stg.ap[:, 2 * blk:2 * blk + 2, left:left + L], (stg.key,))
                src = V(src_ps_list[blk].ap.rearrange("p (a b) -> p a b", a=2), (src_ps_list[blk].key,))
            k.act(dst, src, AF.Copy)
        k.act(acc.v(), V(stg.ap[:, :, 0:L], (stg.key,)), AF.Identity, bias=cols[:, bcol:bcol + 1], scale=cols[:, wcol0:wcol0 + 1])
        for j in range(1, width):
            wc = wcol0 + j * wstride
            k.stt(acc.v(), V(stg.ap[:, :, j:j + L], (stg.key,)), cols[:, wc:wc + 1], acc.v(), ALU.mult, ALU.add)

    def resid_add(l, which_gate, P, d, blk, pst):
        tok = slice(P["t0"] + blk * 512, P["t0"] + blk * 512 + 512)
        k.stt(x[:, d, tok], pst.v(), modcol(l, which_gate, d, P["v"]), x[:, d, tok], ALU.mult, ALU.add)

    def phase_ffn(l, P, cw, cb):
        m0 = arena_mark()
        h = alloc([128, 8, 1024], BF16, "h")
        phase_norm(l, gcols_ffn[l], 3, 4, P, h)
        nseq, L = P["nseq"], P["L"]
        actT = [alloc([128, 1024], BF16, "act") for _ in range(NJ)]
        stg = [[alloc([128, nseq, L + 2], F32, "stg") for _ in range(2)] for _ in range(2)]
        accv = [alloc([128, nseq, L], F32, "accv") for _ in range(2)]
        accg = [alloc([128, nseq, L], F32, "accg") for _ in range(2)]
        for sp_ in stg:
            for s in sp_:
                k.memset(s.v(), 0.0)
        specs = []
        for jj in range(11):
            j0 = jj * 2
            specs.append(([(ffn_w_up[l][:, j0 * 128:(j0 + 2) * 128], 0, 256),
                           (ffn_w_up[l][:, D_FF + j0 * 128:D_FF + (j0 + 2) * 128], 256, 256)], 8, 512))
        for d in range(8):
            specs.append(([(ffn_w_down[l][:, d * 128:(d + 1) * 128], 0, 128)], NJ, 128))
        wq = WSeq(specs)
        for j in range(NJ):
            w = wq.get(j // 2)
            jo = (j % 2) * 128
            par = j % 2
            for half, (coff, stgt, acc) in enumerate(((jo, stg[par][0], accv[par]), (256 + jo, stg[par][1], accg[par]))):
                pl = [ps[par * 4 + half * 2], ps[par * 4 + half * 2 + 1]]
                for blk in range(2):
                    for kc in range(8):
                        k.mm(pl[blk].v(), w[:, kc, coff:coff + 128], h[:, kc, blk * 512:(blk + 1) * 512], start=(kc == 0), stop=(kc == 7))
                fcol = j + (NJ if half else 0)
                conv_fm(pl, P, stgt, 3, 1, cw + fcol, 2 * NJ, cb + fcol, acc)
            k.act(accg[par].v(), accg[par].v(), AF.Silu)
            k.tt(V(actT[j].ap.rearrange("p (a b) -> p a b", a=nseq), (actT[j].key,)), accg[par].v(), accv[par].v(), ALU.mult)
        for d in range(8):
            w = wq.get(11 + d)
            for blk in range(2):
                pt = ps[4 + (d * 2 + blk) % 2]
                for j in range(NJ):
                    k.mm(pt.v(), w[:, j, :], actT[j][:, blk * 512:(blk + 1) * 512], start=(j == 0), stop=(j == NJ - 1))
                resid_add(l, 5, P, d, blk, pt)
        arena_reset(m0)

    def phase_even(l, P, ec):
        j = l // 2
        lam_init = 0.8 - 0.6 * math.exp(-0.3 * l)
        m0 = arena_mark()
        h = alloc([128, 8, 1024], BF16, "h")
        phase_norm(l, gcols_mix[l], 0, 1, P, h)
        yo = alloc([128, 8, 1024], BF16, "yo")
        nseq, L, isB = P["nseq"], P["L"], (P["v"] == 1)
        B0, B1, B2, B3 = big
        P3bf = V(B3.ap.bitcast(BF16), (B3.key,))

        def ccol(name, i=0):
            return cols[:, ec[name] + i:ec[name] + i + 1]

        def out_proj(kbase, pts=None):
            specs = [([(w_out_e[j][kbase * 128:kbase * 128 + 1024, dg * 256:(dg + 1) * 256], 0, 256)], 8, 256) for dg in range(4)]
            wq = WSeq(specs)
            for d in range(8):
                w = wq.get(d // 2)
                for blk in range(2):
                    pt = pts[blk] if pts is not None else V(B0.ap[:, blk * 512:(blk + 1) * 512], (B0.key,))
                    for kc in range(8):
                        k.mm(pt, w[:, kc, (d % 2) * 128:(d % 2 + 1) * 128], V(yo.ap[:, kc, blk * 512:(blk + 1) * 512], (yo.key,)),
                             start=(kc == 0), stop=(kc == 7))
                    tok = slice(P["t0"] + blk * 512, P["t0"] + blk * 512 + 512)
                    k.stt(x[:, d, tok], pt, modcol(l, 2, d, P["v"]), x[:, d, tok], ALU.mult, ALU.add)

        ms = arena_mark()
        xbc = [alloc([128, 256], BF16, "xbc") for _ in range(10)]
        stg = [alloc([128, 262], F32, "stg") for _ in range(2)]
        acc = [alloc([128, 256], F32, "acc") for _ in range(2)]
        xsT = [alloc([128, 1024], BF16, "xsT") for _ in range(2)]
        BT = [alloc([128, 128], BF16, "BT") for _ in range(2)]
        sz = [alloc([128, 1024], BF16, "sz") for _ in range(2)]
        dtr = alloc([128, 2, 32], F32, "dtr"); dt = alloc([128, 2, 32], F32, "dt"); dta = alloc([128, 2, 32], F32, "dta")
        Scol = alloc([128, 2, 32], F32, "Scol"); eU = alloc([128, 2, 32], F32, "eU"); dend = alloc([128, 2, 32], F32, "dend")
        dec = alloc([128, 2, 32], F32, "dec"); wdd = alloc([128, 2, 32], F32, "wdd"); Utot = alloc([128, 2, 32], F32, "Utot")
        xdt = [alloc([128, 1024], BF16, "xdt") for _ in range(2)]
        xdd = xdt
        xD = alloc([128, 1024], BF16, "xD")
        CBT = alloc([128, 2, 128], BF16, "CBT")
        segT = alloc([128, 8, 128], F32, "segT")
        LTs = [alloc([128, 8, 128], BF16, "LT") for _ in range(2)]
        MT = [[alloc([128, 8, 128], BF16, "MT") for _ in range(2)] for _ in range(2)]
        ysb = alloc([128, 1024], F32, "ysb")
        tmpy = Tile(segT.ap.rearrange("p a b -> p (a b)"), segT.key)
        gn = alloc([128, 1024], BF16, "gn")
        junk = gn
        ssq = alloc([128, 2], F32, "ssq")
        Hs = [alloc([128, 512], F32, "Hs") for _ in range(2)]
        Hb16 = [[alloc([128, 512], BF16, "Hb16") for _ in range(2)] for _ in range(2)]
        Hent_b = [alloc([128, 512], BF16, "Hentb") for _ in range(8)] if isB else None
        stF = alloc([128, 4, 2, 64], F32, "stF")
        Dbc = V(cols.ap[:, ec["d"]:ec["d"] + 16].unsqueeze(2).broadcast_to([128, 16, 64]), (cols.key,))

        def v3(t, a):
            return V(t.ap.rearrange("p (a b) -> p a b", a=a), (t.key,))

        def group_prep(gi, need_z):
            g0 = gi * 256
            seq0 = (g0 // L) * L
            lo, hi = max(seq0, g0 - 2), min(seq0 + L, g0 + 257)
            n_in = hi - lo
            so = lo - (g0 - 2)
            specs = []
            if need_z:
                specs += [([(w_in_e[j][:, 0:512], 0, 512)], 8, 512), ([(w_in_e[j][:, 512:1024], 0, 512)], 8, 512)]
            specs += [([(w_in_e[j][:, 1024:1536], 0, 512)], 8, 512), ([(w_in_e[j][:, 1536:2048], 0, 512)], 8, 512),
                      ([(w_in_e[j][:, 2048:2336], 0, 288)], 8, 288)]
            wq = WSeq(specs)
            wi = 0
            if need_z:
                for half in range(2):
                    w = wq.get(wi); wi += 1
                    for tt in range(2):
                        pt = V(B3.ap[:, (tt % 2) * 512:(tt % 2) * 512 + 512], (B3.key,))
                        for kc in range(8):
                            k.mm(pt, h[:, kc, g0 + tt * 128:g0 + (tt + 1) * 128], w[:, kc, :], start=(kc == 0), stop=(kc == 7))
                        k.act(sz[tt][:, half * 512:(half + 1) * 512], pt, AF.Silu)
            pend_silu = []
            for c in range(10):
                if c % 4 == 0:
                    w = wq.get(wi); wi += 1
                st_, ac_ = stg[c % 2], acc[c % 2]
                pt = V(B0.ap[:, (c % 2) * 512:(c % 2) * 512 + n_in], (B0.key,))
                for kc in range(8):
                    k.mm(pt, w[:, kc, (c % 4) * 128:(c % 4 + 1) * 128], h[:, kc, lo:hi], start=(kc == 0), stop=(kc == 7))
                k.memset(st_.v(), 0.0)
                k.act(st_[:, so:so + n_in], pt, AF.Copy)
                cw, cb = ec["cw"] + c, ec["cb"] + c
                k.act(ac_.v(), st_[:, 0:256], AF.Identity, bias=cols[:, cb:cb + 1], scale=cols[:, cw:cw + 1])
                if pend_silu:
                    pend_silu.pop()()
                for kk in range(1, 4):
                    k.stt(ac_.v(), st_[:, kk:kk + 256], cols[:, cw + 10 * kk:cw + 10 * kk + 1], ac_.v(), ALU.mult, ALU.add)
                pend_silu.append(lambda c=c, ac_=ac_: k.act(xbc[c].v(), ac_.v(), AF.Silu))
            pend_silu.pop()()
            if int(os.environ.get("PREP_LEVEL", "9")) < 3:
                return
            pdt = V(B1.ap[:, 0:64].rearrange("p (a b) -> p a b", a=2), (B1.key,))
            pS = V(B1.ap[:, 64:128].rearrange("p (a b) -> p a b", a=2), (B1.key,))
            pU = V(B1.ap[:, 128:192].rearrange("p (a b) -> p a b", a=2), (B1.key,))
            steps = []
            def s1():
                for tt in range(2):
                    for kc in range(8):
                        k.mm(V(B1.ap[:, tt * 32:(tt + 1) * 32], (B1.key,)), h[:, kc, g0 + tt * 128:g0 + (tt + 1) * 128], w[:, kc, 256:288],
                             start=(kc == 0), stop=(kc == 7))
            steps.append(s1)
            steps.append(lambda: k.tt(dtr.v(), pdt, V(cols.ap[:, ec["dtb"]:ec["dtb"] + 32].unsqueeze(1).broadcast_to([128, 2, 32]), (cols.key,)), ALU.add))
            steps.append(lambda: k.act(dtr.v(), dtr.v(), AF.Exp))
            steps.append(lambda: k.act(dt.v(), dtr.v(), AF.Ln, bias=1.0))
            steps.append(lambda: k.tt(dta.v(), dt.v(), V(cols.ap[:, ec["abc"]:ec["abc"] + 32].unsqueeze(1).broadcast_to([128, 2, 32]), (cols.key,)), ALU.mult))
            def s6():
                for tt in range(2):
                    for dr in range(2):
                        k.mm(V(B1.ap[:, 64 + tt * 32 + dr * 16:64 + tt * 32 + dr * 16 + 16], (B1.key,)), Tdir[dr], dta[:, tt, dr * 16:(dr + 1) * 16])
                    k.mm(V(B1.ap[:, 128 + tt * 32:128 + (tt + 1) * 32], (B1.key,)), ones, dta[:, tt, :])
            steps.append(s6)
            steps.append(lambda: k.copy(Scol.v(), pS))
            steps.append(lambda: k.copy(Utot.v(), pU))
            steps.append(lambda: k.act(eU.v(), Scol.v(), AF.Exp))
            steps.append(lambda: k.act(dec.v(), Utot.v(), AF.Exp))
            steps.append(lambda: k.tt(dend.v(), Utot.v(), Scol.v(), ALU.subtract))
            steps.append(lambda: k.act(dend.v(), dend.v(), AF.Exp))
            steps.append(lambda: k.tt(wdd.v(), dend.v(), dt.v(), ALU.mult))
            for st_i, st_f in enumerate(steps):
                if st_i < int(os.environ.get("PREP_STEPS", "99")):
                    st_f()
            if int(os.environ.get("PREP_LEVEL", "9")) < 4:
                return
            for tt in range(2):
                for c in range(8):
                    k.tr(V(P3bf.ap[:, c * 128:(c + 1) * 128], (B3.key,)), xbc[c][:, tt * 128:(tt + 1) * 128], identb)
                k.tr(V(P3bf.ap[:, 1024:1152], (B3.key,)), xbc[8][:, tt * 128:(tt + 1) * 128], identb)
                k.act(xsT[tt].v(), V(P3bf.ap[:, 0:1024], (B3.key,)), AF.Copy)
                k.act(BT[tt].v(), V(P3bf.ap[:, 1024:1152], (B3.key,)), AF.Copy)

        def bc16(t, tt, dr):
            return V(t.ap[:, tt, dr * 16:(dr + 1) * 16].unsqueeze(2).broadcast_to([128, 16, 64]), (t.key,))

        SSD_LEVEL = int(os.environ.get("SSD_LEVEL", "3"))

        def chunk_states(tt, dr, pst):
            if SSD_LEVEL < 2:
                return
            k.tt(v3(xdd[dr], 16), v3(xsT[tt], 16), bc16(wdd, tt, dr), ALU.mult, eng="dve")
            for g in range(2):
                k.mm(V(pst.ap[g * 64:(g + 1) * 64, :], pst.keys), BT[tt][:, g * 64:(g + 1) * 64], xdd[dr][:, g * 512:(g + 1) * 512])

        def state_step(Ht, tt, dr, pst, have):
            if SSD_LEVEL < 2:
                return
            if not have:
                k.copy(Ht.v(), pst)
                return
            for g in range(2):
                hv = V(Ht.ap[g * 64:(g + 1) * 64, :].rearrange("p (a b) -> p a b", a=8), (Ht.key,))
                dv = V(dec.ap[g * 64:(g + 1) * 64, tt, dr * 16 + g * 8:dr * 16 + g * 8 + 8].unsqueeze(2).broadcast_to([64, 8, 64]), (dec.key,))
                k.tt(hv, hv, dv, ALU.mult)
            k.tt(Ht.v(), Ht.v(), pst, ALU.add)

        def chunk_y(gi, tt, ent):
            tok = slice(gi * 256 + tt * 128, gi * 256 + (tt + 1) * 128)
            tl = slice(tt * 128, (tt + 1) * 128)
            if SSD_LEVEL < 3:
                return
            YS = int(os.environ.get("Y_STEPS", "99"))
            for g in range(2):
                k.mm(V(B3.ap[:, g * 512:g * 512 + 128], (B3.key,)), xbc[8][g * 64:(g + 1) * 64, tl], xbc[9][g * 64:(g + 1) * 64, tl])
            for g in range(2):
                k.copy(CBT[:, g, :], V(B3.ap[:, g * 512:g * 512 + 128], (B3.key,)))
            DB = debug and (not isB) and gi == 0 and tt == 0
            if DB:
                dbg("CBT", V(CBT.ap.rearrange("p a b -> p (a b)"), (CBT.key,)), [128, 256])
                dbg("xsT", xsT[tt].v(), [128, 1024])
                dbg("dt", V(dt.ap.rearrange("p a b -> p (a b)"), (dt.key,)), [128, 64])
                dbg("Scol", V(Scol.ap.rearrange("p a b -> p (a b)"), (Scol.key,)), [128, 64])
            if YS < 2:
                return
            for dr in range(2):
                k.tt(v3(xdt[dr], 16), v3(xsT[tt], 16), bc16(dt, tt, dr), ALU.mult, eng="dve")
            k.tt(v3(xD, 16), v3(xsT[tt], 16), Dbc, ALU.mult, eng="dve")
            if YS < 3:
                return
            its = [(0, 0), (0, 1), (1, 0), (1, 1)]

            def stage_a(i):
                g, dr = its[i]
                pb = big[dr]
                pbv = V(pb.ap.rearrange("p (a b) -> p a b", a=8), (pb.key,))
                for h8 in range(8):
                    hd = dr * 16 + g * 8 + h8
                    k.mm(V(pb.ap[:, h8 * 128:(h8 + 1) * 128], (pb.key,)), V(dta.ap[:, tt, hd:hd + 1].broadcast_to([128, 128]), (dta.key,)),
                         Tdir[dr], start=True, stop=False)
                    k.mm(V(pb.ap[:, h8 * 128:(h8 + 1) * 128], (pb.key,)), identb, maskb[dr], start=False, stop=True)
                hd0 = dr * 16 + g * 8
                for bk in range(2):
                    k.tt(segT[:, bk * 4:(bk + 1) * 4, :], V(pbv.ap[:, bk * 4:(bk + 1) * 4, :], pbv.keys),
                         V(Scol.ap[:, tt, hd0 + bk * 4:hd0 + bk * 4 + 4].unsqueeze(2).broadcast_to([128, 4, 128]), (Scol.key,)), ALU.subtract)
                k.act(LTs[i % 2].v(), segT.v(), AF.Exp)

            def stage_b(i):
                g, dr = its[i]
                k.tt(MT[g][dr].v(), LTs[i % 2].v(), V(CBT.ap[:, g, :].unsqueeze(1).broadcast_to([128, 8, 128]), (CBT.key,)), ALU.mult, eng="dve")
                if DB and g == 0:
                    dbg("MT%d" % dr, V(MT[g][dr].ap.rearrange("p a b -> p (a b)"), (MT[g][dr].key,)), [128, 1024])

            stage_a(0)
            for i in range(4):
                if i + 1 < 4:
                    stage_a(i + 1)
                stage_b(i)
                g, dr = its[i]
                if dr == 0 or YS < 4:
                    continue
                for h8 in range(8):
                    hsl = slice((g * 8 + h8) * 64, (g * 8 + h8 + 1) * 64)
                    yv = V(B2.ap[:, hsl], (B2.key,))
                    k.mm(yv, identb, xD[:, hsl], start=True, stop=False)
                    k.mm(yv, MT[g][0][:, h8, :], xdt[0][:, hsl], start=False, stop=False)
                    k.mm(yv, MT[g][1][:, h8, :], xdt[1][:, hsl], start=False, stop=True)
            if YS < 5:
                return
            for bk in range(2):
                k.act(ysb[:, bk * 512:(bk + 1) * 512], B2[:, bk * 512:(bk + 1) * 512], AF.Copy)
            if YS < 6:
                return
            if DB:
                dbg("ydiag", ysb.v(), [128, 1024])
            for dr in range(2):
                if ent[dr] is None:
                    continue
                for g in range(2):
                    k.mm(V(B3.ap[:, g * 512:(g + 1) * 512], (B3.key,)), xbc[9][g * 64:(g + 1) * 64, tl], ent[dr][g * 64:(g + 1) * 64, :])
                for bk in range(2):
                    k.tt(V(tmpy.ap[:, bk * 512:(bk + 1) * 512].rearrange("p (a b) -> p a b", a=8), (tmpy.key,)),
                         V(B3.ap[:, bk * 512:(bk + 1) * 512].rearrange("p (a b) -> p a b", a=8), (B3.key,)),
                         V(eU.ap[:, tt, dr * 16 + bk * 8:dr * 16 + bk * 8 + 8].unsqueeze(2).broadcast_to([128, 8, 64]), (eU.key,)), ALU.mult)
                k.tt(ysb.v(), ysb.v(), tmpy.v(), ALU.add, eng="dve")
            if DB:
                dbg("ysb", ysb.v(), [128, 1024])
            if YS < 7:
                return
            k.tt(ysb.v(), ysb.v(), sz[tt].v(), ALU.mult)
            k.memset(ssq[:, 0:1], 0.0)
            k.act(junk.v(), ysb.v(), AF.Square, accum=ssq[:, 0:1])
            k.act(ssq[:, 1:2], ssq[:, 0:1], AF.Ln, bias=EPS, scale=1.0 / 1024)
            k.act(ssq[:, 1:2], ssq[:, 1:2], AF.Exp, scale=-0.5)
            k.act(gn.v(), ysb.v(), AF.Copy, scale=ssq[:, 1:2])
            for c in range(8):
                k.tr(V(P3bf.ap[:, c * 128:(c + 1) * 128], (B3.key,)), gn[:, c * 128:(c + 1) * 128], identb)
            k.tt(V(yo.ap[:, 0:8, tok], (yo.key,)), V(P3bf.ap[:, 0:1024].rearrange("p (a b) -> p a b", a=8), (B3.key,)),
                 V(cols.ap[:, ec["ng"]:ec["ng"] + 8].unsqueeze(2).broadcast_to([128, 8, 128]), (cols.key,)), ALU.mult)

        def write_state(Ht, dst):
            if SSD_LEVEL < 2:
                return
            for pr in range(4):
                k.tr(V(B3.ap[:, pr * 128:(pr + 1) * 128], (B3.key,)), Ht[:, pr * 128:(pr + 1) * 128], ident)
            k.copy(V(stF.ap.rearrange("p a b c -> p (a b c)"), (stF.key,)), V(B3.ap[:, 0:512], (B3.key,)))
            dv_ = dst.rearrange("(g pr h2) p n -> g (h2 p) pr n", g=2, pr=4)
            for g in range(2):
                k.dma(dv_[g], V(stF.ap[:, :, g, :], (stF.key,)), chan="st")

        pstates = [V(B3.ap[:, 0:512], (B3.key,)), V(B3.ap[:, 512:1024], (B3.key,))]
        SKIP_SSD = bool(os.environ.get("SKIP_SSD")); SKIP_ATT = bool(os.environ.get("SKIP_ATT"))
        if SKIP_SSD:
            pass
        elif not isB:
            for s in range(nseq):
                group_prep(s, True)
                chunk_states(0, 0, pstates[0]); state_step(Hs[0], 0, 0, pstates[0], False)
                k.copy(Hb16[0][1].v(), Hs[0].v(), eng="act")
                chunk_states(1, 1, pstates[1]); state_step(Hs[1], 1, 1, pstates[1], False)
                k.copy(Hb16[1][0].v(), Hs[1].v(), eng="act")
                chunk_states(1, 0, pstates[0]); state_step(Hs[0], 1, 0, pstates[0], True)
                write_state(Hs[0], sf_out[s, j])
                chunk_states(0, 1, pstates[1]); state_step(Hs[1], 0, 1, pstates[1], True)
                write_state(Hs[1], sb_out[s, j])
                chunk_y(s, 0, [None, Hb16[1][0]])
                chunk_y(s, 1, [Hb16[0][1], None])
        else:
            for dr, src in enumerate((ssd_f0, ssd_b0)):
                sv_ = src[j].rearrange("(g pr h2) p n -> g (h2 p) pr n", g=2, pr=4)
                for g in range(2):
                    k.dma(V(stF.ap[:, :, g, :], (stF.key,)), sv_[g], chan="ld")
                for pr in range(4):
                    k.tr(V(B3.ap[:, pr * 128:(pr + 1) * 128], (B3.key,)), V(stF.ap[:, pr, :, :].rearrange("p a b -> p (a b)"), (stF.key,)), ident)
                k.copy(Hs[dr].v(), V(B3.ap[:, 0:512], (B3.key,)))
            for gi in (3, 2, 1, 0):
                group_prep(gi, False)
                for tt in (1, 0):
                    k.copy(Hent_b[gi * 2 + tt].v(), Hs[1].v(), eng="act")
                    if gi * 2 + tt > 0:
                        chunk_states(tt, 1, pstates[tt]); state_step(Hs[1], tt, 1, pstates[tt], True)
            for gi in range(4):
                group_prep(gi, True)
                for tt in range(2):
                    k.copy(Hb16[0][tt].v(), Hs[0].v(), eng="act")
                    if gi * 2 + tt < 7:
                        chunk_states(tt, 0, pstates[tt]); state_step(Hs[0], tt, 0, pstates[tt], True)
                for tt in range(2):
                    chunk_y(gi, tt, [Hb16[0][tt], Hent_b[gi * 2 + tt]])
        if not SKIP_SSD:
            out_proj(0)
        arena_reset(ms)

        nkt = 12 if isB else 8
        nk = nkt * 128
        koff = 4 if isB else 0
        qT = [alloc([128, 1024], BF16, "qT") for _ in range(4)]
        kT = [alloc([128, nk], BF16, "kT") for _ in range(4)]
        vaug = alloc([128, nkt, 4, 130], BF16, "vaug")
        PT = [alloc([128, 512], BF16, "PT") for _ in range(3)]
        raw = [alloc([128, 512], BF16, "raw") for _ in range(2)]
        t1 = alloc([128, 512], F32, "t1"); t2 = alloc([128, 512], F32, "t2")
        ost = alloc([128, 512], F32, "ost")
        o_t = alloc([128, 4, 128], F32, "o_t"); o_n = alloc([128, 4, 128], BF16, "o_n")
        ojunk = Tile(t1.ap.rearrange("p (a b) -> p a b", a=4), t1.key)
        Osb = [alloc([128, 4, 130], F32, "Osb") for _ in range(2)]
        rs = alloc([128, 2, 4], F32, "rs"); rs2 = alloc([128, 2, 4], F32, "rs2"); sso = alloc([128, 2, 4], F32, "sso")
        if isB:
            rope = alloc([128, 2, 1024], F32, "rope")
            k.dma(rope.v(), ropetab.rearrange("a p n -> p a n"), chan="ld")
            ckst = alloc([128, 4, 512], F32, "ckst")
        Sbank = [Tile(B0.ap[:, 0:512], "B0_lo"), Tile(B0.ap[:, 512:1024], "B0_hi")]
        for hg in range(0 if SKIP_ATT else 2):
            specs = [([(w_in_e[j][:, c0 + hg * 512:c0 + (hg + 1) * 512], 0, 512)], 8, 512) for c0 in (C_Q0, C_K0, C_V0)]
            wq = WSeq(specs)
            wQ, wK, wV = wq.get(0), wq.get(1), wq.get(2)
            k.memset(V(vaug.ap[:, :, :, 128:130], (vaug.key,)), 1.0)
            if isB:
                k.dma(ckst.v(), cache_k[j][:, hg * 512:(hg + 1) * 512].rearrange("(a p) n -> p a n", p=128), chan="ld")
                for kt in range(4):
                    k.dma(V(vaug.ap[:, kt, :, 0:128], (vaug.key,)),
                          cache_v[j][kt * 128:(kt + 1) * 128, hg * 512:(hg + 1) * 512].rearrange("p (a b) -> p a b", a=4), chan="cv", q="pool")
                    for hh in range(4):
                        k.tr(V(B3.ap[:, hh * 128:(hh + 1) * 128], (B3.key,)), ckst[:, kt, hh * 128:(hh + 1) * 128], ident)
                    for hh in range(4):
                        k.copy(kT[hh][:, kt * 128:(kt + 1) * 128], V(B3.ap[:, hh * 128:(hh + 1) * 128], (B3.key,)), eng="act")
            for which, (wt, dstT, doff) in enumerate(((wQ, qT, 0), (wK, kT, koff * 128))):
                for hh in range(4):
                    for blk in range(2):
                        pt = Sbank[blk].v()
                        for kc in range(8):
                            k.mm(pt, wt[:, kc, hh * 128:(hh + 1) * 128], h[:, kc, blk * 512:(blk + 1) * 512], start=(kc == 0), stop=(kc == 7))
                        dst = dstT[hh][:, doff + blk * 512:doff + (blk + 1) * 512]
                        if not isB:
                            k.act(dst, pt, AF.Copy)
                        else:
                            rw = raw[blk]
                            k.act(rw.v(), pt, AF.Copy)
                            p2 = V(B1.ap[:, blk * 512:(blk + 1) * 512], (B1.key,))
                            k.mm(p2, Rb, rw.v())
                            k.tt(t1.v(), rw.v(), rope[:, 0, blk * 512:(blk + 1) * 512], ALU.mult, eng="dve")
                            k.tt(t2.v(), p2, rope[:, 1, blk * 512:(blk + 1) * 512], ALU.mult)
                            k.tt(dst, t1.v(), t2.v(), ALU.add)
            for t in range(8):
                pt = V(B2.ap[:, (t % 2) * 512:(t % 2) * 512 + 512], (B2.key,))
                for kc in range(8):
                    k.mm(pt, h[:, kc, t * 128:(t + 1) * 128], wV[:, kc, :], start=(kc == 0), stop=(kc == 7))
                k.act(V(vaug.ap[:, koff + t, :, 0:128], (vaug.key,)), V(pt.ap.rearrange("p (a b) -> p a b", a=4), pt.keys), AF.Copy)
                if not isB:
                    s, tl = t // 2, (t % 2) * 128
                    k.copy(ost.v(), pt, eng="act")
                    k.dma(nv_out[s, j, tl:tl + 128, hg * 512:(hg + 1) * 512], ost.v(), chan="stv")
                    pk = V(B3.ap[:, (t % 2) * 512:(t % 2) * 512 + 512], (B3.key,))
                    for kc in range(8):
                        k.mm(pk, h[:, kc, t * 128:(t + 1) * 128], wK[:, kc, :], start=(kc == 0), stop=(kc == 7))
                    k.copy(ost.v(), pk)
                    k.dma(nk_out[s, j, tl:tl + 128, hg * 512:(hg + 1) * 512], ost.v(), chan="stk")
            if isB:
                qblocks = [(0, 512, list(range(12))), (512, 512, list(range(12)))]
            else:
                qblocks = [(s * 256, 256, [2 * s, 2 * s + 1]) for s in range(4)]
            pti = 0
            oslots = [V(B1.ap[:, 0:129], (B1.key,)), V(B1.ap[:, 512:641], (B1.key,)),
                      V(B2.ap[:, 0:129], (B2.key,)), V(B2.ap[:, 512:641], (B2.key,))]
            for hh in range(4):
                for (q0, nq, kts) in qblocks:
                    nqt = nq // 128
                    for c in range(2):
                        def s_mm(ki, kt):
                            S = Sbank[ki % 2]
                            k.mm(V(S.ap[:, 0:nq], (S.key,)), kT[hh][c * 64:(c + 1) * 64, kt * 128:(kt + 1) * 128], qT[hh][c * 64:(c + 1) * 64, q0:q0 + nq])
                        s_mm(0, kts[0])
                        for ki, kt in enumerate(kts):
                            S = Sbank[ki % 2]
                            pt_ = PT[pti % 3]; pti += 1
                            k.act(pt_[:, 0:nq], V(S.ap[:, 0:nq], (S.key,)), AF.Exp, scale=ATT_SCALE)
                            if ki + 1 < len(kts):
                                s_mm(ki + 1, kts[ki + 1])
                            for qt in range(nqt):
                                k.mm(oslots[qt], pt_[:, qt * 128:(qt + 1) * 128],
                                     V(vaug.ap[:, kt, hh, 0:129], (vaug.key,)), start=(ki == 0), stop=(ki == len(kts) - 1))
                        for qt in range(nqt):
                            k.copy(Osb[c][:, qt, 0:129], oslots[qt])
                    for c in range(2):
                        k.recip(rs[:, c, 0:nqt], V(Osb[c].ap[:, 0:nqt, 128], (Osb[c].key,)))
                    k.act(rs2[:, 0, 0:nqt], rs[:, 0, 0:nqt], AF.Copy)
                    k.act(rs2[:, 1, 0:nqt], rs[:, 1, 0:nqt], AF.Copy, scale=ccol("nlam"))
                    o3 = V(o_t.ap[:, 0:nqt, :], (o_t.key,))
                    k.tt(o3, Osb[0][:, 0:nqt, 0:128], V(rs2.ap[:, 0, 0:nqt].unsqueeze(2).broadcast_to([128, nqt, 128]), (rs2.key,)), ALU.mult)
                    k.tt(V(ojunk.ap[:, 0:nqt, :], (ojunk.key,)), Osb[1][:, 0:nqt, 0:128],
                         V(rs2.ap[:, 1, 0:nqt].unsqueeze(2).broadcast_to([128, nqt, 128]), (rs2.key,)), ALU.mult)
                    k.tt(o3, o3, V(ojunk.ap[:, 0:nqt, :], (ojunk.key,)), ALU.add)
                    k.tt(V(ojunk.ap[:, 0:nqt, :], (ojunk.key,)), o3, o3, ALU.mult)
                    k.op("dve", (lambda nqt=nqt: nc.vector.reduce_sum(out=sso.ap[:, 0, 0:nqt], in_=ojunk.ap[:, 0:nqt, :], axis=mybir.AxisListType.X)),
                         reads=[ojunk.key], writes=[sso.key], osize=nqt)
                    k.act(sso[:, 1, 0:nqt], sso[:, 0, 0:nqt], AF.Ln, bias=EPS, scale=1.0 / 128)
                    k.act(sso[:, 1, 0:nqt], sso[:, 1, 0:nqt], AF.Exp, scale=-0.5)
                    k.tt(V(o_n.ap[:, 0:nqt, :], (o_n.key,)), o3, V(sso.ap[:, 1, 0:nqt].unsqueeze(2).broadcast_to([128, nqt, 128]), (sso.key,)), ALU.mult)
                    for qt in range(nqt):
                        k.tr(V(P3bf.ap[:, qt * 128:(qt + 1) * 128], (B3.key,)), o_n[:, qt, :], identb)
                    k.ts(V(yo.ap[:, hg * 4 + hh, q0:q0 + nq].rearrange("p (a b) -> p a b", a=nqt), (yo.key,)),
                         V(P3bf.ap[:, 0:nq].rearrange("p (a b) -> p a b", a=nqt), (B3.key,)), ccol("sgl"), None, ALU.mult)
        if not SKIP_ATT:
            out_proj(8, pts=[Sbank[0].v(), Sbank[1].v()])
        arena_reset(m0)

    def phase_odd(l, P, pc):
        j = l // 2
        m0 = arena_mark()
        h = alloc([128, 8, 1024], BF16, "h")
        phase_norm(l, gcols_mix[l], 0, 1, P, h)
        nseq, L = P["nseq"], P["L"]
        gg = [alloc([128, 1024], BF16, "gg") for _ in range(8)]
        xr = [alloc([128, 1024], BF16, "xr") for _ in range(8)]
        stg = alloc([128, nseq, L + 3], F32, "stg")
        k.memset(stg.v(), 0.0)
        bd = alloc([128, 4, 8, 128], BF16, "bd")
        k.memset(bd.v(), 0.0)
        for g, (src, dr) in enumerate(((lru_wa, 0), (lru_wx, 0), (lru_wa, 1), (lru_wx, 1))):
            sv = src[j, dr].rearrange("(c two) kk jj -> two kk c jj", two=2)
            for half in range(2):
                k.dma(V(bd.ap[half * 64:(half + 1) * 64, g, :, half * 64:(half + 1) * 64], (bd.key,)), sv[half], chan="bd", q="pool")
        tAll = alloc([128, 1024], F32, "tAll")
        tA = [Tile(tAll.ap[:, i * 512:(i + 1) * 512], tAll.key) for i in range(2)]
        specs = [([(lru_w_in[j][:, g * 512:(g + 1) * 512], 0, 512)], 8, 512) for g in range(4)]
        specs += [([(lru_w_out[j][:, g * 512:(g + 1) * 512], 0, 512)], 8, 512) for g in range(2)]
        wq = WSeq(specs)
        acc = Tile(tAll.ap.rearrange("p (a b) -> p a b", a=nseq), tAll.key)
        for c in range(8):
            w = wq.get(c // 4)
            for blk in range(2):
                pt = ps[blk]
                for kc in range(8):
                    k.mm(pt.v(), w[:, kc, (c % 4) * 128:(c % 4 + 1) * 128], h[:, kc, blk * 512:(blk + 1) * 512], start=(kc == 0), stop=(kc == 7))
                a = tA[blk]
                k.act(a.v(), pt.v(), AF.Square)
                k.ts(a.v(), a.v(), 0.044715, 1.0, ALU.mult, ALU.add)
                k.tt(a.v(), a.v(), pt.v(), ALU.mult)
                k.act(a.v(), a.v(), AF.Sigmoid, scale=2.0 * 0.7978845608028654)
                k.tt(gg[c][:, blk * 512:(blk + 1) * 512], a.v(), pt.v(), ALU.mult)
        for c in range(8):
            w = wq.get(2 + c // 4)
            pl = [ps[2], ps[3]]
            for blk in range(2):
                for kc in range(8):
                    k.mm(pl[blk].v(), w[:, kc, (c % 4) * 128:(c % 4 + 1) * 128], h[:, kc, blk * 512:(blk + 1) * 512], start=(kc == 0), stop=(kc == 7))
            conv_fm(pl, P, stg, 4, 2, pc["cw"] + c, 8, pc["cb"] + c, acc)
            k.copy(V(xr[c].ap.rearrange("p (a b) -> p a b", a=nseq), (xr[c].key,)), acc.v())
            if c == 0 and l == 1:
                dbg("xr0_%d" % P["v"], xr[0].v(), [128, 1024])
                dbg("gg0_%d" % P["v"], gg[0].v(), [128, 1024])
                dbg("h0_%d" % P["v"], h[:, 0, :], [128, 1024])
        rr = alloc([128, 1024], F32, "rr"); ii = alloc([128, 1024], F32, "ii")
        aa = alloc([128, 1024], F32, "aa"); uu = alloc([128, 1024], F32, "uu")
        hhb = [[alloc([128, 1024], F32, "hh") for _ in range(2)] for _ in range(2)]
        lst = alloc([128, 8, 2, NP_SEQ], F32, "lst")
        h0 = alloc([128, 2, 8], F32, "h0")
        USE_H0 = (P["v"] == 1) and not os.environ.get("NOH0")
        if USE_H0:
            for dr, src in enumerate((lru_f0, lru_b0)):
                k.dma(stage[0:8, :], rows(src[j], 8), chan="ld")
                k.tr(ps[7][:, 0:8], stage[0:8, :], V(cst.ap[0:8, 0, 0:8], (cst.key,)))
                k.copy(h0[:, dr, :], ps[7][:, 0:8])
            if debug:
                od = nc.dram_tensor("dbg_h0s", [128, 16], F32, kind="ExternalOutput").ap()
                k.dma(od, V(h0.ap.rearrange("p a b -> p (a b)"), (h0.key,)), chan="dbg")
        aaD = [aa, Tile(stg.ap.rearrange("p a b -> p (a b)")[:, 0:1024], stg.key)]
        uuD = [uu, Tile(tAll.ap, tAll.key)]

        def finish(c):
            hh = hhb[c % 2]
            k.tt(rr.v(), hh[0].v(), hh[1].v(), ALU.add)
            if P["v"] == 0:
                for s_ in range(nseq):
                    for dr_, col_ in ((0, (s_ + 1) * L - 1), (1, s_ * L)):
                        o_ap = lst.ap[:, c, dr_, s_:s_ + 1]
                        i_ap = hh[dr_].ap[:, col_:col_ + 1]
                        k.op("act", (lambda o_ap=o_ap, i_ap=i_ap: nc.scalar.activation(out=o_ap, in_=i_ap, func=AF.Copy)),
                             reads=[hh[dr_].key, rr.key], writes=[lst.key], osize=1)
            k.tt(h[:, c, :], rr.v(), gg[c].v(), ALU.mult)

        for c in range(8):
            hh = hhb[c % 2]
            for dr in range(2):
                for gi, dst in ((0, rr), (1, ii)):
                    g = dr * 2 + gi
                    bcolx = (pc["ba"] if gi == 0 else pc["bx"]) + dr * 8 + c
                    for blk in range(2):
                        pt = ps[(g * 2 + blk) % 4]
                        k.mm(pt.v(), V(bd.ap[:, g, c, :], (bd.key,)), xr[c][:, blk * 512:(blk + 1) * 512])
                        k.act(dst[:, blk * 512:(blk + 1) * 512], pt.v(), AF.Sigmoid, bias=cols[:, bcolx:bcolx + 1])
                lc = pc["nc8"] + dr * 8 + c
                k.act(aaD[dr].v(), rr.v(), AF.Exp, scale=cols[:, lc:lc + 1])
                k.act(uuD[dr].v(), aaD[dr].v(), AF.Square)
                k.act(uuD[dr].v(), uuD[dr].v(), AF.Identity, bias=1.0, scale=-1.0)
                k.act(uuD[dr].v(), uuD[dr].v(), AF.Sqrt)
                k.tt(uuD[dr].v(), uuD[dr].v(), ii.v(), ALU.mult)
                k.tt(uuD[dr].v(), uuD[dr].v(), xr[c].v(), ALU.mult)
                if c == 0 and l == 1 and dr == 0:
                    dbg("rr_%d" % P["v"], rr.v(), [128, 1024]); dbg("ii_%d" % P["v"], ii.v(), [128, 1024])
                    dbg("aa_%d" % P["v"], aaD[dr].v(), [128, 1024]); dbg("uu_%d" % P["v"], uuD[dr].v(), [128, 1024])
                for s in range(nseq):
                    sl = slice(s * L, (s + 1) * L)
                    first = s * L if dr == 0 else (s + 1) * L - 1
                    if USE_H0:
                        k.act(uuD[dr][:, first:first + 1], aaD[dr][:, first:first + 1], AF.Identity, bias=uuD[dr][:, first:first + 1], scale=h0[:, dr, c:c + 1])
                    if dr == 0:
                        k.scan(hh[0][:, sl], aaD[dr][:, sl], uuD[dr][:, sl], 0.0)
                    else:
                        rs = slice((s + 1) * L - 1, s * L - 1 if s > 0 else None, -1)
                        k.scan(V(hh[1].ap[:, rs], (hh[1].key,)), V(aaD[dr].ap[:, rs], (aaD[dr].key,)), V(uuD[dr].ap[:, rs], (uuD[dr].key,)), 0.0)
            if c >= 1:
                finish(c - 1)
        finish(7)
        if P["v"] == 0:
            k.tr(ps[7][0:64, 0:128], V(lst.ap.rearrange("p a b c -> p (a b c)"), (lst.key,)), ident)
            lrow = alloc([64, 128], F32, "lrow")
            k.copy(lrow.v(), ps[7][0:64, 0:128])
            if debug:
                od = nc.dram_tensor("dbg_lst", [128, 64], F32, kind="ExternalOutput").ap()
                k.dma(od, V(lst.ap.rearrange("p a b c -> p (a b c)"), (lst.key,)), chan="dbg")
                od2 = nc.dram_tensor("dbg_lrow", [64, 128], F32, kind="ExternalOutput").ap()
                k.dma(od2, lrow.v(), chan="dbg")
            for dr, dst in enumerate((lf_out, lb_out)):
                for c in range(8):
                    r0 = c * 8 + dr * 4
                    k.dma(dst[:, j, c * 128:(c + 1) * 128], lrow[r0:r0 + 4, :], chan="st")
        if l == 1:
            dbg("yo0_%d" % P["v"], h[:, 0, :], [128, 1024])
            dbg("yo5_%d" % P["v"], h[:, 5, :], [128, 1024])
        for d in range(8):
            w = wq.get(4 + d // 4)
            for blk in range(2):
                pt = ps[4 + (d * 2 + blk) % 2]
                for kc in range(8):
                    k.mm(pt.v(), w[:, kc, (d % 4) * 128:(d % 4 + 1) * 128], h[:, kc, blk * 512:(blk + 1) * 512], start=(kc == 0), stop=(kc == 7))
                resid_add(l, 2, P, d, blk, pt)
        if l == 1:
            dbg("xm0_%d" % P["v"], x[:, 0, P["t0"]:P["t0"] + 1024], [128, 1024])
        arena_reset(m0)

    gcols_mix, gcols_ffn, fcw, fcb = [], [], [], []
    ocols = {}
    ecols = {}
    for l in range(nlayers):
        gcols_mix.append(load_cols(rows(norm_mix_g[l], 8), 8))
        gcols_ffn.append(load_cols(rows(norm_ffn_g[l], 8), 8))
        fcw.append(load_cols(ffn_conv_w[l].rearrange("w (r c) -> (w r) c", c=128), 3 * 2 * NJ))
        fcb.append(load_cols(rows(ffn_conv_b[l], 2 * NJ), 2 * NJ))
        if l % 2 == 0:
            j = l // 2
            ec = {}
            ec["cw"] = load_cols(conv_w_e[j].rearrange("w (r c) -> (w r) c", c=128), 40)
            ec["cb"] = load_cols(rows(conv_b_e[j], 10), 10)
            ec["ng"] = load_cols(rows(ssd_norm_g[j], 8), 8)
            dn = load_cols(rows(diff_norm_g[j], 1), 1)
            base = colstate["n"]
            colstate["n"] += 32 + 32 + 16 + 256 + 32 + 8
            assert colstate["n"] <= NCOL
            ec["dtb"], alg, ec["d"], lp = base, base + 32, base + 64, base + 80
            ec["abc"] = base + 336
            sc = base + 368
            k.dma(cols[:, ec["dtb"]:ec["dtb"] + 32], dt_bias[j:j + 1, :].partition_broadcast(128), chan="ld")
            k.dma(cols[:, alg:alg + 32], a_log[j:j + 1, :].partition_broadcast(128), chan="ld")
            k.dma(cols[:, ec["d"]:ec["d"] + 16], ssd_d[j:j + 1, :].partition_broadcast(128), chan="ld")
            k.dma(cols[:, lp:lp + 256], diff_lambda[j:j + 1, :].partition_broadcast(128), chan="ld")
            k.act(cols[:, ec["abc"]:ec["abc"] + 32], cols[:, alg:alg + 32], AF.Exp)
            k.ts(cols[:, ec["abc"]:ec["abc"] + 32], cols[:, ec["abc"]:ec["abc"] + 32], -1.0, None, ALU.mult)
            lam_init = 0.8 - 0.6 * math.exp(-0.3 * l)
            k.tt(cols[:, lp:lp + 64], cols[:, lp:lp + 64], cols[:, lp + 64:lp + 128], ALU.mult)
            k.tt(cols[:, lp + 128:lp + 192], cols[:, lp + 128:lp + 192], cols[:, lp + 192:lp + 256], ALU.mult)
            k.memset(cols[:, sc:sc + 2], 0.0)
            k.act(cols[:, lp + 64:lp + 128], cols[:, lp:lp + 64], AF.Copy, accum=cols[:, sc:sc + 1])
            k.act(cols[:, lp + 192:lp + 256], cols[:, lp + 128:lp + 192], AF.Copy, accum=cols[:, sc + 1:sc + 2])
            k.act(cols[:, sc + 2:sc + 4], cols[:, sc:sc + 2], AF.Exp)
            k.tt(cols[:, sc + 4:sc + 5], cols[:, sc + 2:sc + 3], cols[:, sc + 3:sc + 4], ALU.subtract)
            k.act(cols[:, sc + 5:sc + 6], cols[:, sc + 4:sc + 5], AF.Identity, bias=-lam_init, scale=-1.0)
            k.act(cols[:, sc + 6:sc + 7], cols[:, dn:dn + 1], AF.Copy, scale=(1.0 - lam_init))
            ec["nlam"], ec["sgl"] = sc + 5, sc + 6
            ecols[l] = ec
        if l % 2 == 1:
            j = l // 2
            pc = {}
            pc["cw"] = load_cols(lru_conv_w[j].rearrange("w (r c) -> (w r) c", c=128), 32)
            pc["cb"] = load_cols(rows(lru_conv_b[j], 8), 8)
            pc["ba"] = load_cols(lru_ba[j].rearrange("d (r c) -> (d r) c", c=128), 16)
            pc["bx"] = load_cols(lru_bx[j].rearrange("d (r c) -> (d r) c", c=128), 16)
            lam = load_cols(lru_lambda[j].rearrange("d (r c) -> (d r) c", c=128), 16)
            pc["nc8"] = colstate["n"]
            colstate["n"] += 16
            dst = cols[:, pc["nc8"]:pc["nc8"] + 16]
            k.act(dst, cols[:, lam:lam + 16], AF.Exp, scale=-1.0)
            k.act(dst, dst, AF.Ln, bias=1.0)
            k.ts(dst, dst, -8.0, None, ALU.mult)
            ocols[l] = pc
    gfin = load_cols(rows(final_norm_g, 8), 8)

    for l in range(nlayers):
        for P in PASSES:
            if l % 2 == 1:
                phase_odd(l, P, ocols[l])
            else:
                if not os.environ.get("DISABLE_EVEN"):
                    phase_even(l, P, ecols[l])
            phase_ffn(l, P, fcw[l], fcb[l])

    for P in PASSES:
        m0 = arena_mark()
        hf = alloc([128, 8, 1024], F32, "hf")
        phase_norm(0, gfin, 0, 0, P, None, final=True, hf=hf)
        ost = [alloc([128, D], F32, "ost") for _ in range(2)]
        for t in range(8):
            o = ost[t % 2]
            for half in range(2):
                pt = ps[2 + half]
                for c4 in range(4):
                    c = half * 4 + c4
                    k.tr(pt[:, c4 * 128:(c4 + 1) * 128], hf[:, c, t * 128:(t + 1) * 128], ident)
                k.copy(o[:, half * 512:(half + 1) * 512], pt.v(), eng=("act" if half else "dve"))
            k.dma(y_out[P["t0"] + t * 128:P["t0"] + (t + 1) * 128, :], o.v(), chan="st")
        arena_reset(m0)

    k.emit()
    return nc, es


def host_consts():
    c = np.zeros((10, 128, 128), np.float32)
    c[0] = np.eye(128)
    R = np.zeros((128, 128), np.float32)
    for m in range(128):
        d = m % 32
        if d < 16:
            R[m + 16, m] = -1.0
        else:
            R[m - 16, m] = 1.0
    c[1] = R
    jj, qq = np.meshgrid(np.arange(128), np.arange(128), indexing="ij")
    c[2] = (jj <= qq)
    c[3] = (jj >= qq)
    c[4] = np.where(qq >= jj, 0.0, -30000.0)
    c[5] = np.where(qq <= jj, 0.0, -30000.0)
    c[6] = 1.0
    t = np.arange(LS)
    row = (t // 64).astype(np.float32)
    col = (t % 64).astype(np.float32)
    freqs = (10000.0 ** (-np.arange(0, 32, 2, dtype=np.float32) / 32.0)).astype(np.float32)
    tab = np.zeros((2, 128, LS), np.float32)
    for p in range(128):
        d = p % 64
        pos = row if d < 32 else col
        f = freqs[(d % 32) % 16]
        ang = (pos * f).astype(np.float32)
        tab[0, p] = np.cos(ang)
        tab[1, p] = np.sin(ang)
    return c, tab


_CACHE = {}


def make_in_maps(inputs):
    consts, tab = host_consts()
    maps = []
    for i in range(8):
        m = {}
        m["xin"] = np.ascontiguousarray(np.concatenate(
            [inputs["x_prompt"][4 * i:4 * i + 4].reshape(4 * LP, D), inputs["x_sample"][i]], axis=0))
        m["cvec"] = np.ascontiguousarray(np.stack([inputs["c_ctx"], inputs["c"][i]], axis=0))
        m["cache_k"] = np.ascontiguousarray(inputs["cache_attn_k"][i].reshape(2, PAST, D))
        m["cache_v"] = np.ascontiguousarray(inputs["cache_attn_v"][i].reshape(2, PAST, D))
        m["ssd_f0"] = np.ascontiguousarray(inputs["state_ssd_fwd"][i])
        m["ssd_b0"] = np.ascontiguousarray(inputs["state_ssd_bwd"][i])
        m["lru_f0"] = np.ascontiguousarray(inputs["state_lru_fwd"][i])
        m["lru_b0"] = np.ascontiguousarray(inputs["state_lru_bwd"][i])
        for nm in ("w_mod", "b_mod", "norm_mix_g", "norm_ffn_g", "ssd_attn_w_in", "ssd_conv_w", "ssd_conv_b",
                   "ssd_norm_g", "diff_norm_g", "ssd_attn_w_out", "lru_w_in", "lru_conv_w", "lru_conv_b", "lru_wa",
                   "lru_ba", "lru_wx", "lru_bx", "lru_lambda", "lru_w_out", "ffn_w_up", "ffn_conv_w", "ffn_conv_b",
                   "ffn_w_down", "final_norm_g"):
            m[nm] = np.ascontiguousarray(inputs[nm])
        m["ssd_a_log"] = np.ascontiguousarray(inputs["ssd_a_log"].reshape(2, 32))
        m["ssd_dt_bias"] = np.ascontiguousarray(inputs["ssd_dt_bias"].reshape(2, 32))
        m["ssd_d"] = np.ascontiguousarray(inputs["ssd_d"])
        m["diff_lambda"] = np.ascontiguousarray(inputs["diff_lambda"].reshape(2, 256))
        m["consts"] = consts
        m["ropetab"] = tab
        maps.append(m)
    return maps


def kernel(**inputs):
    inputs = {k_: np.asarray(v, dtype=np.float32) for k_, v in inputs.items()}
    if "nc" not in _CACHE:
        _CACHE["nc"] = build_program()
    nc, _es = _CACHE["nc"]
    maps = make_in_maps(inputs)
    res = run_bass_kernel_spmd(nc, maps, core_ids=list(range(8)))
    R = res.results
    y = np.stack([r["y_out"] for r in R])
    y_prompt = y[:, :1024].reshape(32, LP, D)
    y_sample = y[:, 1024:].reshape(8, LS, D)
    nk = np.concatenate([r["nk_out"] for r in R], axis=0).reshape(32, 2, LP, 8, 2, 64)
    nv = np.concatenate([r["nv_out"] for r in R], axis=0).reshape(32, 2, LP, 8, 128)
    sf = np.concatenate([r["sf_out"] for r in R], axis=0)
    sb = np.concatenate([r["sb_out"] for r in R], axis=0)
    lf = np.concatenate([r["lf_out"] for r in R], axis=0)
    lb = np.concatenate([r["lb_out"] for r in R], axis=0)
    return (y_prompt, y_sample, nk, nv, sf, sb, lf, lb)
```

```python
import math
import os
from contextlib import ExitStack
import numpy as np
import concourse.bass as bass
import concourse.mybir as mybir
from concourse.bass_utils import run_bass_kernel_spmd

F32 = mybir.dt.float32
BF16 = mybir.dt.bfloat16
AF = mybir.ActivationFunctionType
ALU = mybir.AluOpType

D = 1024
DEPTH = 4
NP_SEQ = 4
LP = 256
LS = 1024
NTOK = 2048
PAST = 512
EPS = 1e-6
D_FF = 2816
NJ = 22
C_XBC0, C_DT0, C_Q0, C_K0, C_V0 = 1024, 2304, 2336, 3360, 4384
ATT_SCALE = 64 ** -0.5


class V:
    __slots__ = ("ap", "keys")

    def __init__(self, ap, keys):
        self.ap = ap
        self.keys = tuple(keys)


class Tile:
    def __init__(self, ap, key):
        self.ap = ap
        self.key = key

    def __getitem__(self, idx):
        return V(self.ap[idx], (self.key,))

    def v(self):
        return V(self.ap, (self.key,))


def _ap(x):
    return x.ap if isinstance(x, V) else x


class KB:
    ENGS = ("pe", "act", "dve", "pool", "sp")

    def __init__(self, nc, es):
        self.nc = nc
        self.es = es
        self.eng = {"pe": nc.tensor, "act": nc.scalar, "dve": nc.vector, "pool": nc.gpsimd, "sp": nc.sync}
        self.ops = []
        self.count = {e: 0 for e in self.ENGS}
        self.writers = {}
        self.readers = {}
        self.seen = {e: {} for e in self.ENGS}
        self.chan_n = {}
        self.milestones = {e: set() for e in self.ENGS}
        self.pending_bar = {e: {} for e in self.ENGS}
        self.osize = {e: [] for e in self.ENGS}

    def op(self, eng, fn, reads=(), writes=(), chan=None, osize=1 << 20):
        idx = self.count[eng]
        self.count[eng] += 1
        self.osize[eng].append(osize)
        need = {}

        def add(src, val):
            if src == ("e", eng) and chan is None:
                if not (val >= idx - 4 and self.osize[eng][val] < 512):
                    return
            if src[0] == "c":
                val = self.chan_n[src[1]]
            if need.get(src, -1) < val:
                need[src] = val

        for k in reads:
            for s, v in self.writers.get(k, {}).items():
                add(s, v)
        for k in writes:
            for s, v in self.writers.get(k, {}).items():
                add(s, v)
            for s, v in self.readers.get(k, {}).items():
                add(s, v)
        for s, v in self.pending_bar[eng].items():
            add(s, v)
        self.pending_bar[eng] = {}
        if chan is not None and self.chan_n.get(chan, 0) > 0:
            add(("c", chan), self.chan_n[chan])
        deps = []
        for s, v in need.items():
            if self.seen[eng].get(s, -1) >= v:
                continue
            self.seen[eng][s] = v
            deps.append((s, v))
            if s[0] == "e":
                self.milestones[s[1]].add(v)
        if chan is not None:
            self.chan_n[chan] = self.chan_n.get(chan, 0) + 1
            me, myv = ("c", chan), self.chan_n[chan]
        else:
            me, myv = ("e", eng), idx
        for k in writes:
            self.writers[k] = {me: myv}
            self.readers[k] = {}
        for k in reads:
            if k not in writes:
                self.readers.setdefault(k, {})[me] = myv
        self.ops.append((eng, idx, fn, deps, chan))

    def barrier(self):
        comp = ("pe", "act", "dve")
        for e in comp + ("sp",):
            for o in comp:
                if o != e and self.count[o] > 0:
                    self.pending_bar[e][("e", o)] = self.count[o] - 1
            for c, n in self.chan_n.items():
                if not str(c).startswith("w") and n > 0:
                    self.pending_bar[e][("c", c)] = n

    def emit(self):
        nc = self.nc
        sems = {e: self.es.enter_context(nc.semaphore("s_" + e)) for e in self.ENGS}
        csems = {c: self.es.enter_context(nc.semaphore("c_%s" % str(c))) for c in self.chan_n}
        ranks = {}
        for e in self.ENGS:
            for r, i in enumerate(sorted(self.milestones[e])):
                ranks[(e, i)] = r + 1
        for eng, idx, fn, deps, chan in self.ops:
            E = self.eng[eng]
            for s, v in deps:
                if s[0] == "e":
                    E.wait_ge(sems[s[1]], ranks[(s[1], v)])
                else:
                    E.wait_ge(csems[s[1]], 16 * v)
            ins = fn()
            if chan is not None:
                ins.then_inc(csems[chan], 16)
            elif idx in self.milestones[eng]:
                ins.then_inc(sems[eng], 1)
        for c, n in self.chan_n.items():
            nc.sync.wait_ge(csems[c], 16 * n)

    @staticmethod
    def _fs(x):
        ap = _ap(x)
        n = 1
        for d in list(ap.shape)[1:]:
            n *= int(d)
        return n

    @staticmethod
    def _keys(*xs):
        ks = []
        for x in xs:
            if isinstance(x, V):
                ks.extend(x.keys)
        return ks

    def mm(self, out, lhsT, rhs, start=True, stop=True):
        self.op("pe", lambda: self.nc.tensor.matmul(_ap(out), lhsT=_ap(lhsT), rhs=_ap(rhs), start=start, stop=stop),
                reads=self._keys(lhsT, rhs), writes=self._keys(out))

    def tr(self, out, in_, ident):
        self.op("pe", lambda: self.nc.tensor.transpose(_ap(out), _ap(in_), _ap(ident)),
                reads=self._keys(in_, ident), writes=self._keys(out))

    def act(self, out, in_, func, bias=0.0, scale=1.0, accum=None):
        def f():
            kw = {}
            if accum is not None:
                kw["accum_out"] = _ap(accum)
            return self.nc.scalar.activation(out=_ap(out), in_=_ap(in_), func=func, bias=_ap(bias), scale=_ap(scale), **kw)
        self.op("act", f, reads=self._keys(in_, bias, scale), writes=self._keys(out, accum),
                osize=(1 if accum is not None else self._fs(out)))

    def tt(self, out, in0, in1, op, eng="dve"):
        E = self.eng[eng]
        self.op(eng, lambda: E.tensor_tensor(out=_ap(out), in0=_ap(in0), in1=_ap(in1), op=op),
                reads=self._keys(in0, in1), writes=self._keys(out), osize=self._fs(out))

    def ts(self, out, in0, s1, s2, op0, op1=None, eng="dve"):
        E = self.eng[eng]
        if op1 is None:
            f = lambda: E.tensor_scalar(out=_ap(out), in0=_ap(in0), scalar1=_ap(s1), scalar2=None, op0=op0)
        else:
            f = lambda: E.tensor_scalar(out=_ap(out), in0=_ap(in0), scalar1=_ap(s1), scalar2=_ap(s2), op0=op0, op1=op1)
        self.op(eng, f, reads=self._keys(in0, s1, s2), writes=self._keys(out), osize=self._fs(out))

    def stt(self, out, in0, scalar, in1, op0, op1, eng="dve"):
        E = self.eng[eng]
        self.op(eng, lambda: E.scalar_tensor_tensor(out=_ap(out), in0=_ap(in0), scalar=_ap(scalar), in1=_ap(in1), op0=op0, op1=op1),
                reads=self._keys(in0, scalar, in1), writes=self._keys(out), osize=self._fs(out))

    def copy(self, out, in_, eng="dve"):
        if eng == "act":
            return self.act(out, in_, AF.Copy)
        E = self.eng[eng]
        self.op(eng, lambda: E.tensor_copy(out=_ap(out), in_=_ap(in_)), reads=self._keys(in_), writes=self._keys(out), osize=self._fs(out))

    def memset(self, out, val, eng="dve"):
        E = self.eng[eng]
        self.op(eng, lambda: E.memset(_ap(out), val), writes=self._keys(out), osize=self._fs(out))

    def recip(self, out, in_):
        self.op("dve", lambda: self.nc.vector.reciprocal(out=_ap(out), in_=_ap(in_)), reads=self._keys(in_), writes=self._keys(out), osize=self._fs(out))

    def scan(self, out, d0, d1, init):
        self.op("dve", lambda: self.nc.vector.tensor_tensor_scan(out=_ap(out), data0=_ap(d0), data1=_ap(d1), initial=_ap(init),
                                                                 op0=ALU.mult, op1=ALU.add),
                reads=self._keys(d0, d1, init), writes=self._keys(out))

    def dma(self, out, in_, chan, q="sp"):
        E = self.eng[q]
        self.op(q, lambda: E.dma_start(out=_ap(out), in_=_ap(in_)), reads=self._keys(in_), writes=self._keys(out), chan=chan)


def build_program(nlayers=DEPTH, debug=False):
    nc = bass.Bass("TRN2", target_bir_lowering=False)
    es = ExitStack()
    k = KB(nc, es)

    def din(name, shape):
        return nc.dram_tensor(name, list(shape), F32, kind="ExternalInput").ap()

    def dout(name, shape):
        return nc.dram_tensor(name, list(shape), F32, kind="ExternalOutput").ap()

    xin = din("xin", [NTOK, D])
    cvec = din("cvec", [2, D])
    cache_k = din("cache_k", [2, PAST, D])
    cache_v = din("cache_v", [2, PAST, D])
    ssd_f0 = din("ssd_f0", [2, 16, 64, 64])
    ssd_b0 = din("ssd_b0", [2, 16, 64, 64])
    lru_f0 = din("lru_f0", [2, D])
    lru_b0 = din("lru_b0", [2, D])
    w_mod = din("w_mod", [4, D, 6 * D]); b_mod = din("b_mod", [4, 6 * D])
    norm_mix_g = din("norm_mix_g", [4, D]); norm_ffn_g = din("norm_ffn_g", [4, D])
    w_in_e = din("ssd_attn_w_in", [2, D, 5408]); conv_w_e = din("ssd_conv_w", [2, 4, 1280]); conv_b_e = din("ssd_conv_b", [2, 1280])
    a_log = din("ssd_a_log", [2, 32]); dt_bias = din("ssd_dt_bias", [2, 32]); ssd_d = din("ssd_d", [2, 16])
    ssd_norm_g = din("ssd_norm_g", [2, D]); diff_lambda = din("diff_lambda", [2, 256]); diff_norm_g = din("diff_norm_g", [2, 128])
    w_out_e = din("ssd_attn_w_out", [2, 2048, D])
    lru_w_in = din("lru_w_in", [2, D, 2048]); lru_conv_w = din("lru_conv_w", [2, 4, D]); lru_conv_b = din("lru_conv_b", [2, D])
    lru_wa = din("lru_wa", [2, 2, 16, 64, 64]); lru_ba = din("lru_ba", [2, 2, D])
    lru_wx = din("lru_wx", [2, 2, 16, 64, 64]); lru_bx = din("lru_bx", [2, 2, D])
    lru_lambda = din("lru_lambda", [2, 2, D]); lru_w_out = din("lru_w_out", [2, D, D])
    ffn_w_up = din("ffn_w_up", [4, D, 2 * D_FF]); ffn_conv_w = din("ffn_conv_w", [4, 3, 2 * D_FF]); ffn_conv_b = din("ffn_conv_b", [4, 2 * D_FF])
    ffn_w_down = din("ffn_w_down", [4, D_FF, D]); final_norm_g = din("final_norm_g", [D])
    consts = din("consts", [10, 128, 128])
    ropetab = din("ropetab", [2, 128, LS])

    y_out = dout("y_out", [NTOK, D])
    nk_out = dout("nk_out", [NP_SEQ, 2, LP, D])
    nv_out = dout("nv_out", [NP_SEQ, 2, LP, D])
    sf_out = dout("sf_out", [NP_SEQ, 2, 16, 64, 64])
    sb_out = dout("sb_out", [NP_SEQ, 2, 16, 64, 64])
    lf_out = dout("lf_out", [NP_SEQ, 2, D])
    lb_out = dout("lb_out", [NP_SEQ, 2, D])

    dbgst = {}

    def dbg(name, v, shape):
        if not debug:
            return
        o = nc.dram_tensor("dbg_" + name, list(shape), F32, kind="ExternalOutput").ap()
        if "t" not in dbgst:
            dbgst["t"] = sbt("dbgst", [128, 1024], F32)
        st_ = dbgst["t"]
        n_ = shape[1]
        k.copy(st_[:, 0:n_], v)
        k.dma(o, st_[:, 0:n_], chan="dbg")

    def sbt(name, shape, dt):
        return Tile(es.enter_context(nc.sbuf_tensor(name, list(shape), dt))[:], name)

    x = sbt("x", [128, 8, NTOK], F32)
    cst = sbt("cst", [128, 10, 128], F32)
    cstb = sbt("cstb", [128, 10, 128], BF16)
    NCOL = 1150 if debug else 2300
    cols = sbt("cols", [128, NCOL], F32)
    mod = sbt("mod", [128, 4, 48, 2], F32)
    wslots = [sbt("wslot%d" % i, [128, 4096], BF16) for i in range(3)]
    ARENA_W = 25400
    arena = es.enter_context(nc.sbuf_tensor("arena", [128, ARENA_W], F32))[:]
    big = [Tile(es.enter_context(nc.psum_tensor("big%d" % i, [128, 1024], F32))[:], "big%d" % i) for i in range(4)]
    ps = [Tile(big[i // 2].ap[:, (i % 2) * 512:(i % 2 + 1) * 512], "ps%d" % i) for i in range(8)]
    stage = sbt("stage", [128, 128], F32)

    astate = {"off": 0, "n": 0}

    def alloc(shape, dt, name=None):
        n = 1
        for s in shape[1:]:
            n *= s
        words = n if dt == F32 else (n + 1) // 2
        o = astate["off"]
        assert o + words <= ARENA_W, ("arena overflow", o, words)
        astate["off"] = o + words
        astate["n"] += 1
        ap = arena[:, o:o + words]
        if dt != F32:
            ap = ap.bitcast(dt)[:, 0:n]
        if len(shape) == 3:
            ap = ap.rearrange("p (a b) -> p a b", a=shape[1])
        elif len(shape) == 4:
            ap = ap.rearrange("p (a b c) -> p a b c", a=shape[1], b=shape[2])
        if shape[0] != 128:
            ap = ap[0:shape[0]]
        return Tile(ap, "%s_%d" % (name or "a", astate["n"]))

    def arena_mark():
        return astate["off"]

    def arena_reset(m):
        astate["off"] = m
        k.barrier()

    wstate = {"i": 0}

    def wload(pieces, kc, cols_total):
        i = wstate["i"] % 3
        wstate["i"] += 1
        slot = wslots[i]
        view = slot.ap[:, 0:kc * cols_total].rearrange("p (a b) -> p a b", a=kc)
        for (src, off, c) in pieces:
            k.dma(V(view[:, :, off:off + c], (slot.key,)), src.rearrange("(a p) n -> p a n", p=128), chan="w%d" % i, q="pool")
        return Tile(view, slot.key)

    class WSeq:
        def __init__(self, specs, ahead=2):
            self.specs = specs
            self.tiles = {}
            self.nxt = 0
            self.ahead = ahead

        def get(self, i):
            while self.nxt < len(self.specs) and self.nxt <= i + self.ahead:
                self.tiles[self.nxt] = wload(*self.specs[self.nxt])
                self.nxt += 1
            return self.tiles[i]

    k.dma(cst.v(), consts.rearrange("a p n -> p a n"), chan="ld")
    k.copy(cstb.v(), cst.v())
    ident, identb = cst[:, 0, :], cstb[:, 0, :]
    Rb = cstb[:, 1, :]
    Tdir = [cst[:, 2, :], cst[:, 3, :]]
    maskb = [cstb[:, 4, :], cstb[:, 5, :]]
    onesb = cstb[:, 6, :]
    ones = cst[:, 6, :]

    colstate = {"n": 0}

    def load_cols(src_rows, nrows):
        off = colstate["n"]
        done = 0
        while done < nrows:
            r = min(128, nrows - done)
            k.dma(stage[0:r, :], src_rows[done:done + r, :], chan="ld")
            k.tr(ps[7][:, 0:r], stage[0:r, :], V(cst.ap[0:r, 0, 0:r], (cst.key,)))
            k.copy(cols[:, off + done:off + done + r], ps[7][:, 0:r])
            done += r
        colstate["n"] += nrows
        assert colstate["n"] <= NCOL
        return off

    def rows(ap1d_or_2d, n):
        return ap1d_or_2d.rearrange("(r c) -> r c", c=128)

    xm = arena_mark()
    xst = [alloc([128, D], F32, "xst") for _ in range(2)]
    for t in range(NTOK // 128):
        st = xst[t % 2]
        k.dma(st.v(), xin[t * 128:(t + 1) * 128, :], chan="xl%d" % (t % 2))
        for half in range(2):
            pt = ps[half]
            for c4 in range(4):
                c = half * 4 + c4
                k.tr(pt[:, c4 * 128:(c4 + 1) * 128], st[:, c * 128:(c + 1) * 128], ident)
            k.copy(V(x.ap[:, half * 4:half * 4 + 4, t * 128:(t + 1) * 128], (x.key,)),
                   V(pt.ap.rearrange("p (a b) -> p a b", a=4), (pt.key,)), eng=("act" if half else "dve"))
    arena_reset(xm)

    cm = arena_mark()
    cT = alloc([128, 16], F32, "cT")
    cTb = alloc([128, 16], BF16, "cTb")
    k.dma(stage[0:16, :], cvec.rearrange("v (c q) -> (v c) q", q=128), chan="ld")
    k.tr(ps[7][:, 0:16], stage[0:16, :], V(cst.ap[0:16, 0, 0:16], (cst.key,)))
    k.act(cT.v(), ps[7][:, 0:16], AF.Silu)
    k.copy(cTb.v(), cT.v())
    cTb3 = cTb.ap.rearrange("p (v c) -> p c v", v=2)
    for l in range(nlayers):
        bcol = load_cols(rows(b_mod[l], 48), 48)
        specs = [([(w_mod[l][:, g * 512:(g + 1) * 512], 0, 512)], 8, 512) for g in range(12)]
        wq = WSeq(specs)
        for g in range(12):
            w = wq.get(g)
            for s4 in range(4):
                ch = g * 4 + s4
                for kc in range(8):
                    k.mm(ps[6][:, ch * 2:ch * 2 + 2], w[:, kc, s4 * 128:(s4 + 1) * 128], V(cTb3[:, kc, :], (cTb.key,)),
                         start=(kc == 0), stop=(kc == 7))
        k.tt(V(mod.ap[:, l, :, :], (mod.key,)), V(ps[6].ap[:, 0:96].rearrange("p (a b) -> p a b", b=2), (ps[6].key,)),
             V(cols.ap[:, bcol:bcol + 48].unsqueeze(2).broadcast_to([128, 48, 2]), (cols.key,)), ALU.add)
    arena_reset(cm)

    PASSES = [dict(t0=0, nseq=NP_SEQ, L=LP, v=0), dict(t0=1024, nseq=1, L=LS, v=1)]

    def modcol(l, which, c, v):
        return V(mod.ap[:, l, which * 8 + c, v:v + 1], (mod.key,))

    def phase_norm(l, gcol, which_shift, which_scale, P, h, final=False, hf=None):
        m = arena_mark()
        AB = alloc([128, 8, 2], F32, "AB")
        sq = [alloc([128, 512], BF16, "sq") for _ in range(2)]
        lnv = alloc([128, 512], F32, "lnv")
        rstd = alloc([128, 512], F32, "rstd")
        tmp = [alloc([128, 512], F32, "tmp") for _ in range(2)]
        for c in range(8):
            if final:
                k.copy(AB[:, c, 0:1], cols[:, gcol + c:gcol + c + 1])
                k.memset(AB[:, c, 1:2], 0.0)
            else:
                k.ts(AB[:, c, 0:1], modcol(l, which_scale, c, P["v"]), 1.0, cols[:, gcol + c:gcol + c + 1], ALU.add, ALU.mult)
                k.copy(AB[:, c, 1:2], modcol(l, which_shift, c, P["v"]))
        for blk in range(2):
            tok = slice(P["t0"] + blk * 512, P["t0"] + blk * 512 + 512)
            for c in range(8):
                s = sq[c % 2]
                k.act(s.v(), x[:, c, tok], AF.Square)
                k.mm(ps[0].v(), onesb, s.v(), start=(c == 0), stop=(c == 7))
            k.act(lnv.v(), ps[0].v(), AF.Ln, bias=EPS, scale=1.0 / D)
            k.act(rstd.v(), lnv.v(), AF.Exp, scale=-0.5)
            for c in range(8):
                t = tmp[c % 2]
                k.tt(t.v(), x[:, c, tok], rstd.v(), ALU.mult)
                dst = (hf if final else h)[:, c, blk * 512:(blk + 1) * 512]
                k.ts(dst, t.v(), AB[:, c, 0:1], AB[:, c, 1:2], ALU.mult, ALU.add)
        astate["off"] = m

    def conv_fm(src_ps_list, P, stg, width, left, wcol0, wstride, bcol, acc):
        nseq, L = P["nseq"], P["L"]
        for blk in range(2):
            if nseq == 1:
                dst = V(stg.ap[:, 0, left + blk * 512:left + blk * 512 + 512], (stg.key,))
                src = src_ps_list[blk].v()
            else:
                dst = V(stg.ap[:, 2 * blk:2 * blk + 2, left:left + L], (stg.key,))
                src = V(src_ps_list[blk].ap.rearrange("p (a b) -> p a b", a=2), (src_ps_list[blk].key,))
            k.act(dst, src, AF.Copy)
        k.act(acc.v(), V(stg.ap[:, :, 0:L], (stg.key,)), AF.Identity, bias=cols[:, bcol:bcol + 1], scale=cols[:, wcol0:wcol0 + 1])
        for j in range(1, width):
            wc = wcol0 + j * wstride
            k.stt(acc.v(), V(stg.ap[:, :, j:j + L], (stg.key,)), cols[:, wc:wc + 1], acc.v(), ALU.mult, ALU.add)

    def resid_add(l, which_gate, P, d, blk, pst):
        tok = slice(P["t0"] + blk * 512, P["t0"] + blk * 512 + 512)
        k.stt(x[:, d, tok], pst.v(), modcol(l, which_gate, d, P["v"]), x[:, d, tok], ALU.mult, ALU.add)

    def phase_ffn(l, P, cw, cb):
        m0 = arena_mark()
        h = alloc([128, 8, 1024], BF16, "h")
        phase_norm(l, gcols_ffn[l], 3, 4, P, h)
        nseq, L = P["nseq"], P["L"]
        actT = [alloc([128, 1024], BF16, "act") for _ in range(NJ)]
        stg = [[alloc([128, nseq, L + 2], F32, "stg") for _ in range(2)] for _ in range(2)]
        accv = [alloc([128, nseq, L], F32, "accv") for _ in range(2)]
        accg = [alloc([128, nseq, L], F32, "accg") for _ in range(2)]
        for sp_ in stg:
            for s in sp_:
                k.memset(s.v(), 0.0)
        specs = []
        for jj in range(11):
            j0 = jj * 2
            specs.append(([(ffn_w_up[l][:, j0 * 128:(j0 + 2) * 128], 0, 256),
                           (ffn_w_up[l][:, D_FF + j0 * 128:D_FF + (j0 + 2) * 128], 256, 256)], 8, 512))
        for d in range(8):
            specs.append(([(ffn_w_down[l][:, d * 128:(d + 1) * 128], 0, 128)], NJ, 128))
        wq = WSeq(specs)
        for j in range(NJ):
            w = wq.get(j // 2)
            jo = (j % 2) * 128
            par = j % 2
            for half, (coff, stgt, acc) in enumerate(((jo, stg[par][0], accv[par]), (256 + jo, stg[par][1], accg[par]))):
                pl = [ps[par * 4 + half * 2], ps[par * 4 + half * 2 + 1]]
                for blk in range(2):
                    for kc in range(8):
                        k.mm(pl[blk].v(), w[:, kc, coff:coff + 128], h[:, kc, blk * 512:(blk + 1) * 512], start=(kc == 0), stop=(kc == 7))
                fcol = j + (NJ if half else 0)
                conv_fm(pl, P, stgt, 3, 1, cw + fcol, 2 * NJ, cb + fcol, acc)
            k.act(accg[par].v(), accg[par].v(), AF.Silu)
            k.tt(V(actT[j].ap.rearrange("p (a b) -> p a b", a=nseq), (actT[j].key,)), accg[par].v(), accv[par].v(), ALU.mult)
        for d in range(8):
            w = wq.get(11 + d)
            for blk in range(2):
                pt = ps[4 + (d * 2 + blk) % 2]
                for j in range(NJ):
                    k.mm(pt.v(), w[:, j, :], actT[j][:, blk * 512:(blk + 1) * 512], start=(j == 0), stop=(j == NJ - 1))
                resid_add(l, 5, P, d, blk, pt)
        arena_reset(m0)

    def phase_even(l, P, ec):
        j = l // 2
        lam_init = 0.8 - 0.6 * math.exp(-0.3 * l)
        m0 = arena_mark()
        h = alloc([128, 8, 1024], BF16, "h")
        phase_norm(l, gcols_mix[l], 0, 1, P, h)
        yo = alloc([128, 8, 1024], BF16, "yo")
        nseq, L, isB = P["nseq"], P["L"], (P["v"] == 1)
        B0, B1, B2, B3 = big
        P3bf = V(B3.ap.bitcast(BF16), (B3.key,))

        def ccol(name, i=0):
            return cols[:, ec[name] + i:ec[name] + i + 1]

        def out_proj(kbase, pts=None):
            specs = [([(w_out_e[j][kbase * 128:kbase * 128 + 1024, dg * 256:(dg + 1) * 256], 0, 256)], 8, 256) for dg in range(4)]
            wq = WSeq(specs)
            for d in range(8):
                w = wq.get(d // 2)
                for blk in range(2):
                    pt = pts[blk] if pts is not None else V(B0.ap[:, blk * 512:(blk + 1) * 512], (B0.key,))
                    for kc in range(8):
                        k.mm(pt, w[:, kc, (d % 2) * 128:(d % 2 + 1) * 128], V(yo.ap[:, kc, blk * 512:(blk + 1) * 512], (yo.key,)),
                             start=(kc == 0), stop=(kc == 7))
                    tok = slice(P["t0"] + blk * 512, P["t0"] + blk * 512 + 512)
                    k.stt(x[:, d, tok], pt, modcol(l, 2, d, P["v"]), x[:, d, tok], ALU.mult, ALU.add)

        ms = arena_mark()
        xbc = [alloc([128, 256], BF16, "xbc") for _ in range(10)]
        stg = [alloc([128, 262], F32, "stg") for _ in range(2)]
        acc = [alloc([128, 256], F32, "acc") for _ in range(2)]
        xsT = [alloc([128, 1024], BF16, "xsT") for _ in range(2)]
        BT = [alloc([128, 128], BF16, "BT") for _ in range(2)]
        sz = [alloc([128, 1024], BF16, "sz") for _ in range(2)]
        dtr = alloc([128, 2, 32], F32, "dtr"); dt = alloc([128, 2, 32], F32, "dt"); dta = alloc([128, 2, 32], F32, "dta")
        Scol = alloc([128, 2, 32], F32, "Scol"); eU = alloc([128, 2, 32], F32, "eU"); dend = alloc([128, 2, 32], F32, "dend")
        dec = alloc([128, 2, 32], F32, "dec"); wdd = alloc([128, 2, 32], F32, "wdd"); Utot = alloc([128, 2, 32], F32, "Utot")
        xdt = [alloc([128, 1024], BF16, "xdt") for _ in range(2)]
        xdd = xdt
        xD = alloc([128, 1024], BF16, "xD")
        CBT = alloc([128, 2, 128], BF16, "CBT")
        segT = alloc([128, 8, 128], F32, "segT")
        LTs = [alloc([128, 8, 128], BF16, "LT") for _ in range(2)]
        MT = [[alloc([128, 8, 128], BF16, "MT") for _ in range(2)] for _ in range(2)]
        ysb = alloc([128, 1024], F32, "ysb")
        tmpy = Tile(segT.ap.rearrange("p a b -> p (a b)"), segT.key)
        gn = alloc([128, 1024], BF16, "gn")
        junk = gn
        ssq = alloc([128, 2], F32, "ssq")
        Hs = [alloc([128, 512], F32, "Hs") for _ in range(2)]
        Hb16 = [[alloc([128, 512], BF16, "Hb16") for _ in range(2)] for _ in range(2)]
        Hent_b = [alloc([128, 512], BF16, "Hentb") for _ in range(8)] if isB else None
        stF = alloc([128, 4, 2, 64], F32, "stF")
        Dbc = V(cols.ap[:, ec["d"]:ec["d"] + 16].unsqueeze(2).broadcast_to([128, 16, 64]), (cols.key,))

        def v3(t, a):
            return V(t.ap.rearrange("p (a b) -> p a b", a=a), (t.key,))

        def group_prep(gi, need_z):
            g0 = gi * 256
            seq0 = (g0 // L) * L
            lo, hi = max(seq0, g0 - 2), min(seq0 + L, g0 + 257)
            n_in = hi - lo
            so = lo - (g0 - 2)
            specs = []
            if need_z:
                specs += [([(w_in_e[j][:, 0:512], 0, 512)], 8, 512), ([(w_in_e[j][:, 512:1024], 0, 512)], 8, 512)]
            specs += [([(w_in_e[j][:, 1024:1536], 0, 512)], 8, 512), ([(w_in_e[j][:, 1536:2048], 0, 512)], 8, 512),
                      ([(w_in_e[j][:, 2048:2336], 0, 288)], 8, 288)]
            wq = WSeq(specs)
            wi = 0
            if need_z:
                for half in range(2):
                    w = wq.get(wi); wi += 1
                    for tt in range(2):
                        pt = V(B3.ap[:, (tt % 2) * 512:(tt % 2) * 512 + 512], (B3.key,))
                        for kc in range(8):
                            k.mm(pt, h[:, kc, g0 + tt * 128:g0 + (tt + 1) * 128], w[:, kc, :], start=(kc == 0), stop=(kc == 7))
                        k.act(sz[tt][:, half * 512:(half + 1) * 512], pt, AF.Silu)
            pend_silu = []
            for c in range(10):
                if c % 4 == 0:
                    w = wq.get(wi); wi += 1
                st_, ac_ = stg[c % 2], acc[c % 2]
                pt = V(B0.ap[:, (c % 2) * 512:(c % 2) * 512 + n_in], (B0.key,))
                for kc in range(8):
                    k.mm(pt, w[:, kc, (c % 4) * 128:(c % 4 + 1) * 128], h[:, kc, lo:hi], start=(kc == 0), stop=(kc == 7))
                k.memset(st_.v(), 0.0)
                k.act(st_[:, so:so + n_in], pt, AF.Copy)
                cw, cb = ec["cw"] + c, ec["cb"] + c
                k.act(ac_.v(), st_[:, 0:256], AF.Identity, bias=cols[:, cb:cb + 1], scale=cols[:, cw:cw + 1])
                if pend_silu:
                    pend_silu.pop()()
                for kk in range(1, 4):
                    k.stt(ac_.v(), st_[:, kk:kk + 256], cols[:, cw + 10 * kk:cw + 10 * kk + 1], ac_.v(), ALU.mult, ALU.add)
                pend_silu.append(lambda c=c, ac_=ac_: k.act(xbc[c].v(), ac_.v(), AF.Silu))
            pend_silu.pop()()
            if int(os.environ.get("PREP_LEVEL", "9")) < 3:
                return
            pdt = V(B1.ap[:, 0:64].rearrange("p (a b) -> p a b", a=2), (B1.key,))
            pS = V(B1.ap[:, 64:128].rearrange("p (a b) -> p a b", a=2), (B1.key,))
            pU = V(B1.ap[:, 128:192].rearrange("p (a b) -> p a b", a=2), (B1.key,))
            steps = []
            def s1():
                for tt in range(2):
                    for kc in range(8):
                        k.mm(V(B1.ap[:, tt * 32:(tt + 1) * 32], (B1.key,)), h[:, kc, g0 + tt * 128:g0 + (tt + 1) * 128], w[:, kc, 256:288],
                             start=(kc == 0), stop=(kc == 7))
            steps.append(s1)
            steps.append(lambda: k.tt(dtr.v(), pdt, V(cols.ap[:, ec["dtb"]:ec["dtb"] + 32].unsqueeze(1).broadcast_to([128, 2, 32]), (cols.key,)), ALU.add))
            steps.append(lambda: k.act(dtr.v(), dtr.v(), AF.Exp))
            steps.append(lambda: k.act(dt.v(), dtr.v(), AF.Ln, bias=1.0))
            steps.append(lambda: k.tt(dta.v(), dt.v(), V(cols.ap[:, ec["abc"]:ec["abc"] + 32].unsqueeze(1).broadcast_to([128, 2, 32]), (cols.key,)), ALU.mult))
            def s6():
                for tt in range(2):
                    for dr in range(2):
                        k.mm(V(B1.ap[:, 64 + tt * 32 + dr * 16:64 + tt * 32 + dr * 16 + 16], (B1.key,)), Tdir[dr], dta[:, tt, dr * 16:(dr + 1) * 16])
                    k.mm(V(B1.ap[:, 128 + tt * 32:128 + (tt + 1) * 32], (B1.key,)), ones, dta[:, tt, :])
            steps.append(s6)
            steps.append(lambda: k.copy(Scol.v(), pS))
            steps.append(lambda: k.copy(Utot.v(), pU))
            steps.append(lambda: k.act(eU.v(), Scol.v(), AF.Exp))
            steps.append(lambda: k.act(dec.v(), Utot.v(), AF.Exp))
            steps.append(lambda: k.tt(dend.v(), Utot.v(), Scol.v(), ALU.subtract))
            steps.append(lambda: k.act(dend.v(), dend.v(), AF.Exp))
            steps.append(lambda: k.tt(wdd.v(), dend.v(), dt.v(), ALU.mult))
            for st_i, st_f in enumerate(steps):
                if st_i < int(os.environ.get("PREP_STEPS", "99")):
                    st_f()
            if int(os.environ.get("PREP_LEVEL", "9")) < 4:
                return
            for tt in range(2):
                for c in range(8):
                    k.tr(V(P3bf.ap[:, c * 128:(c + 1) * 128], (B3.key,)), xbc[c][:, tt * 128:(tt + 1) * 128], identb)
                k.tr(V(P3bf.ap[:, 1024:1152], (B3.key,)), xbc[8][:, tt * 128:(tt + 1) * 128], identb)
                k.act(xsT[tt].v(), V(P3bf.ap[:, 0:1024], (B3.key,)), AF.Copy)
                k.act(BT[tt].v(), V(P3bf.ap[:, 1024:1152], (B3.key,)), AF.Copy)

        def bc16(t, tt, dr):
            return V(t.ap[:, tt, dr * 16:(dr + 1) * 16].unsqueeze(2).broadcast_to([128, 16, 64]), (t.key,))

        SSD_LEVEL = int(os.environ.get("SSD_LEVEL", "3"))

        def chunk_states(tt, dr, pst):
            if SSD_LEVEL < 2:
                return
            k.tt(v3(xdd[dr], 16), v3(xsT[tt], 16), bc16(wdd, tt, dr), ALU.mult, eng="dve")
            for g in range(2):
                k.mm(V(pst.ap[g * 64:(g + 1) * 64, :], pst.keys), BT[tt][:, g * 64:(g + 1) * 64], xdd[dr][:, g * 512:(g + 1) * 512])

        def state_step(Ht, tt, dr, pst, have):
            if SSD_LEVEL < 2:
                return
            if not have:
                k.copy(Ht.v(), pst)
                return
            for g in range(2):
                hv = V(Ht.ap[g * 64:(g + 1) * 64, :].rearrange("p (a b) -> p a b", a=8), (Ht.key,))
                dv = V(dec.ap[g * 64:(g + 1) * 64, tt, dr * 16 + g * 8:dr * 16 + g * 8 + 8].unsqueeze(2).broadcast_to([64, 8, 64]), (dec.key,))
                k.tt(hv, hv, dv, ALU.mult)
            k.tt(Ht.v(), Ht.v(), pst, ALU.add)

        def chunk_y(gi, tt, ent):
            tok = slice(gi * 256 + tt * 128, gi * 256 + (tt + 1) * 128)
            tl = slice(tt * 128, (tt + 1) * 128)
            if SSD_LEVEL < 3:
                return
            YS = int(os.environ.get("Y_STEPS", "99"))
            for g in range(2):
                k.mm(V(B3.ap[:, g * 512:g * 512 + 128], (B3.key,)), xbc[8][g * 64:(g + 1) * 64, tl], xbc[9][g * 64:(g + 1) * 64, tl])
            for g in range(2):
                k.copy(CBT[:, g, :], V(B3.ap[:, g * 512:g * 512 + 128], (B3.key,)))
            DB = debug and (not isB) and gi == 0 and tt == 0
            if DB:
                dbg("CBT", V(CBT.ap.rearrange("p a b -> p (a b)"), (CBT.key,)), [128, 256])
                dbg("xsT", xsT[tt].v(), [128, 1024])
                dbg("dt", V(dt.ap.rearrange("p a b -> p (a b)"), (dt.key,)), [128, 64])
                dbg("Scol", V(Scol.ap.rearrange("p a b -> p (a b)"), (Scol.key,)), [128, 64])
            if YS < 2:
                return
            for dr in range(2):
                k.tt(v3(xdt[dr], 16), v3(xsT[tt], 16), bc16(dt, tt, dr), ALU.mult, eng="dve")
            k.tt(v3(xD, 16), v3(xsT[tt], 16), Dbc, ALU.mult, eng="dve")
            if YS < 3:
                return
            its = [(0, 0), (0, 1), (1, 0), (1, 1)]

            def stage_a(i):
                g, dr = its[i]
                pb = big[dr]
                pbv = V(pb.ap.rearrange("p (a b) -> p a b", a=8), (pb.key,))
                for h8 in range(8):
                    hd = dr * 16 + g * 8 + h8
                    k.mm(V(pb.ap[:, h8 * 128:(h8 + 1) * 128], (pb.key,)), V(dta.ap[:, tt, hd:hd + 1].broadcast_to([128, 128]), (dta.key,)),
                         Tdir[dr], start=True, stop=False)
                    k.mm(V(pb.ap[:, h8 * 128:(h8 + 1) * 128], (pb.key,)), identb, maskb[dr], start=False, stop=True)
                hd0 = dr * 16 + g * 8
                for bk in range(2):
                    k.tt(segT[:, bk * 4:(bk + 1) * 4, :], V(pbv.ap[:, bk * 4:(bk + 1) * 4, :], pbv.keys),
                         V(Scol.ap[:, tt, hd0 + bk * 4:hd0 + bk * 4 + 4].unsqueeze(2).broadcast_to([128, 4, 128]), (Scol.key,)), ALU.subtract)
                k.act(LTs[i % 2].v(), segT.v(), AF.Exp)

            def stage_b(i):
                g, dr = its[i]
                k.tt(MT[g][dr].v(), LTs[i % 2].v(), V(CBT.ap[:, g, :].unsqueeze(1).broadcast_to([128, 8, 128]), (CBT.key,)), ALU.mult, eng="dve")
                if DB and g == 0:
                    dbg("MT%d" % dr, V(MT[g][dr].ap.rearrange("p a b -> p (a b)"), (MT[g][dr].key,)), [128, 1024])

            stage_a(0)
            for i in range(4):
                if i + 1 < 4:
                    stage_a(i + 1)
                stage_b(i)
                g, dr = its[i]
                if dr == 0 or YS < 4:
                    continue
                for h8 in range(8):
                    hsl = slice((g * 8 + h8) * 64, (g * 8 + h8 + 1) * 64)
                    yv = V(B2.ap[:, hsl], (B2.key,))
                    k.mm(yv, identb, xD[:, hsl], start=True, stop=False)
                    k.mm(yv, MT[g][0][:, h8, :], xdt[0][:, hsl], start=False, stop=False)
                    k.mm(yv, MT[g][1][:, h8, :], xdt[1][:, hsl], start=False, stop=True)
            if YS < 5:
                return
            for bk in range(2):
                k.act(ysb[:, bk * 512:(bk + 1) * 512], B2[:, bk * 512:(bk + 1) * 512], AF.Copy)
            if YS < 6:
                return
            if DB:
                dbg("ydiag", ysb.v(), [128, 1024])
            for dr in range(2):
                if ent[dr] is None:
                    continue
                for g in range(2):
                    k.mm(V(B3.ap[:, g * 512:(g + 1) * 512], (B3.key,)), xbc[9][g * 64:(g + 1) * 64, tl], ent[dr][g * 64:(g + 1) * 64, :])
                for bk in range(2):
                    k.tt(V(tmpy.ap[:, bk * 512:(bk + 1) * 512].rearrange("p (a b) -> p a b", a=8), (tmpy.key,)),
                         V(B3.ap[:, bk * 512:(bk + 1) * 512].rearrange("p (a b) -> p a b", a=8), (B3.key,)),
                         V(eU.ap[:, tt, dr * 16 + bk * 8:dr * 16 + bk * 8 + 8].unsqueeze(2).broadcast_to([128, 8, 64]), (eU.key,)), ALU.mult)
                k.tt(ysb.v(), ysb.v(), tmpy.v(), ALU.add, eng="dve")
            if DB:
                dbg("ysb", ysb.v(), [128, 1024])
            if YS < 7:
                return
            k.tt(ysb.v(), ysb.v(), sz[tt].v(), ALU.mult)
            k.memset(ssq[:, 0:1], 0.0)
            k.act(junk.v(), ysb.v(), AF.Square, accum=ssq[:, 0:1])
            k.act(ssq[:, 1:2], ssq[:, 0:1], AF.Ln, bias=EPS, scale=1.0 / 1024)
            k.act(ssq[:, 1:2], ssq[:, 1:2], AF.Exp, scale=-0.5)
            k.act(gn.v(), ysb.v(), AF.Copy, scale=ssq[:, 1:2])
            for c in range(8):
                k.tr(V(P3bf.ap[:, c * 128:(c + 1) * 128], (B3.key,)), gn[:, c * 128:(c + 1) * 128], identb)
            k.tt(V(yo.ap[:, 0:8, tok], (yo.key,)), V(P3bf.ap[:, 0:1024].rearrange("p (a b) -> p a b", a=8), (B3.key,)),
                 V(cols.ap[:, ec["ng"]:ec["ng"] + 8].unsqueeze(2).broadcast_to([128, 8, 128]), (cols.key,)), ALU.mult)

        def write_state(Ht, dst):
            if SSD_LEVEL < 2:
                return
            for pr in range(4):
                k.tr(V(B3.ap[:, pr * 128:(pr + 1) * 128], (B3.key,)), Ht[:, pr * 128:(pr + 1) * 128], ident)
            k.copy(V(stF.ap.rearrange("p a b c -> p (a b c)"), (stF.key,)), V(B3.ap[:, 0:512], (B3.key,)))
            dv_ = dst.rearrange("(g pr h2) p n -> g (h2 p) pr n", g=2, pr=4)
            for g in range(2):
                k.dma(dv_[g], V(stF.ap[:, :, g, :], (stF.key,)), chan="st")

        pstates = [V(B3.ap[:, 0:512], (B3.key,)), V(B3.ap[:, 512:1024], (B3.key,))]
        SKIP_SSD = bool(os.environ.get("SKIP_SSD")); SKIP_ATT = bool(os.environ.get("SKIP_ATT"))
        if SKIP_SSD:
            pass
        elif not isB:
            for s in range(nseq):
                group_prep(s, True)
                chunk_states(0, 0, pstates[0]); state_step(Hs[0], 0, 0, pstates[0], False)
                k.copy(Hb16[0][1].v(), Hs[0].v(), eng="act")
                chunk_states(1, 1, pstates[1]); state_step(Hs[1], 1, 1, pstates[1], False)
                k.copy(Hb16[1][0].v(), Hs[1].v(), eng="act")
                chunk_states(1, 0, pstates[0]); state_step(Hs[0], 1, 0, pstates[0], True)
                write_state(Hs[0], sf_out[s, j])
                chunk_states(0, 1, pstates[1]); state_step(Hs[1], 0, 1, pstates[1], True)
                write_state(Hs[1], sb_out[s, j])
                chunk_y(s, 0, [None, Hb16[1][0]])
                chunk_y(s, 1, [Hb16[0][1], None])
        else:
            for dr, src in enumerate((ssd_f0, ssd_b0)):
                sv_ = src[j].rearrange("(g pr h2) p n -> g (h2 p) pr n", g=2, pr=4)
                for g in range(2):
                    k.dma(V(stF.ap[:, :, g, :], (stF.key,)), sv_[g], chan="ld")
                for pr in range(4):
                    k.tr(V(B3.ap[:, pr * 128:(pr + 1) * 128], (B3.key,)), V(stF.ap[:, pr, :, :].rearrange("p a b -> p (a b)"), (stF.key,)), ident)
                k.copy(Hs[dr].v(), V(B3.ap[:, 0:512], (B3.key,)))
            for gi in (3, 2, 1, 0):
                group_prep(gi, False)
                for tt in (1, 0):
                    k.copy(Hent_b[gi * 2 + tt].v(), Hs[1].v(), eng="act")
                    if gi * 2 + tt > 0:
                        chunk_states(tt, 1, pstates[tt]); state_step(Hs[1], tt, 1, pstates[tt], True)
            for gi in range(4):
                group_prep(gi, True)
                for tt in range(2):
                    k.copy(Hb16[0][tt].v(), Hs[0].v(), eng="act")
                    if gi * 2 + tt < 7:
                        chunk_states(tt, 0, pstates[tt]); state_step(Hs[0], tt, 0, pstates[tt], True)
                for tt in range(2):
                    chunk_y(gi, tt, [Hb16[0][tt], Hent_b[gi * 2 + tt]])
        if not SKIP_SSD:
            out_proj(0)
        arena_reset(ms)

        nkt = 12 if isB else 8
        nk = nkt * 128
        koff = 4 if isB else 0
        qT = [alloc([128, 1024], BF16, "qT") for _ in range(4)]
        kT = [alloc([128, nk], BF16, "kT") for _ in range(4)]
        vaug = alloc([128, nkt, 4, 130], BF16, "vaug")
        PT = [alloc([128, 512], BF16, "PT") for _ in range(3)]
        raw = [alloc([128, 512], BF16, "raw") for _ in range(2)]
        t1 = alloc([128, 512], F32, "t1"); t2 = alloc([128, 512], F32, "t2")
        ost = alloc([128, 512], F32, "ost")
        o_t = alloc([128, 4, 128], F32, "o_t"); o_n = alloc([128, 4, 128], BF16, "o_n")
        ojunk = Tile(t1.ap.rearrange("p (a b) -> p a b", a=4), t1.key)
        o_ns = [o_n, Tile(t2.ap.bitcast(BF16)[:, 0:512].rearrange("p (a b) -> p a b", a=4), t2.key)]
        Osb = [alloc([128, 4, 130], F32, "Osb") for _ in range(2)]
        rs = alloc([128, 2, 4], F32, "rs"); rs2 = alloc([128, 2, 4], F32, "rs2"); sso = alloc([128, 2, 4], F32, "sso")
        if isB:
            rope = alloc([128, 2, 1024], F32, "rope")
            k.dma(rope.v(), ropetab.rearrange("a p n -> p a n"), chan="ld")
            ckst = alloc([128, 4, 512], F32, "ckst")
        Sbank = [Tile(B0.ap[:, 0:512], "B0_lo"), Tile(B0.ap[:, 512:1024], "B0_hi")]
        for hg in range(0 if SKIP_ATT else 2):
            specs = [([(w_in_e[j][:, c0 + hg * 512:c0 + (hg + 1) * 512], 0, 512)], 8, 512) for c0 in (C_Q0, C_K0, C_V0)]
            wq = WSeq(specs)
            wQ, wK, wV = wq.get(0), wq.get(1), wq.get(2)
            k.memset(V(vaug.ap[:, :, :, 128:130], (vaug.key,)), 1.0)
            if isB:
                k.dma(ckst.v(), cache_k[j][:, hg * 512:(hg + 1) * 512].rearrange("(a p) n -> p a n", p=128), chan="ld")
                for kt in range(4):
                    k.dma(V(vaug.ap[:, kt, :, 0:128], (vaug.key,)),
                          cache_v[j][kt * 128:(kt + 1) * 128, hg * 512:(hg + 1) * 512].rearrange("p (a b) -> p a b", a=4), chan="cv", q="pool")
                    for hh in range(4):
                        k.tr(V(B3.ap[:, hh * 128:(hh + 1) * 128], (B3.key,)), ckst[:, kt, hh * 128:(hh + 1) * 128], ident)
                    for hh in range(4):
                        k.copy(kT[hh][:, kt * 128:(kt + 1) * 128], V(B3.ap[:, hh * 128:(hh + 1) * 128], (B3.key,)), eng="act")
            for which, (wt, dstT, doff) in enumerate(((wQ, qT, 0), (wK, kT, koff * 128))):
                for hh in range(4):
                    for blk in range(2):
                        pt = Sbank[blk].v()
                        for kc in range(8):
                            k.mm(pt, wt[:, kc, hh * 128:(hh + 1) * 128], h[:, kc, blk * 512:(blk + 1) * 512], start=(kc == 0), stop=(kc == 7))
                        dst = dstT[hh][:, doff + blk * 512:doff + (blk + 1) * 512]
                        if not isB:
                            k.act(dst, pt, AF.Copy)
                        else:
                            rw = raw[blk]
                            k.act(rw.v(), pt, AF.Copy)
                            p2 = V(B1.ap[:, blk * 512:(blk + 1) * 512], (B1.key,))
                            k.mm(p2, Rb, rw.v())
                            k.tt(t1.v(), rw.v(), rope[:, 0, blk * 512:(blk + 1) * 512], ALU.mult, eng="dve")
                            k.tt(t2.v(), p2, rope[:, 1, blk * 512:(blk + 1) * 512], ALU.mult)
                            k.tt(dst, t1.v(), t2.v(), ALU.add)
            for t in range(8):
                pt = V(B2.ap[:, (t % 2) * 512:(t % 2) * 512 + 512], (B2.key,))
                for kc in range(8):
                    k.mm(pt, h[:, kc, t * 128:(t + 1) * 128], wV[:, kc, :], start=(kc == 0), stop=(kc == 7))
                k.act(V(vaug.ap[:, koff + t, :, 0:128], (vaug.key,)), V(pt.ap.rearrange("p (a b) -> p a b", a=4), pt.keys), AF.Copy)
                if not isB:
                    s, tl = t // 2, (t % 2) * 128
                    k.copy(ost.v(), pt, eng="act")
                    k.dma(nv_out[s, j, tl:tl + 128, hg * 512:(hg + 1) * 512], ost.v(), chan="stv")
                    pk = V(B3.ap[:, (t % 2) * 512:(t % 2) * 512 + 512], (B3.key,))
                    for kc in range(8):
                        k.mm(pk, h[:, kc, t * 128:(t + 1) * 128], wK[:, kc, :], start=(kc == 0), stop=(kc == 7))
                    k.copy(ost.v(), pk)
                    k.dma(nk_out[s, j, tl:tl + 128, hg * 512:(hg + 1) * 512], ost.v(), chan="stk")
            if isB:
                qblocks = [(0, 512, list(range(12))), (512, 512, list(range(12)))]
            else:
                qblocks = [(s * 256, 256, [2 * s, 2 * s + 1]) for s in range(4)]
            pti = 0
            oslots = [V(B1.ap[:, 0:129], (B1.key,)), V(B1.ap[:, 512:641], (B1.key,)),
                      V(B2.ap[:, 0:129], (B2.key,)), V(B2.ap[:, 512:641], (B2.key,))]
            pend2 = []
            bi = 0
            for hh in range(4):
                for (q0, nq, kts) in qblocks:
                    nqt = nq // 128
                    for c in range(2):
                        def s_mm(ki, kt):
                            S = Sbank[ki % 2]
                            k.mm(V(S.ap[:, 0:nq], (S.key,)), kT[hh][c * 64:(c + 1) * 64, kt * 128:(kt + 1) * 128], qT[hh][c * 64:(c + 1) * 64, q0:q0 + nq])
                        s_mm(0, kts[0])
                        for ki, kt in enumerate(kts):
                            S = Sbank[ki % 2]
                            pt_ = PT[pti % 3]; pti += 1
                            k.act(pt_[:, 0:nq], V(S.ap[:, 0:nq], (S.key,)), AF.Exp, scale=ATT_SCALE)
                            if ki + 1 < len(kts):
                                s_mm(ki + 1, kts[ki + 1])
                            for qt in range(nqt):
                                k.mm(oslots[qt], pt_[:, qt * 128:(qt + 1) * 128],
                                     V(vaug.ap[:, kt, hh, 0:129], (vaug.key,)), start=(ki == 0), stop=(ki == len(kts) - 1))
                        for qt in range(nqt):
                            k.copy(Osb[c][:, qt, 0:129], oslots[qt])
                    if pend2:
                        pend2.pop()()
                    o_n = o_ns[bi % 2]; bi += 1
                    for c in range(2):
                        k.recip(rs[:, c, 0:nqt], V(Osb[c].ap[:, 0:nqt, 128], (Osb[c].key,)))
                    k.act(rs2[:, 0, 0:nqt], rs[:, 0, 0:nqt], AF.Copy)
                    k.act(rs2[:, 1, 0:nqt], rs[:, 1, 0:nqt], AF.Copy, scale=ccol("nlam"))
                    o3 = V(o_t.ap[:, 0:nqt, :], (o_t.key,))
                    k.tt(o3, Osb[0][:, 0:nqt, 0:128], V(rs2.ap[:, 0, 0:nqt].unsqueeze(2).broadcast_to([128, nqt, 128]), (rs2.key,)), ALU.mult)
                    k.tt(V(ojunk.ap[:, 0:nqt, :], (ojunk.key,)), Osb[1][:, 0:nqt, 0:128],
                         V(rs2.ap[:, 1, 0:nqt].unsqueeze(2).broadcast_to([128, nqt, 128]), (rs2.key,)), ALU.mult)
                    k.tt(o3, o3, V(ojunk.ap[:, 0:nqt, :], (ojunk.key,)), ALU.add)
                    k.tt(V(ojunk.ap[:, 0:nqt, :], (ojunk.key,)), o3, o3, ALU.mult)
                    k.op("dve", (lambda nqt=nqt: nc.vector.reduce_sum(out=sso.ap[:, 0, 0:nqt], in_=ojunk.ap[:, 0:nqt, :], axis=mybir.AxisListType.X)),
                         reads=[ojunk.key], writes=[sso.key], osize=nqt)
                    k.act(sso[:, 1, 0:nqt], sso[:, 0, 0:nqt], AF.Ln, bias=EPS, scale=1.0 / 128)
                    k.act(sso[:, 1, 0:nqt], sso[:, 1, 0:nqt], AF.Exp, scale=-0.5)
                    k.tt(V(o_n.ap[:, 0:nqt, :], (o_n.key,)), o3, V(sso.ap[:, 1, 0:nqt].unsqueeze(2).broadcast_to([128, nqt, 128]), (sso.key,)), ALU.mult)
                    def part2(o_n=o_n, nqt=nqt, nq=nq, q0=q0, hh=hh):
                        for qt in range(nqt):
                            k.tr(V(P3bf.ap[:, qt * 128:(qt + 1) * 128], (B3.key,)), o_n[:, qt, :], identb)
                        k.ts(V(yo.ap[:, hg * 4 + hh, q0:q0 + nq].rearrange("p (a b) -> p a b", a=nqt), (yo.key,)),
                             V(P3bf.ap[:, 0:nq].rearrange("p (a b) -> p a b", a=nqt), (B3.key,)), ccol("sgl"), None, ALU.mult)
                    pend2.append(part2)
            pend2.pop()()
        if not SKIP_ATT:
            out_proj(8, pts=[Sbank[0].v(), Sbank[1].v()])
        arena_reset(m0)

    def phase_odd(l, P, pc):
        j = l // 2
        m0 = arena_mark()
        h = alloc([128, 8, 1024], BF16, "h")
        phase_norm(l, gcols_mix[l], 0, 1, P, h)
        nseq, L = P["nseq"], P["L"]
        gg = [alloc([128, 1024], BF16, "gg") for _ in range(8)]
        xr = [alloc([128, 1024], BF16, "xr") for _ in range(8)]
        stg = alloc([128, nseq, L + 3], F32, "stg")
        k.memset(stg.v(), 0.0)
        bd = alloc([128, 4, 8, 128], BF16, "bd")
        k.memset(bd.v(), 0.0)
        for g, (src, dr) in enumerate(((lru_wa, 0), (lru_wx, 0), (lru_wa, 1), (lru_wx, 1))):
            sv = src[j, dr].rearrange("(c two) kk jj -> two kk c jj", two=2)
            for half in range(2):
                k.dma(V(bd.ap[half * 64:(half + 1) * 64, g, :, half * 64:(half + 1) * 64], (bd.key,)), sv[half], chan="bd", q="pool")
        tAll = alloc([128, 1024], F32, "tAll")
        tA = [Tile(tAll.ap[:, i * 512:(i + 1) * 512], tAll.key) for i in range(2)]
        specs = [([(lru_w_in[j][:, g * 512:(g + 1) * 512], 0, 512)], 8, 512) for g in range(4)]
        specs += [([(lru_w_out[j][:, g * 512:(g + 1) * 512], 0, 512)], 8, 512) for g in range(2)]
        wq = WSeq(specs)
        acc = Tile(tAll.ap.rearrange("p (a b) -> p a b", a=nseq), tAll.key)
        for c in range(8):
            w = wq.get(c // 4)
            for blk in range(2):
                pt = ps[blk]
                for kc in range(8):
                    k.mm(pt.v(), w[:, kc, (c % 4) * 128:(c % 4 + 1) * 128], h[:, kc, blk * 512:(blk + 1) * 512], start=(kc == 0), stop=(kc == 7))
                a = tA[blk]
                k.act(a.v(), pt.v(), AF.Square)
                k.ts(a.v(), a.v(), 0.044715, 1.0, ALU.mult, ALU.add)
                k.tt(a.v(), a.v(), pt.v(), ALU.mult)
                k.act(a.v(), a.v(), AF.Sigmoid, scale=2.0 * 0.7978845608028654)
                k.tt(gg[c][:, blk * 512:(blk + 1) * 512], a.v(), pt.v(), ALU.mult)
        for c in range(8):
            w = wq.get(2 + c // 4)
            pl = [ps[2], ps[3]]
            for blk in range(2):
                for kc in range(8):
                    k.mm(pl[blk].v(), w[:, kc, (c % 4) * 128:(c % 4 + 1) * 128], h[:, kc, blk * 512:(blk + 1) * 512], start=(kc == 0), stop=(kc == 7))
            conv_fm(pl, P, stg, 4, 2, pc["cw"] + c, 8, pc["cb"] + c, acc)
            k.copy(V(xr[c].ap.rearrange("p (a b) -> p a b", a=nseq), (xr[c].key,)), acc.v())
            if c == 0 and l == 1:
                dbg("xr0_%d" % P["v"], xr[0].v(), [128, 1024])
                dbg("gg0_%d" % P["v"], gg[0].v(), [128, 1024])
                dbg("h0_%d" % P["v"], h[:, 0, :], [128, 1024])
        rr = alloc([128, 1024], F32, "rr"); ii = alloc([128, 1024], F32, "ii")
        aa = alloc([128, 1024], F32, "aa"); uu = alloc([128, 1024], F32, "uu")
        hhb = [[alloc([128, 1024], F32, "hh") for _ in range(2)] for _ in range(2)]
        lst = alloc([128, 8, 2, NP_SEQ], F32, "lst")
        h0 = alloc([128, 2, 8], F32, "h0")
        USE_H0 = (P["v"] == 1) and not os.environ.get("NOH0")
        if USE_H0:
            for dr, src in enumerate((lru_f0, lru_b0)):
                k.dma(stage[0:8, :], rows(src[j], 8), chan="ld")
                k.tr(ps[7][:, 0:8], stage[0:8, :], V(cst.ap[0:8, 0, 0:8], (cst.key,)))
                k.copy(h0[:, dr, :], ps[7][:, 0:8])
            if debug:
                od = nc.dram_tensor("dbg_h0s", [128, 16], F32, kind="ExternalOutput").ap()
                k.dma(od, V(h0.ap.rearrange("p a b -> p (a b)"), (h0.key,)), chan="dbg")
        aaD = [aa, Tile(stg.ap.rearrange("p a b -> p (a b)")[:, 0:1024], stg.key)]
        uuD = [uu, Tile(tAll.ap, tAll.key)]

        def finish(c):
            hh = hhb[c % 2]
            k.tt(rr.v(), hh[0].v(), hh[1].v(), ALU.add)
            if P["v"] == 0:
                for s_ in range(nseq):
                    for dr_, col_ in ((0, (s_ + 1) * L - 1), (1, s_ * L)):
                        o_ap = lst.ap[:, c, dr_, s_:s_ + 1]
                        i_ap = hh[dr_].ap[:, col_:col_ + 1]
                        k.op("act", (lambda o_ap=o_ap, i_ap=i_ap: nc.scalar.activation(out=o_ap, in_=i_ap, func=AF.Copy)),
                             reads=[hh[dr_].key, rr.key], writes=[lst.key], osize=1)
            k.tt(h[:, c, :], rr.v(), gg[c].v(), ALU.mult)

        for c in range(8):
            hh = hhb[c % 2]
            for dr in range(2):
                for gi, dst in ((0, rr), (1, ii)):
                    g = dr * 2 + gi
                    bcolx = (pc["ba"] if gi == 0 else pc["bx"]) + dr * 8 + c
                    for blk in range(2):
                        pt = ps[(g * 2 + blk) % 4]
                        k.mm(pt.v(), V(bd.ap[:, g, c, :], (bd.key,)), xr[c][:, blk * 512:(blk + 1) * 512])
                        k.act(dst[:, blk * 512:(blk + 1) * 512], pt.v(), AF.Sigmoid, bias=cols[:, bcolx:bcolx + 1])
                lc = pc["nc8"] + dr * 8 + c
                k.act(aaD[dr].v(), rr.v(), AF.Exp, scale=cols[:, lc:lc + 1])
                k.act(uuD[dr].v(), aaD[dr].v(), AF.Square)
                k.act(uuD[dr].v(), uuD[dr].v(), AF.Identity, bias=1.0, scale=-1.0)
                k.act(uuD[dr].v(), uuD[dr].v(), AF.Sqrt)
                k.tt(uuD[dr].v(), uuD[dr].v(), ii.v(), ALU.mult)
                k.tt(uuD[dr].v(), uuD[dr].v(), xr[c].v(), ALU.mult)
                if c == 0 and l == 1 and dr == 0:
                    dbg("rr_%d" % P["v"], rr.v(), [128, 1024]); dbg("ii_%d" % P["v"], ii.v(), [128, 1024])
                    dbg("aa_%d" % P["v"], aaD[dr].v(), [128, 1024]); dbg("uu_%d" % P["v"], uuD[dr].v(), [128, 1024])
                for s in range(nseq):
                    sl = slice(s * L, (s + 1) * L)
                    first = s * L if dr == 0 else (s + 1) * L - 1
                    if USE_H0:
                        k.act(uuD[dr][:, first:first + 1], aaD[dr][:, first:first + 1], AF.Identity, bias=uuD[dr][:, first:first + 1], scale=h0[:, dr, c:c + 1])
                    if dr == 0:
                        k.scan(hh[0][:, sl], aaD[dr][:, sl], uuD[dr][:, sl], 0.0)
                    else:
                        rs = slice((s + 1) * L - 1, s * L - 1 if s > 0 else None, -1)
                        k.scan(V(hh[1].ap[:, rs], (hh[1].key,)), V(aaD[dr].ap[:, rs], (aaD[dr].key,)), V(uuD[dr].ap[:, rs], (uuD[dr].key,)), 0.0)
            if c >= 1:
                finish(c - 1)
        finish(7)
        if P["v"] == 0:
            k.tr(ps[7][0:64, 0:128], V(lst.ap.rearrange("p a b c -> p (a b c)"), (lst.key,)), ident)
            lrow = alloc([64, 128], F32, "lrow")
            k.copy(lrow.v(), ps[7][0:64, 0:128])
            if debug:
                od = nc.dram_tensor("dbg_lst", [128, 64], F32, kind="ExternalOutput").ap()
                k.dma(od, V(lst.ap.rearrange("p a b c -> p (a b c)"), (lst.key,)), chan="dbg")
                od2 = nc.dram_tensor("dbg_lrow", [64, 128], F32, kind="ExternalOutput").ap()
                k.dma(od2, lrow.v(), chan="dbg")
            for dr, dst in enumerate((lf_out, lb_out)):
                for c in range(8):
                    r0 = c * 8 + dr * 4
                    k.dma(dst[:, j, c * 128:(c + 1) * 128], lrow[r0:r0 + 4, :], chan="st")
        if l == 1:
            dbg("yo0_%d" % P["v"], h[:, 0, :], [128, 1024])
            dbg("yo5_%d" % P["v"], h[:, 5, :], [128, 1024])
        for d in range(8):
            w = wq.get(4 + d // 4)
            for blk in range(2):
                pt = ps[4 + (d * 2 + blk) % 2]
                for kc in range(8):
                    k.mm(pt.v(), w[:, kc, (d % 4) * 128:(d % 4 + 1) * 128], h[:, kc, blk * 512:(blk + 1) * 512], start=(kc == 0), stop=(kc == 7))
                resid_add(l, 2, P, d, blk, pt)
        if l == 1:
            dbg("xm0_%d" % P["v"], x[:, 0, P["t0"]:P["t0"] + 1024], [128, 1024])
        arena_reset(m0)

    gcols_mix, gcols_ffn, fcw, fcb = [], [], [], []
    ocols = {}
    ecols = {}
    for l in range(nlayers):
        gcols_mix.append(load_cols(rows(norm_mix_g[l], 8), 8))
        gcols_ffn.append(load_cols(rows(norm_ffn_g[l], 8), 8))
        fcw.append(load_cols(ffn_conv_w[l].rearrange("w (r c) -> (w r) c", c=128), 3 * 2 * NJ))
        fcb.append(load_cols(rows(ffn_conv_b[l], 2 * NJ), 2 * NJ))
        if l % 2 == 0:
            j = l // 2
            ec = {}
            ec["cw"] = load_cols(conv_w_e[j].rearrange("w (r c) -> (w r) c", c=128), 40)
            ec["cb"] = load_cols(rows(conv_b_e[j], 10), 10)
            ec["ng"] = load_cols(rows(ssd_norm_g[j], 8), 8)
            dn = load_cols(rows(diff_norm_g[j], 1), 1)
            base = colstate["n"]
            colstate["n"] += 32 + 32 + 16 + 256 + 32 + 8
            assert colstate["n"] <= NCOL
            ec["dtb"], alg, ec["d"], lp = base, base + 32, base + 64, base + 80
            ec["abc"] = base + 336
            sc = base + 368
            k.dma(cols[:, ec["dtb"]:ec["dtb"] + 32], dt_bias[j:j + 1, :].partition_broadcast(128), chan="ld")
            k.dma(cols[:, alg:alg + 32], a_log[j:j + 1, :].partition_broadcast(128), chan="ld")
            k.dma(cols[:, ec["d"]:ec["d"] + 16], ssd_d[j:j + 1, :].partition_broadcast(128), chan="ld")
            k.dma(cols[:, lp:lp + 256], diff_lambda[j:j + 1, :].partition_broadcast(128), chan="ld")
            k.act(cols[:, ec["abc"]:ec["abc"] + 32], cols[:, alg:alg + 32], AF.Exp)
            k.ts(cols[:, ec["abc"]:ec["abc"] + 32], cols[:, ec["abc"]:ec["abc"] + 32], -1.0, None, ALU.mult)
            lam_init = 0.8 - 0.6 * math.exp(-0.3 * l)
            k.tt(cols[:, lp:lp + 64], cols[:, lp:lp + 64], cols[:, lp + 64:lp + 128], ALU.mult)
            k.tt(cols[:, lp + 128:lp + 192], cols[:, lp + 128:lp + 192], cols[:, lp + 192:lp + 256], ALU.mult)
            k.memset(cols[:, sc:sc + 2], 0.0)
            k.act(cols[:, lp + 64:lp + 128], cols[:, lp:lp + 64], AF.Copy, accum=cols[:, sc:sc + 1])
            k.act(cols[:, lp + 192:lp + 256], cols[:, lp + 128:lp + 192], AF.Copy, accum=cols[:, sc + 1:sc + 2])
            k.act(cols[:, sc + 2:sc + 4], cols[:, sc:sc + 2], AF.Exp)
            k.tt(cols[:, sc + 4:sc + 5], cols[:, sc + 2:sc + 3], cols[:, sc + 3:sc + 4], ALU.subtract)
            k.act(cols[:, sc + 5:sc + 6], cols[:, sc + 4:sc + 5], AF.Identity, bias=-lam_init, scale=-1.0)
            k.act(cols[:, sc + 6:sc + 7], cols[:, dn:dn + 1], AF.Copy, scale=(1.0 - lam_init))
            ec["nlam"], ec["sgl"] = sc + 5, sc + 6
            ecols[l] = ec
        if l % 2 == 1:
            j = l // 2
            pc = {}
            pc["cw"] = load_cols(lru_conv_w[j].rearrange("w (r c) -> (w r) c", c=128), 32)
            pc["cb"] = load_cols(rows(lru_conv_b[j], 8), 8)
            pc["ba"] = load_cols(lru_ba[j].rearrange("d (r c) -> (d r) c", c=128), 16)
            pc["bx"] = load_cols(lru_bx[j].rearrange("d (r c) -> (d r) c", c=128), 16)
            lam = load_cols(lru_lambda[j].rearrange("d (r c) -> (d r) c", c=128), 16)
            pc["nc8"] = colstate["n"]
            colstate["n"] += 16
            dst = cols[:, pc["nc8"]:pc["nc8"] + 16]
            k.act(dst, cols[:, lam:lam + 16], AF.Exp, scale=-1.0)
            k.act(dst, dst, AF.Ln, bias=1.0)
            k.ts(dst, dst, -8.0, None, ALU.mult)
            ocols[l] = pc
    gfin = load_cols(rows(final_norm_g, 8), 8)

    for l in range(nlayers):
        for P in PASSES:
            if l % 2 == 1:
                phase_odd(l, P, ocols[l])
            else:
                if not os.environ.get("DISABLE_EVEN"):
                    phase_even(l, P, ecols[l])
            phase_ffn(l, P, fcw[l], fcb[l])

    for P in PASSES:
        m0 = arena_mark()
        hf = alloc([128, 8, 1024], F32, "hf")
        phase_norm(0, gfin, 0, 0, P, None, final=True, hf=hf)
        ost = [alloc([128, D], F32, "ost") for _ in range(2)]
        for t in range(8):
            o = ost[t % 2]
            for half in range(2):
                pt = ps[2 + half]
                for c4 in range(4):
                    c = half * 4 + c4
                    k.tr(pt[:, c4 * 128:(c4 + 1) * 128], hf[:, c, t * 128:(t + 1) * 128], ident)
                k.copy(o[:, half * 512:(half + 1) * 512], pt.v(), eng=("act" if half else "dve"))
            k.dma(y_out[P["t0"] + t * 128:P["t0"] + (t + 1) * 128, :], o.v(), chan="st")
        arena_reset(m0)

    k.emit()
    return nc, es


def host_consts():
    c = np.zeros((10, 128, 128), np.float32)
    c[0] = np.eye(128)
    R = np.zeros((128, 128), np.float32)
    for m in range(128):
        d = m % 32
        if d < 16:
            R[m + 16, m] = -1.0
        else:
            R[m - 16, m] = 1.0
    c[1] = R
    jj, qq = np.meshgrid(np.arange(128), np.arange(128), indexing="ij")
    c[2] = (jj <= qq)
    c[3] = (jj >= qq)
    c[4] = np.where(qq >= jj, 0.0, -30000.0)
    c[5] = np.where(qq <= jj, 0.0, -30000.0)
    c[6] = 1.0
    t = np.arange(LS)
    row = (t // 64).astype(np.float32)
    col = (t % 64).astype(np.float32)
    freqs = (10000.0 ** (-np.arange(0, 32, 2, dtype=np.float32) / 32.0)).astype(np.float32)
    tab = np.zeros((2, 128, LS), np.float32)
    for p in range(128):
        d = p % 64
        pos = row if d < 32 else col
        f = freqs[(d % 32) % 16]
        ang = (pos * f).astype(np.float32)
        tab[0, p] = np.cos(ang)
        tab[1, p] = np.sin(ang)
    return c, tab


_CACHE = {}


def make_in_maps(inputs):
    consts, tab = host_consts()
    maps = []
    for i in range(8):
        m = {}
        m["xin"] = np.ascontiguousarray(np.concatenate(
            [inputs["x_prompt"][4 * i:4 * i + 4].reshape(4 * LP, D), inputs["x_sample"][i]], axis=0))
        m["cvec"] = np.ascontiguousarray(np.stack([inputs["c_ctx"], inputs["c"][i]], axis=0))
        m["cache_k"] = np.ascontiguousarray(inputs["cache_attn_k"][i].reshape(2, PAST, D))
        m["cache_v"] = np.ascontiguousarray(inputs["cache_attn_v"][i].reshape(2, PAST, D))
        m["ssd_f0"] = np.ascontiguousarray(inputs["state_ssd_fwd"][i])
        m["ssd_b0"] = np.ascontiguousarray(inputs["state_ssd_bwd"][i])
        m["lru_f0"] = np.ascontiguousarray(inputs["state_lru_fwd"][i])
        m["lru_b0"] = np.ascontiguousarray(inputs["state_lru_bwd"][i])
        for nm in ("w_mod", "b_mod", "norm_mix_g", "norm_ffn_g", "ssd_attn_w_in", "ssd_conv_w", "ssd_conv_b",
                   "ssd_norm_g", "diff_norm_g", "ssd_attn_w_out", "lru_w_in", "lru_conv_w", "lru_conv_b", "lru_wa",
                   "lru_ba", "lru_wx", "lru_bx", "lru_lambda", "lru_w_out", "ffn_w_up", "ffn_conv_w", "ffn_conv_b",
                   "ffn_w_down", "final_norm_g"):
            m[nm] = np.ascontiguousarray(inputs[nm])
        m["ssd_a_log"] = np.ascontiguousarray(inputs["ssd_a_log"].reshape(2, 32))
        m["ssd_dt_bias"] = np.ascontiguousarray(inputs["ssd_dt_bias"].reshape(2, 32))
        m["ssd_d"] = np.ascontiguousarray(inputs["ssd_d"])
        m["diff_lambda"] = np.ascontiguousarray(inputs["diff_lambda"].reshape(2, 256))
        m["consts"] = consts
        m["ropetab"] = tab
        maps.append(m)
    return maps


def kernel(**inputs):
    inputs = {k_: np.asarray(v, dtype=np.float32) for k_, v in inputs.items()}
    if "nc" not in _CACHE:
        _CACHE["nc"] = build_program()
    nc, _es = _CACHE["nc"]
    maps = make_in_maps(inputs)
    res = run_bass_kernel_spmd(nc, maps, core_ids=list(range(8)))
    R = res.results
    y = np.stack([r["y_out"] for r in R])
    y_prompt = y[:, :1024].reshape(32, LP, D)
    y_sample = y[:, 1024:].reshape(8, LS, D)
    nk = np.concatenate([r["nk_out"] for r in R], axis=0).reshape(32, 2, LP, 8, 2, 64)
    nv = np.concatenate([r["nv_out"] for r in R], axis=0).reshape(32, 2, LP, 8, 128)
    sf = np.concatenate([r["sf_out"] for r in R], axis=0)
    sb = np.concatenate([r["sb_out"] for r in R], axis=0)
    lf = np.concatenate([r["lf_out"] for r in R], axis=0)
    lb = np.concatenate([r["lb_out"] for r in R], axis=0)
    return (y_prompt, y_sample, nk, nv, sf, sb, lf, lb)
```

```python
import math
import os
from contextlib import ExitStack
import numpy as np
import concourse.bass as bass
import concourse.mybir as mybir
from concourse.bass_utils import run_bass_kernel_spmd

F32 = mybir.dt.float32
BF16 = mybir.dt.bfloat16
AF = mybir.ActivationFunctionType
ALU = mybir.AluOpType

D = 1024
DEPTH = 4
NP_SEQ = 4
LP = 256
LS = 1024
NTOK = 2048
PAST = 512
EPS = 1e-6
D_FF = 2816
NJ = 22
C_XBC0, C_DT0, C_Q0, C_K0, C_V0 = 1024, 2304, 2336, 3360, 4384
ATT_SCALE = 64 ** -0.5


class V:
    __slots__ = ("ap", "keys")

    def __init__(self, ap, keys):
        self.ap = ap
        self.keys = tuple(keys)


class Tile:
    def __init__(self, ap, key):
        self.ap = ap
        self.key = key

    def __getitem__(self, idx):
        return V(self.ap[idx], (self.key,))

    def v(self):
        return V(self.ap, (self.key,))


def _ap(x):
    return x.ap if isinstance(x, V) else x


class KB:
    ENGS = ("pe", "act", "dve", "pool", "sp")

    def __init__(self, nc, es):
        self.nc = nc
        self.es = es
        self.eng = {"pe": nc.tensor, "act": nc.scalar, "dve": nc.vector, "pool": nc.gpsimd, "sp": nc.sync}
        self.ops = []
        self.count = {e: 0 for e in self.ENGS}
        self.writers = {}
        self.readers = {}
        self.seen = {e: {} for e in self.ENGS}
        self.chan_n = {}
        self.milestones = {e: set() for e in self.ENGS}
        self.pending_bar = {e: {} for e in self.ENGS}
        self.osize = {e: [] for e in self.ENGS}

    def op(self, eng, fn, reads=(), writes=(), chan=None, osize=1 << 20):
        idx = self.count[eng]
        self.count[eng] += 1
        self.osize[eng].append(osize)
        need = {}

        def add(src, val):
            if src == ("e", eng) and chan is None:
                if not (val >= idx - 4 and self.osize[eng][val] < 512):
                    return
            if src[0] == "c":
                val = self.chan_n[src[1]]
            if need.get(src, -1) < val:
                need[src] = val

        for k in reads:
            for s, v in self.writers.get(k, {}).items():
                add(s, v)
        for k in writes:
            for s, v in self.writers.get(k, {}).items():
                add(s, v)
            for s, v in self.readers.get(k, {}).items():
                add(s, v)
        for s, v in self.pending_bar[eng].items():
            add(s, v)
        self.pending_bar[eng] = {}
        if chan is not None and self.chan_n.get(chan, 0) > 0:
            add(("c", chan), self.chan_n[chan])
        deps = []
        for s, v in need.items():
            if self.seen[eng].get(s, -1) >= v:
                continue
            self.seen[eng][s] = v
            deps.append((s, v))
            if s[0] == "e":
                self.milestones[s[1]].add(v)
        if chan is not None:
            self.chan_n[chan] = self.chan_n.get(chan, 0) + 1
            me, myv = ("c", chan), self.chan_n[chan]
        else:
            me, myv = ("e", eng), idx
        for k in writes:
            self.writers[k] = {me: myv}
            self.readers[k] = {}
        for k in reads:
            if k not in writes:
                self.readers.setdefault(k, {})[me] = myv
        self.ops.append((eng, idx, fn, deps, chan))

    def barrier(self):
        comp = ("pe", "act", "dve")
        for e in comp + ("sp",):
            for o in comp:
                if o != e and self.count[o] > 0:
                    self.pending_bar[e][("e", o)] = self.count[o] - 1
            for c, n in self.chan_n.items():
                if not str(c).startswith("w") and n > 0:
                    self.pending_bar[e][("c", c)] = n

    def emit(self):
        nc = self.nc
        sems = {e: self.es.enter_context(nc.semaphore("s_" + e)) for e in self.ENGS}
        csems = {c: self.es.enter_context(nc.semaphore("c_%s" % str(c))) for c in self.chan_n}
        ranks = {}
        for e in self.ENGS:
            for r, i in enumerate(sorted(self.milestones[e])):
                ranks[(e, i)] = r + 1
        for eng, idx, fn, deps, chan in self.ops:
            E = self.eng[eng]
            for s, v in deps:
                if s[0] == "e":
                    E.wait_ge(sems[s[1]], ranks[(s[1], v)])
                else:
                    E.wait_ge(csems[s[1]], 16 * v)
            ins = fn()
            if chan is not None:
                ins.then_inc(csems[chan], 16)
            elif idx in self.milestones[eng]:
                ins.then_inc(sems[eng], 1)
        for c, n in self.chan_n.items():
            nc.sync.wait_ge(csems[c], 16 * n)

    @staticmethod
    def _fs(x):
        ap = _ap(x)
        n = 1
        for d in list(ap.shape)[1:]:
            n *= int(d)
        return n

    @staticmethod
    def _keys(*xs):
        ks = []
        for x in xs:
            if isinstance(x, V):
                ks.extend(x.keys)
        return ks

    def mm(self, out, lhsT, rhs, start=True, stop=True):
        self.op("pe", lambda: self.nc.tensor.matmul(_ap(out), lhsT=_ap(lhsT), rhs=_ap(rhs), start=start, stop=stop),
                reads=self._keys(lhsT, rhs), writes=self._keys(out))

    def tr(self, out, in_, ident):
        self.op("pe", lambda: self.nc.tensor.transpose(_ap(out), _ap(in_), _ap(ident)),
                reads=self._keys(in_, ident), writes=self._keys(out))

    def act(self, out, in_, func, bias=0.0, scale=1.0, accum=None):
        def f():
            kw = {}
            if accum is not None:
                kw["accum_out"] = _ap(accum)
            return self.nc.scalar.activation(out=_ap(out), in_=_ap(in_), func=func, bias=_ap(bias), scale=_ap(scale), **kw)
        self.op("act", f, reads=self._keys(in_, bias, scale), writes=self._keys(out, accum),
                osize=(1 if accum is not None else self._fs(out)))

    def tt(self, out, in0, in1, op, eng="dve"):
        E = self.eng[eng]
        self.op(eng, lambda: E.tensor_tensor(out=_ap(out), in0=_ap(in0), in1=_ap(in1), op=op),
                reads=self._keys(in0, in1), writes=self._keys(out), osize=self._fs(out))

    def ts(self, out, in0, s1, s2, op0, op1=None, eng="dve"):
        E = self.eng[eng]
        if op1 is None:
            f = lambda: E.tensor_scalar(out=_ap(out), in0=_ap(in0), scalar1=_ap(s1), scalar2=None, op0=op0)
        else:
            f = lambda: E.tensor_scalar(out=_ap(out), in0=_ap(in0), scalar1=_ap(s1), scalar2=_ap(s2), op0=op0, op1=op1)
        self.op(eng, f, reads=self._keys(in0, s1, s2), writes=self._keys(out), osize=self._fs(out))

    def stt(self, out, in0, scalar, in1, op0, op1, eng="dve"):
        E = self.eng[eng]
        self.op(eng, lambda: E.scalar_tensor_tensor(out=_ap(out), in0=_ap(in0), scalar=_ap(scalar), in1=_ap(in1), op0=op0, op1=op1),
                reads=self._keys(in0, scalar, in1), writes=self._keys(out), osize=self._fs(out))

    def copy(self, out, in_, eng="dve"):
        if eng == "act":
            return self.act(out, in_, AF.Copy)
        E = self.eng[eng]
        self.op(eng, lambda: E.tensor_copy(out=_ap(out), in_=_ap(in_)), reads=self._keys(in_), writes=self._keys(out), osize=self._fs(out))

    def memset(self, out, val, eng="dve"):
        E = self.eng[eng]
        self.op(eng, lambda: E.memset(_ap(out), val), writes=self._keys(out), osize=self._fs(out))

    def recip(self, out, in_):
        self.op("dve", lambda: self.nc.vector.reciprocal(out=_ap(out), in_=_ap(in_)), reads=self._keys(in_), writes=self._keys(out), osize=self._fs(out))

    def scan(self, out, d0, d1, init):
        self.op("dve", lambda: self.nc.vector.tensor_tensor_scan(out=_ap(out), data0=_ap(d0), data1=_ap(d1), initial=_ap(init),
                                                                 op0=ALU.mult, op1=ALU.add),
                reads=self._keys(d0, d1, init), writes=self._keys(out))

    def dma(self, out, in_, chan, q="sp"):
        E = self.eng[q]
        self.op(q, lambda: E.dma_start(out=_ap(out), in_=_ap(in_)), reads=self._keys(in_), writes=self._keys(out), chan=chan)


def build_program(nlayers=DEPTH, debug=False):
    nc = bass.Bass("TRN2", target_bir_lowering=False)
    es = ExitStack()
    k = KB(nc, es)

    def din(name, shape):
        return nc.dram_tensor(name, list(shape), F32, kind="ExternalInput").ap()

    def dout(name, shape):
        return nc.dram_tensor(name, list(shape), F32, kind="ExternalOutput").ap()

    xin = din("xin", [NTOK, D])
    cvec = din("cvec", [2, D])
    cache_k = din("cache_k", [2, PAST, D])
    cache_v = din("cache_v", [2, PAST, D])
    ssd_f0 = din("ssd_f0", [2, 16, 64, 64])
    ssd_b0 = din("ssd_b0", [2, 16, 64, 64])
    lru_f0 = din("lru_f0", [2, D])
    lru_b0 = din("lru_b0", [2, D])
    w_mod = din("w_mod", [4, D, 6 * D]); b_mod = din("b_mod", [4, 6 * D])
    norm_mix_g = din("norm_mix_g", [4, D]); norm_ffn_g = din("norm_ffn_g", [4, D])
    w_in_e = din("ssd_attn_w_in", [2, D, 5408]); conv_w_e = din("ssd_conv_w", [2, 4, 1280]); conv_b_e = din("ssd_conv_b", [2, 1280])
    a_log = din("ssd_a_log", [2, 32]); dt_bias = din("ssd_dt_bias", [2, 32]); ssd_d = din("ssd_d", [2, 16])
    ssd_norm_g = din("ssd_norm_g", [2, D]); diff_lambda = din("diff_lambda", [2, 256]); diff_norm_g = din("diff_norm_g", [2, 128])
    w_out_e = din("ssd_attn_w_out", [2, 2048, D])
    lru_w_in = din("lru_w_in", [2, D, 2048]); lru_conv_w = din("lru_conv_w", [2, 4, D]); lru_conv_b = din("lru_conv_b", [2, D])
    lru_wa = din("lru_wa", [2, 2, 16, 64, 64]); lru_ba = din("lru_ba", [2, 2, D])
    lru_wx = din("lru_wx", [2, 2, 16, 64, 64]); lru_bx = din("lru_bx", [2, 2, D])
    lru_lambda = din("lru_lambda", [2, 2, D]); lru_w_out = din("lru_w_out", [2, D, D])
    ffn_w_up = din("ffn_w_up", [4, D, 2 * D_FF]); ffn_conv_w = din("ffn_conv_w", [4, 3, 2 * D_FF]); ffn_conv_b = din("ffn_conv_b", [4, 2 * D_FF])
    ffn_w_down = din("ffn_w_down", [4, D_FF, D]); final_norm_g = din("final_norm_g", [D])
    consts = din("consts", [10, 128, 128])
    ropetab = din("ropetab", [2, 128, LS])

    y_out = dout("y_out", [NTOK, D])
    nk_out = dout("nk_out", [NP_SEQ, 2, LP, D])
    nv_out = dout("nv_out", [NP_SEQ, 2, LP, D])
    sf_out = dout("sf_out", [NP_SEQ, 2, 16, 64, 64])
    sb_out = dout("sb_out", [NP_SEQ, 2, 16, 64, 64])
    lf_out = dout("lf_out", [NP_SEQ, 2, D])
    lb_out = dout("lb_out", [NP_SEQ, 2, D])

    dbgst = {}

    def dbg(name, v, shape):
        if not debug:
            return
        o = nc.dram_tensor("dbg_" + name, list(shape), F32, kind="ExternalOutput").ap()
        if "t" not in dbgst:
            dbgst["t"] = sbt("dbgst", [128, 1024], F32)
        st_ = dbgst["t"]
        n_ = shape[1]
        k.copy(st_[:, 0:n_], v)
        k.dma(o, st_[:, 0:n_], chan="dbg")

    def sbt(name, shape, dt):
        return Tile(es.enter_context(nc.sbuf_tensor(name, list(shape), dt))[:], name)

    x = sbt("x", [128, 8, NTOK], F32)
    cst = sbt("cst", [128, 10, 128], F32)
    cstb = sbt("cstb", [128, 10, 128], BF16)
    NCOL = 1150 if debug else 2300
    cols = sbt("cols", [128, NCOL], F32)
    mod = sbt("mod", [128, 4, 48, 2], F32)
    wslots = [sbt("wslot%d" % i, [128, 4096], BF16) for i in range(3)]
    ARENA_W = 25400
    arena = es.enter_context(nc.sbuf_tensor("arena", [128, ARENA_W], F32))[:]
    big = [Tile(es.enter_context(nc.psum_tensor("big%d" % i, [128, 1024], F32))[:], "big%d" % i) for i in range(4)]
    ps = [Tile(big[i // 2].ap[:, (i % 2) * 512:(i % 2 + 1) * 512], "ps%d" % i) for i in range(8)]
    stage = sbt("stage", [128, 128], F32)

    astate = {"off": 0, "n": 0}

    def alloc(shape, dt, name=None):
        n = 1
        for s in shape[1:]:
            n *= s
        words = n if dt == F32 else (n + 1) // 2
        o = astate["off"]
        assert o + words <= ARENA_W, ("arena overflow", o, words)
        astate["off"] = o + words
        astate["n"] += 1
        ap = arena[:, o:o + words]
        if dt != F32:
            ap = ap.bitcast(dt)[:, 0:n]
        if len(shape) == 3:
            ap = ap.rearrange("p (a b) -> p a b", a=shape[1])
        elif len(shape) == 4:
            ap = ap.rearrange("p (a b c) -> p a b c", a=shape[1], b=shape[2])
        if shape[0] != 128:
            ap = ap[0:shape[0]]
        return Tile(ap, "%s_%d" % (name or "a", astate["n"]))

    def arena_mark():
        return astate["off"]

    def arena_reset(m):
        astate["off"] = m
        k.barrier()

    wstate = {"i": 0}

    def wload(pieces, kc, cols_total):
        i = wstate["i"] % 3
        wstate["i"] += 1
        slot = wslots[i]
        view = slot.ap[:, 0:kc * cols_total].rearrange("p (a b) -> p a b", a=kc)
        for (src, off, c) in pieces:
            k.dma(V(view[:, :, off:off + c], (slot.key,)), src.rearrange("(a p) n -> p a n", p=128), chan="w%d" % i, q="pool")
        return Tile(view, slot.key)

    class WSeq:
        def __init__(self, specs, ahead=2):
            self.specs = specs
            self.tiles = {}
            self.nxt = 0
            self.ahead = ahead

        def get(self, i):
            while self.nxt < len(self.specs) and self.nxt <= i + self.ahead:
                self.tiles[self.nxt] = wload(*self.specs[self.nxt])
                self.nxt += 1
            return self.tiles[i]

    k.dma(cst.v(), consts.rearrange("a p n -> p a n"), chan="ld")
    k.copy(cstb.v(), cst.v())
    ident, identb = cst[:, 0, :], cstb[:, 0, :]
    Rb = cstb[:, 1, :]
    Tdir = [cst[:, 2, :], cst[:, 3, :]]
    maskb = [cstb[:, 4, :], cstb[:, 5, :]]
    onesb = cstb[:, 6, :]
    ones = cst[:, 6, :]

    colstate = {"n": 0}

    def load_cols(src_rows, nrows):
        off = colstate["n"]
        done = 0
        while done < nrows:
            r = min(128, nrows - done)
            k.dma(stage[0:r, :], src_rows[done:done + r, :], chan="ld")
            k.tr(ps[7][:, 0:r], stage[0:r, :], V(cst.ap[0:r, 0, 0:r], (cst.key,)))
            k.copy(cols[:, off + done:off + done + r], ps[7][:, 0:r])
            done += r
        colstate["n"] += nrows
        assert colstate["n"] <= NCOL
        return off

    def rows(ap1d_or_2d, n):
        return ap1d_or_2d.rearrange("(r c) -> r c", c=128)

    xm = arena_mark()
    xst = [alloc([128, D], F32, "xst") for _ in range(2)]
    for t in range(NTOK // 128):
        st = xst[t % 2]
        k.dma(st.v(), xin[t * 128:(t + 1) * 128, :], chan="xl%d" % (t % 2))
        for half in range(2):
            pt = ps[half]
            for c4 in range(4):
                c = half * 4 + c4
                k.tr(pt[:, c4 * 128:(c4 + 1) * 128], st[:, c * 128:(c + 1) * 128], ident)
            k.copy(V(x.ap[:, half * 4:half * 4 + 4, t * 128:(t + 1) * 128], (x.key,)),
                   V(pt.ap.rearrange("p (a b) -> p a b", a=4), (pt.key,)), eng=("act" if half else "dve"))
    arena_reset(xm)

    cm = arena_mark()
    cT = alloc([128, 16], F32, "cT")
    cTb = alloc([128, 16], BF16, "cTb")
    k.dma(stage[0:16, :], cvec.rearrange("v (c q) -> (v c) q", q=128), chan="ld")
    k.tr(ps[7][:, 0:16], stage[0:16, :], V(cst.ap[0:16, 0, 0:16], (cst.key,)))
    k.act(cT.v(), ps[7][:, 0:16], AF.Silu)
    k.copy(cTb.v(), cT.v())
    cTb3 = cTb.ap.rearrange("p (v c) -> p c v", v=2)
    for l in range(nlayers):
        bcol = load_cols(rows(b_mod[l], 48), 48)
        specs = [([(w_mod[l][:, g * 512:(g + 1) * 512], 0, 512)], 8, 512) for g in range(12)]
        wq = WSeq(specs)
        for g in range(12):
            w = wq.get(g)
            for s4 in range(4):
                ch = g * 4 + s4
                for kc in range(8):
                    k.mm(ps[6][:, ch * 2:ch * 2 + 2], w[:, kc, s4 * 128:(s4 + 1) * 128], V(cTb3[:, kc, :], (cTb.key,)),
                         start=(kc == 0), stop=(kc == 7))
        k.tt(V(mod.ap[:, l, :, :], (mod.key,)), V(ps[6].ap[:, 0:96].rearrange("p (a b) -> p a b", b=2), (ps[6].key,)),
             V(cols.ap[:, bcol:bcol + 48].unsqueeze(2).broadcast_to([128, 48, 2]), (cols.key,)), ALU.add)
    arena_reset(cm)

    PASSES = [dict(t0=0, nseq=NP_SEQ, L=LP, v=0), dict(t0=1024, nseq=1, L=LS, v=1)]

    def modcol(l, which, c, v):
        return V(mod.ap[:, l, which * 8 + c, v:v + 1], (mod.key,))

    def phase_norm(l, gcol, which_shift, which_scale, P, h, final=False, hf=None):
        m = arena_mark()
        AB = alloc([128, 8, 2], F32, "AB")
        sq = [alloc([128, 512], BF16, "sq") for _ in range(2)]
        lnv = alloc([128, 512], F32, "lnv")
        rstd = alloc([128, 512], F32, "rstd")
        tmp = [alloc([128, 512], F32, "tmp") for _ in range(2)]
        for c in range(8):
            if final:
                k.copy(AB[:, c, 0:1], cols[:, gcol + c:gcol + c + 1])
                k.memset(AB[:, c, 1:2], 0.0)
            else:
                k.ts(AB[:, c, 0:1], modcol(l, which_scale, c, P["v"]), 1.0, cols[:, gcol + c:gcol + c + 1], ALU.add, ALU.mult)
                k.copy(AB[:, c, 1:2], modcol(l, which_shift, c, P["v"]))
        for blk in range(2):
            tok = slice(P["t0"] + blk * 512, P["t0"] + blk * 512 + 512)
            for c in range(8):
                s = sq[c % 2]
                k.act(s.v(), x[:, c, tok], AF.Square)
                k.mm(ps[0].v(), onesb, s.v(), start=(c == 0), stop=(c == 7))
            k.act(lnv.v(), ps[0].v(), AF.Ln, bias=EPS, scale=1.0 / D)
            k.act(rstd.v(), lnv.v(), AF.Exp, scale=-0.5)
            for c in range(8):
                t = tmp[c % 2]
                k.tt(t.v(), x[:, c, tok], rstd.v(), ALU.mult)
                dst = (hf if final else h)[:, c, blk * 512:(blk + 1) * 512]
                k.ts(dst, t.v(), AB[:, c, 0:1], AB[:, c, 1:2], ALU.mult, ALU.add)
        astate["off"] = m

    def conv_fm(src_ps_list, P, stg, width, left, wcol0, wstride, bcol, acc):
        nseq, L = P["nseq"], P["L"]
        for blk in range(2):
            if nseq == 1:
                dst = V(stg.ap[:, 0, left + blk * 512:left + blk * 512 + 512], (stg.key,))
                src = src_ps_list[blk].v()
            else:
                dst = V(stg.ap[:, 2 * blk:2 * blk + 2, left:left + L], (stg.key,))
                src = V(src_ps_list[blk].ap.rearrange("p (a b) -> p a b", a=2), (src_ps_list[blk].key,))
            k.act(dst, src, AF.Copy)
        k.act(acc.v(), V(stg.ap[:, :, 0:L], (stg.key,)), AF.Identity, bias=cols[:, bcol:bcol + 1], scale=cols[:, wcol0:wcol0 + 1])
        for j in range(1, width):
            wc = wcol0 + j * wstride
            k.stt(acc.v(), V(stg.ap[:, :, j:j + L], (stg.key,)), cols[:, wc:wc + 1], acc.v(), ALU.mult, ALU.add)

    def resid_add(l, which_gate, P, d, blk, pst):
        tok = slice(P["t0"] + blk * 512, P["t0"] + blk * 512 + 512)
        k.stt(x[:, d, tok], pst.v(), modcol(l, which_gate, d, P["v"]), x[:, d, tok], ALU.mult, ALU.add)

    def phase_ffn(l, P, cw, cb):
        m0 = arena_mark()
        h = alloc([128, 8, 1024], BF16, "h")
        phase_norm(l, gcols_ffn[l], 3, 4, P, h)
        nseq, L = P["nseq"], P["L"]
        actT = [alloc([128, 1024], BF16, "act") for _ in range(NJ)]
        stg = [[alloc([128, nseq, L + 2], F32, "stg") for _ in range(2)] for _ in range(2)]
        accv = [alloc([128, nseq, L], F32, "accv") for _ in range(2)]
        accg = [alloc([128, nseq, L], F32, "accg") for _ in range(2)]
        for sp_ in stg:
            for s in sp_:
                k.memset(s.v(), 0.0)
        specs = []
        for jj in range(11):
            j0 = jj * 2
            specs.append(([(ffn_w_up[l][:, j0 * 128:(j0 + 2) * 128], 0, 256),
                           (ffn_w_up[l][:, D_FF + j0 * 128:D_FF + (j0 + 2) * 128], 256, 256)], 8, 512))
        for d in range(8):
            specs.append(([(ffn_w_down[l][:, d * 128:(d + 1) * 128], 0, 128)], NJ, 128))
        wq = WSeq(specs)
        for j in range(NJ):
            w = wq.get(j // 2)
            jo = (j % 2) * 128
            par = j % 2
            for half, (coff, stgt, acc) in enumerate(((jo, stg[par][0], accv[par]), (256 + jo, stg[par][1], accg[par]))):
                pl = [ps[par * 4 + half * 2], ps[par * 4 + half * 2 + 1]]
                for blk in range(2):
                    for kc in range(8):
                        k.mm(pl[blk].v(), w[:, kc, coff:coff + 128], h[:, kc, blk * 512:(blk + 1) * 512], start=(kc == 0), stop=(kc == 7))
                fcol = j + (NJ if half else 0)
                conv_fm(pl, P, stgt, 3, 1, cw + fcol, 2 * NJ, cb + fcol, acc)
            k.act(accg[par].v(), accg[par].v(), AF.Silu)
            k.tt(V(actT[j].ap.rearrange("p (a b) -> p a b", a=nseq), (actT[j].key,)), accg[par].v(), accv[par].v(), ALU.mult)
        for d in range(8):
            w = wq.get(11 + d)
            for blk in range(2):
                pt = ps[4 + (d * 2 + blk) % 2]
                for j in range(NJ):
                    k.mm(pt.v(), w[:, j, :], actT[j][:, blk * 512:(blk + 1) * 512], start=(j == 0), stop=(j == NJ - 1))
                resid_add(l, 5, P, d, blk, pt)
        arena_reset(m0)

    def phase_even(l, P, ec):
        j = l // 2
        lam_init = 0.8 - 0.6 * math.exp(-0.3 * l)
        m0 = arena_mark()
        h = alloc([128, 8, 1024], BF16, "h")
        phase_norm(l, gcols_mix[l], 0, 1, P, h)
        yo = alloc([128, 8, 1024], BF16, "yo")
        nseq, L, isB = P["nseq"], P["L"], (P["v"] == 1)
        B0, B1, B2, B3 = big
        P3bf = V(B3.ap.bitcast(BF16), (B3.key,))

        def ccol(name, i=0):
            return cols[:, ec[name] + i:ec[name] + i + 1]

        def out_proj(kbase, pts=None):
            specs = [([(w_out_e[j][kbase * 128:kbase * 128 + 1024, dg * 256:(dg + 1) * 256], 0, 256)], 8, 256) for dg in range(4)]
            wq = WSeq(specs)
            for d in range(8):
                w = wq.get(d // 2)
                for blk in range(2):
                    pt = pts[blk] if pts is not None else V(B0.ap[:, blk * 512:(blk + 1) * 512], (B0.key,))
                    for kc in range(8):
                        k.mm(pt, w[:, kc, (d % 2) * 128:(d % 2 + 1) * 128], V(yo.ap[:, kc, blk * 512:(blk + 1) * 512], (yo.key,)),
                             start=(kc == 0), stop=(kc == 7))
                    tok = slice(P["t0"] + blk * 512, P["t0"] + blk * 512 + 512)
                    k.stt(x[:, d, tok], pt, modcol(l, 2, d, P["v"]), x[:, d, tok], ALU.mult, ALU.add)

        ms = arena_mark()
        xbc = [alloc([128, 256], BF16, "xbc") for _ in range(10)]
        stg = [alloc([128, 262], F32, "stg") for _ in range(2)]
        acc = [alloc([128, 256], F32, "acc") for _ in range(2)]
        xsT = [alloc([128, 1024], BF16, "xsT") for _ in range(2)]
        BT = [alloc([128, 128], BF16, "BT") for _ in range(2)]
        sz = [alloc([128, 1024], BF16, "sz") for _ in range(2)]
        dtr = alloc([128, 2, 32], F32, "dtr"); dt = alloc([128, 2, 32], F32, "dt"); dta = alloc([128, 2, 32], F32, "dta")
        Scol = alloc([128, 2, 32], F32, "Scol"); eU = alloc([128, 2, 32], F32, "eU"); dend = alloc([128, 2, 32], F32, "dend")
        dec = alloc([128, 2, 32], F32, "dec"); wdd = alloc([128, 2, 32], F32, "wdd"); Utot = alloc([128, 2, 32], F32, "Utot")
        xdt = [alloc([128, 1024], BF16, "xdt") for _ in range(2)]
        xdd = xdt
        xD = alloc([128, 1024], BF16, "xD")
        CBT = alloc([128, 2, 128], BF16, "CBT")
        segT = alloc([128, 8, 128], F32, "segT")
        LTs = [alloc([128, 8, 128], BF16, "LT") for _ in range(2)]
        MT = [[alloc([128, 8, 128], BF16, "MT") for _ in range(2)] for _ in range(2)]
        ysb = alloc([128, 1024], F32, "ysb")
        tmpy = Tile(segT.ap.rearrange("p a b -> p (a b)"), segT.key)
        gn = alloc([128, 1024], BF16, "gn")
        junk = gn
        ssq = alloc([128, 2], F32, "ssq")
        Hs = [alloc([128, 512], F32, "Hs") for _ in range(2)]
        Hb16 = [[alloc([128, 512], BF16, "Hb16") for _ in range(2)] for _ in range(2)]
        Hent_b = [alloc([128, 512], BF16, "Hentb") for _ in range(8)] if isB else None
        stF = alloc([128, 4, 2, 64], F32, "stF")
        Dbc = V(cols.ap[:, ec["d"]:ec["d"] + 16].unsqueeze(2).broadcast_to([128, 16, 64]), (cols.key,))

        def v3(t, a):
            return V(t.ap.rearrange("p (a b) -> p a b", a=a), (t.key,))

        def group_prep(gi, need_z):
            g0 = gi * 256
            seq0 = (g0 // L) * L
            lo, hi = max(seq0, g0 - 2), min(seq0 + L, g0 + 257)
            n_in = hi - lo
            so = lo - (g0 - 2)
            specs = []
            if need_z:
                specs += [([(w_in_e[j][:, 0:512], 0, 512)], 8, 512), ([(w_in_e[j][:, 512:1024], 0, 512)], 8, 512)]
            specs += [([(w_in_e[j][:, 1024:1536], 0, 512)], 8, 512), ([(w_in_e[j][:, 1536:2048], 0, 512)], 8, 512),
                      ([(w_in_e[j][:, 2048:2336], 0, 288)], 8, 288)]
            wq = WSeq(specs)
            wi = 0
            if need_z:
                for half in range(2):
                    w = wq.get(wi); wi += 1
                    for tt in range(2):
                        pt = V(B3.ap[:, (tt % 2) * 512:(tt % 2) * 512 + 512], (B3.key,))
                        for kc in range(8):
                            k.mm(pt, h[:, kc, g0 + tt * 128:g0 + (tt + 1) * 128], w[:, kc, :], start=(kc == 0), stop=(kc == 7))
                        k.act(sz[tt][:, half * 512:(half + 1) * 512], pt, AF.Silu)
            pend_silu = []
            for c in range(10):
                if c % 4 == 0:
                    w = wq.get(wi); wi += 1
                st_, ac_ = stg[c % 2], acc[c % 2]
                pt = V(B0.ap[:, (c % 2) * 512:(c % 2) * 512 + n_in], (B0.key,))
                for kc in range(8):
                    k.mm(pt, w[:, kc, (c % 4) * 128:(c % 4 + 1) * 128], h[:, kc, lo:hi], start=(kc == 0), stop=(kc == 7))
                k.memset(st_.v(), 0.0)
                k.act(st_[:, so:so + n_in], pt, AF.Copy)
                cw, cb = ec["cw"] + c, ec["cb"] + c
                k.act(ac_.v(), st_[:, 0:256], AF.Identity, bias=cols[:, cb:cb + 1], scale=cols[:, cw:cw + 1])
                if pend_silu:
                    pend_silu.pop()()
                for kk in range(1, 4):
                    k.stt(ac_.v(), st_[:, kk:kk + 256], cols[:, cw + 10 * kk:cw + 10 * kk + 1], ac_.v(), ALU.mult, ALU.add)
                pend_silu.append(lambda c=c, ac_=ac_: k.act(xbc[c].v(), ac_.v(), AF.Silu))
            pend_silu.pop()()
            if int(os.environ.get("PREP_LEVEL", "9")) < 3:
                return
            pdt = V(B1.ap[:, 0:64].rearrange("p (a b) -> p a b", a=2), (B1.key,))
            pS = V(B1.ap[:, 64:128].rearrange("p (a b) -> p a b", a=2), (B1.key,))
            pU = V(B1.ap[:, 128:192].rearrange("p (a b) -> p a b", a=2), (B1.key,))
            steps = []
            def s1():
                for tt in range(2):
                    for kc in range(8):
                        k.mm(V(B1.ap[:, tt * 32:(tt + 1) * 32], (B1.key,)), h[:, kc, g0 + tt * 128:g0 + (tt + 1) * 128], w[:, kc, 256:288],
                             start=(kc == 0), stop=(kc == 7))
            steps.append(s1)
            steps.append(lambda: k.tt(dtr.v(), pdt, V(cols.ap[:, ec["dtb"]:ec["dtb"] + 32].unsqueeze(1).broadcast_to([128, 2, 32]), (cols.key,)), ALU.add))
            steps.append(lambda: k.act(dtr.v(), dtr.v(), AF.Exp))
            steps.append(lambda: k.act(dt.v(), dtr.v(), AF.Ln, bias=1.0))
            steps.append(lambda: k.tt(dta.v(), dt.v(), V(cols.ap[:, ec["abc"]:ec["abc"] + 32].unsqueeze(1).broadcast_to([128, 2, 32]), (cols.key,)), ALU.mult))
            def s6():
                for tt in range(2):
                    for dr in range(2):
                        k.mm(V(B1.ap[:, 64 + tt * 32 + dr * 16:64 + tt * 32 + dr * 16 + 16], (B1.key,)), Tdir[dr], dta[:, tt, dr * 16:(dr + 1) * 16])
                    k.mm(V(B1.ap[:, 128 + tt * 32:128 + (tt + 1) * 32], (B1.key,)), ones, dta[:, tt, :])
            steps.append(s6)
            steps.append(lambda: k.copy(Scol.v(), pS))
            steps.append(lambda: k.copy(Utot.v(), pU))
            steps.append(lambda: k.act(eU.v(), Scol.v(), AF.Exp))
            steps.append(lambda: k.act(dec.v(), Utot.v(), AF.Exp))
            steps.append(lambda: k.tt(dend.v(), Utot.v(), Scol.v(), ALU.subtract))
            steps.append(lambda: k.act(dend.v(), dend.v(), AF.Exp))
            steps.append(lambda: k.tt(wdd.v(), dend.v(), dt.v(), ALU.mult))
            for st_i, st_f in enumerate(steps):
                if st_i < int(os.environ.get("PREP_STEPS", "99")):
                    st_f()
            if int(os.environ.get("PREP_LEVEL", "9")) < 4:
                return
            for tt in range(2):
                for c in range(8):
                    k.tr(V(P3bf.ap[:, c * 128:(c + 1) * 128], (B3.key,)), xbc[c][:, tt * 128:(tt + 1) * 128], identb)
                k.tr(V(P3bf.ap[:, 1024:1152], (B3.key,)), xbc[8][:, tt * 128:(tt + 1) * 128], identb)
                k.act(xsT[tt].v(), V(P3bf.ap[:, 0:1024], (B3.key,)), AF.Copy)
                k.act(BT[tt].v(), V(P3bf.ap[:, 1024:1152], (B3.key,)), AF.Copy)

        def bc16(t, tt, dr):
            return V(t.ap[:, tt, dr * 16:(dr + 1) * 16].unsqueeze(2).broadcast_to([128, 16, 64]), (t.key,))

        SSD_LEVEL = int(os.environ.get("SSD_LEVEL", "3"))

        def chunk_states(tt, dr, pst):
            if SSD_LEVEL < 2:
                return
            k.tt(v3(xdd[dr], 16), v3(xsT[tt], 16), bc16(wdd, tt, dr), ALU.mult, eng="dve")
            for g in range(2):
                k.mm(V(pst.ap[g * 64:(g + 1) * 64, :], pst.keys), BT[tt][:, g * 64:(g + 1) * 64], xdd[dr][:, g * 512:(g + 1) * 512])

        def state_step(Ht, tt, dr, pst, have):
            if SSD_LEVEL < 2:
                return
            if not have:
                k.copy(Ht.v(), pst)
                return
            for g in range(2):
                hv = V(Ht.ap[g * 64:(g + 1) * 64, :].rearrange("p (a b) -> p a b", a=8), (Ht.key,))
                dv = V(dec.ap[g * 64:(g + 1) * 64, tt, dr * 16 + g * 8:dr * 16 + g * 8 + 8].unsqueeze(2).broadcast_to([64, 8, 64]), (dec.key,))
                k.tt(hv, hv, dv, ALU.mult)
            k.tt(Ht.v(), Ht.v(), pst, ALU.add)

        def chunk_y(gi, tt, ent):
            tok = slice(gi * 256 + tt * 128, gi * 256 + (tt + 1) * 128)
            tl = slice(tt * 128, (tt + 1) * 128)
            if SSD_LEVEL < 3:
                return
            YS = int(os.environ.get("Y_STEPS", "99"))
            for g in range(2):
                k.mm(V(B3.ap[:, g * 512:g * 512 + 128], (B3.key,)), xbc[8][g * 64:(g + 1) * 64, tl], xbc[9][g * 64:(g + 1) * 64, tl])
            for g in range(2):
                k.copy(CBT[:, g, :], V(B3.ap[:, g * 512:g * 512 + 128], (B3.key,)))
            DB = debug and (not isB) and gi == 0 and tt == 0
            if DB:
                dbg("CBT", V(CBT.ap.rearrange("p a b -> p (a b)"), (CBT.key,)), [128, 256])
                dbg("xsT", xsT[tt].v(), [128, 1024])
                dbg("dt", V(dt.ap.rearrange("p a b -> p (a b)"), (dt.key,)), [128, 64])
                dbg("Scol", V(Scol.ap.rearrange("p a b -> p (a b)"), (Scol.key,)), [128, 64])
            if YS < 2:
                return
            for dr in range(2):
                k.tt(v3(xdt[dr], 16), v3(xsT[tt], 16), bc16(dt, tt, dr), ALU.mult, eng="dve")
            k.tt(v3(xD, 16), v3(xsT[tt], 16), Dbc, ALU.mult, eng="dve")
            if YS < 3:
                return
            its = [(0, 0), (0, 1), (1, 0), (1, 1)]

            def stage_a(i):
                g, dr = its[i]
                pb = big[dr]
                pbv = V(pb.ap.rearrange("p (a b) -> p a b", a=8), (pb.key,))
                for h8 in range(8):
                    hd = dr * 16 + g * 8 + h8
                    k.mm(V(pb.ap[:, h8 * 128:(h8 + 1) * 128], (pb.key,)), V(dta.ap[:, tt, hd:hd + 1].broadcast_to([128, 128]), (dta.key,)),
                         Tdir[dr], start=True, stop=False)
                    k.mm(V(pb.ap[:, h8 * 128:(h8 + 1) * 128], (pb.key,)), identb, maskb[dr], start=False, stop=True)
                hd0 = dr * 16 + g * 8
                for bk in range(2):
                    k.tt(segT[:, bk * 4:(bk + 1) * 4, :], V(pbv.ap[:, bk * 4:(bk + 1) * 4, :], pbv.keys),
                         V(Scol.ap[:, tt, hd0 + bk * 4:hd0 + bk * 4 + 4].unsqueeze(2).broadcast_to([128, 4, 128]), (Scol.key,)), ALU.subtract)
                k.act(LTs[i % 2].v(), segT.v(), AF.Exp)

            def stage_b(i):
                g, dr = its[i]
                k.tt(MT[g][dr].v(), LTs[i % 2].v(), V(CBT.ap[:, g, :].unsqueeze(1).broadcast_to([128, 8, 128]), (CBT.key,)), ALU.mult, eng="dve")
                if DB and g == 0:
                    dbg("MT%d" % dr, V(MT[g][dr].ap.rearrange("p a b -> p (a b)"), (MT[g][dr].key,)), [128, 1024])

            stage_a(0)
            for i in range(4):
                if i + 1 < 4:
                    stage_a(i + 1)
                stage_b(i)
                g, dr = its[i]
                if dr == 0 or YS < 4:
                    continue
                for h8 in range(8):
                    hsl = slice((g * 8 + h8) * 64, (g * 8 + h8 + 1) * 64)
                    yv = V(B2.ap[:, hsl], (B2.key,))
                    k.mm(yv, identb, xD[:, hsl], start=True, stop=False)
                    k.mm(yv, MT[g][0][:, h8, :], xdt[0][:, hsl], start=False, stop=False)
                    k.mm(yv, MT[g][1][:, h8, :], xdt[1][:, hsl], start=False, stop=True)
            if YS < 5:
                return
            for bk in range(2):
                k.act(ysb[:, bk * 512:(bk + 1) * 512], B2[:, bk * 512:(bk + 1) * 512], AF.Copy)
            if YS < 6:
                return
            if DB:
                dbg("ydiag", ysb.v(), [128, 1024])
            for dr in range(2):
                if ent[dr] is None:
                    continue
                for g in range(2):
                    k.mm(V(B3.ap[:, g * 512:(g + 1) * 512], (B3.key,)), xbc[9][g * 64:(g + 1) * 64, tl], ent[dr][g * 64:(g + 1) * 64, :])
                for bk in range(2):
                    k.tt(V(tmpy.ap[:, bk * 512:(bk + 1) * 512].rearrange("p (a b) -> p a b", a=8), (tmpy.key,)),
                         V(B3.ap[:, bk * 512:(bk + 1) * 512].rearrange("p (a b) -> p a b", a=8), (B3.key,)),
                         V(eU.ap[:, tt, dr * 16 + bk * 8:dr * 16 + bk * 8 + 8].unsqueeze(2).broadcast_to([128, 8, 64]), (eU.key,)), ALU.mult)
                k.tt(ysb.v(), ysb.v(), tmpy.v(), ALU.add, eng="dve")
            if DB:
                dbg("ysb", ysb.v(), [128, 1024])
            if YS < 7:
                return
            k.tt(ysb.v(), ysb.v(), sz[tt].v(), ALU.mult)
            k.memset(ssq[:, 0:1], 0.0)
            k.act(junk.v(), ysb.v(), AF.Square, accum=ssq[:, 0:1])
            k.act(ssq[:, 1:2], ssq[:, 0:1], AF.Ln, bias=EPS, scale=1.0 / 1024)
            k.act(ssq[:, 1:2], ssq[:, 1:2], AF.Exp, scale=-0.5)
            k.act(gn.v(), ysb.v(), AF.Copy, scale=ssq[:, 1:2])
            for c in range(8):
                k.tr(V(P3bf.ap[:, c * 128:(c + 1) * 128], (B3.key,)), gn[:, c * 128:(c + 1) * 128], identb)
            k.tt(V(yo.ap[:, 0:8, tok], (yo.key,)), V(P3bf.ap[:, 0:1024].rearrange("p (a b) -> p a b", a=8), (B3.key,)),
                 V(cols.ap[:, ec["ng"]:ec["ng"] + 8].unsqueeze(2).broadcast_to([128, 8, 128]), (cols.key,)), ALU.mult)

        def write_state(Ht, dst):
            if SSD_LEVEL < 2:
                return
            for pr in range(4):
                k.tr(V(B3.ap[:, pr * 128:(pr + 1) * 128], (B3.key,)), Ht[:, pr * 128:(pr + 1) * 128], ident)
            k.copy(V(stF.ap.rearrange("p a b c -> p (a b c)"), (stF.key,)), V(B3.ap[:, 0:512], (B3.key,)))
            dv_ = dst.rearrange("(g pr h2) p n -> g (h2 p) pr n", g=2, pr=4)
            for g in range(2):
                k.dma(dv_[g], V(stF.ap[:, :, g, :], (stF.key,)), chan="st")

        pstates = [V(B3.ap[:, 0:512], (B3.key,)), V(B3.ap[:, 512:1024], (B3.key,))]
        SKIP_SSD = bool(os.environ.get("SKIP_SSD")); SKIP_ATT = bool(os.environ.get("SKIP_ATT"))
        if SKIP_SSD:
            pass
        elif not isB:
            for s in range(nseq):
                group_prep(s, True)
                chunk_states(0, 0, pstates[0]); state_step(Hs[0], 0, 0, pstates[0], False)
                k.copy(Hb16[0][1].v(), Hs[0].v(), eng="act")
                chunk_states(1, 1, pstates[1]); state_step(Hs[1], 1, 1, pstates[1], False)
                k.copy(Hb16[1][0].v(), Hs[1].v(), eng="act")
                chunk_states(1, 0, pstates[0]); state_step(Hs[0], 1, 0, pstates[0], True)
                write_state(Hs[0], sf_out[s, j])
                chunk_states(0, 1, pstates[1]); state_step(Hs[1], 0, 1, pstates[1], True)
                write_state(Hs[1], sb_out[s, j])
                chunk_y(s, 0, [None, Hb16[1][0]])
                chunk_y(s, 1, [Hb16[0][1], None])
        else:
            for dr, src in enumerate((ssd_f0, ssd_b0)):
                sv_ = src[j].rearrange("(g pr h2) p n -> g (h2 p) pr n", g=2, pr=4)
                for g in range(2):
                    k.dma(V(stF.ap[:, :, g, :], (stF.key,)), sv_[g], chan="ld")
                for pr in range(4):
                    k.tr(V(B3.ap[:, pr * 128:(pr + 1) * 128], (B3.key,)), V(stF.ap[:, pr, :, :].rearrange("p a b -> p (a b)"), (stF.key,)), ident)
                k.copy(Hs[dr].v(), V(B3.ap[:, 0:512], (B3.key,)))
            for gi in (3, 2, 1, 0):
                group_prep(gi, False)
                for tt in (1, 0):
                    k.copy(Hent_b[gi * 2 + tt].v(), Hs[1].v(), eng="act")
                    if gi * 2 + tt > 0:
                        chunk_states(tt, 1, pstates[tt]); state_step(Hs[1], tt, 1, pstates[tt], True)
            for gi in range(4):
                group_prep(gi, True)
                for tt in range(2):
                    k.copy(Hb16[0][tt].v(), Hs[0].v(), eng="act")
                    if gi * 2 + tt < 7:
                        chunk_states(tt, 0, pstates[tt]); state_step(Hs[0], tt, 0, pstates[tt], True)
                for tt in range(2):
                    chunk_y(gi, tt, [Hb16[0][tt], Hent_b[gi * 2 + tt]])
        if not SKIP_SSD:
            out_proj(0)
        arena_reset(ms)

        nkt = 12 if isB else 8
        nk = nkt * 128
        koff = 4 if isB else 0
        qT = [alloc([128, 1024], BF16, "qT") for _ in range(4)]
        kT = [alloc([128, nk], BF16, "kT") for _ in range(4)]
        vaug = alloc([128, nkt, 4, 130], BF16, "vaug")
        PT = [alloc([128, 512], BF16, "PT") for _ in range(3)]
        raw = [alloc([128, 512], BF16, "raw") for _ in range(2)]
        t1 = alloc([128, 512], F32, "t1"); t2 = alloc([128, 512], F32, "t2")
        ost = alloc([128, 512], F32, "ost")
        o_t = alloc([128, 4, 128], F32, "o_t"); o_n = alloc([128, 4, 128], BF16, "o_n")
        ojunk = Tile(t1.ap.rearrange("p (a b) -> p a b", a=4), t1.key)
        Osb = [alloc([128, 4, 130], F32, "Osb") for _ in range(2)]
        rs = alloc([128, 2, 4], F32, "rs"); rs2 = alloc([128, 2, 4], F32, "rs2"); sso = alloc([128, 2, 4], F32, "sso")
        if isB:
            rope = alloc([128, 2, 1024], F32, "rope")
            k.dma(rope.v(), ropetab.rearrange("a p n -> p a n"), chan="ld")
            ckst = alloc([128, 4, 512], F32, "ckst")
        Sbank = [Tile(B0.ap[:, 0:512], "B0_lo"), Tile(B0.ap[:, 512:1024], "B0_hi")]
        Sb3 = [Sbank[0], Sbank[1], Tile(B3.ap[:, 512:1024], B3.key)]
        for hg in range(0 if SKIP_ATT else 2):
            specs = [([(w_in_e[j][:, c0 + hg * 512:c0 + (hg + 1) * 512], 0, 512)], 8, 512) for c0 in (C_Q0, C_K0, C_V0)]
            wq = WSeq(specs)
            wQ, wK, wV = wq.get(0), wq.get(1), wq.get(2)
            k.memset(V(vaug.ap[:, :, :, 128:130], (vaug.key,)), 1.0)
            if isB:
                k.dma(ckst.v(), cache_k[j][:, hg * 512:(hg + 1) * 512].rearrange("(a p) n -> p a n", p=128), chan="ld")
                for kt in range(4):
                    k.dma(V(vaug.ap[:, kt, :, 0:128], (vaug.key,)),
                          cache_v[j][kt * 128:(kt + 1) * 128, hg * 512:(hg + 1) * 512].rearrange("p (a b) -> p a b", a=4), chan="cv", q="pool")
                    for hh in range(4):
                        k.tr(V(B3.ap[:, hh * 128:(hh + 1) * 128], (B3.key,)), ckst[:, kt, hh * 128:(hh + 1) * 128], ident)
                    for hh in range(4):
                        k.copy(kT[hh][:, kt * 128:(kt + 1) * 128], V(B3.ap[:, hh * 128:(hh + 1) * 128], (B3.key,)), eng="act")
            for which, (wt, dstT, doff) in enumerate(((wQ, qT, 0), (wK, kT, koff * 128))):
                for hh in range(4):
                    for blk in range(2):
                        pt = Sbank[blk].v()
                        for kc in range(8):
                            k.mm(pt, wt[:, kc, hh * 128:(hh + 1) * 128], h[:, kc, blk * 512:(blk + 1) * 512], start=(kc == 0), stop=(kc == 7))
                        dst = dstT[hh][:, doff + blk * 512:doff + (blk + 1) * 512]
                        if not isB:
                            k.act(dst, pt, AF.Copy)
                        else:
                            rw = raw[blk]
                            k.act(rw.v(), pt, AF.Copy)
                            p2 = V(B1.ap[:, blk * 512:(blk + 1) * 512], (B1.key,))
                            k.mm(p2, Rb, rw.v())
                            k.tt(t1.v(), rw.v(), rope[:, 0, blk * 512:(blk + 1) * 512], ALU.mult, eng="dve")
                            k.tt(t2.v(), p2, rope[:, 1, blk * 512:(blk + 1) * 512], ALU.mult)
                            k.tt(dst, t1.v(), t2.v(), ALU.add)
            for t in range(8):
                pt = V(B2.ap[:, (t % 2) * 512:(t % 2) * 512 + 512], (B2.key,))
                for kc in range(8):
                    k.mm(pt, h[:, kc, t * 128:(t + 1) * 128], wV[:, kc, :], start=(kc == 0), stop=(kc == 7))
                k.act(V(vaug.ap[:, koff + t, :, 0:128], (vaug.key,)), V(pt.ap.rearrange("p (a b) -> p a b", a=4), pt.keys), AF.Copy)
                if not isB:
                    s, tl = t // 2, (t % 2) * 128
                    k.copy(ost.v(), pt, eng="act")
                    k.dma(nv_out[s, j, tl:tl + 128, hg * 512:(hg + 1) * 512], ost.v(), chan="stv")
                    pk = V(B3.ap[:, (t % 2) * 512:(t % 2) * 512 + 512], (B3.key,))
                    for kc in range(8):
                        k.mm(pk, h[:, kc, t * 128:(t + 1) * 128], wK[:, kc, :], start=(kc == 0), stop=(kc == 7))
                    k.copy(ost.v(), pk)
                    k.dma(nk_out[s, j, tl:tl + 128, hg * 512:(hg + 1) * 512], ost.v(), chan="stk")
            if isB:
                qblocks = [(0, 512, list(range(12))), (512, 512, list(range(12)))]
            else:
                qblocks = [(s * 256, 256, [2 * s, 2 * s + 1]) for s in range(4)]
            pti = 0
            oslots = [V(B1.ap[:, 0:129], (B1.key,)), V(B1.ap[:, 512:641], (B1.key,)),
                      V(B2.ap[:, 0:129], (B2.key,)), V(B2.ap[:, 512:641], (B2.key,))]
            for hh in range(4):
                for (q0, nq, kts) in qblocks:
                    nqt = nq // 128
                    for c in range(2):
                        def s_mm(ki, kt):
                            S = Sb3[ki % 3]
                            k.mm(V(S.ap[:, 0:nq], (S.key,)), kT[hh][c * 64:(c + 1) * 64, kt * 128:(kt + 1) * 128], qT[hh][c * 64:(c + 1) * 64, q0:q0 + nq])
                        s_mm(0, kts[0])
                        if len(kts) > 1:
                            s_mm(1, kts[1])
                        for ki, kt in enumerate(kts):
                            S = Sb3[ki % 3]
                            pt_ = PT[pti % 3]; pti += 1
                            k.act(pt_[:, 0:nq], V(S.ap[:, 0:nq], (S.key,)), AF.Exp, scale=ATT_SCALE)
                            if ki + 2 < len(kts):
                                s_mm(ki + 2, kts[ki + 2])
                            for qt in range(nqt):
                                k.mm(oslots[qt], pt_[:, qt * 128:(qt + 1) * 128],
                                     V(vaug.ap[:, kt, hh, 0:129], (vaug.key,)), start=(ki == 0), stop=(ki == len(kts) - 1))
                        for qt in range(nqt):
                            k.copy(Osb[c][:, qt, 0:129], oslots[qt])
                    for c in range(2):
                        k.recip(rs[:, c, 0:nqt], V(Osb[c].ap[:, 0:nqt, 128], (Osb[c].key,)))
                    k.act(rs2[:, 0, 0:nqt], rs[:, 0, 0:nqt], AF.Copy)
                    k.act(rs2[:, 1, 0:nqt], rs[:, 1, 0:nqt], AF.Copy, scale=ccol("nlam"))
                    o3 = V(o_t.ap[:, 0:nqt, :], (o_t.key,))
                    k.tt(o3, Osb[0][:, 0:nqt, 0:128], V(rs2.ap[:, 0, 0:nqt].unsqueeze(2).broadcast_to([128, nqt, 128]), (rs2.key,)), ALU.mult)
                    k.tt(V(ojunk.ap[:, 0:nqt, :], (ojunk.key,)), Osb[1][:, 0:nqt, 0:128],
                         V(rs2.ap[:, 1, 0:nqt].unsqueeze(2).broadcast_to([128, nqt, 128]), (rs2.key,)), ALU.mult)
                    k.tt(o3, o3, V(ojunk.ap[:, 0:nqt, :], (ojunk.key,)), ALU.add)
                    k.tt(V(ojunk.ap[:, 0:nqt, :], (ojunk.key,)), o3, o3, ALU.mult)
                    k.op("dve", (lambda nqt=nqt: nc.vector.reduce_sum(out=sso.ap[:, 0, 0:nqt], in_=ojunk.ap[:, 0:nqt, :], axis=mybir.AxisListType.X)),
                         reads=[ojunk.key], writes=[sso.key], osize=nqt)
                    k.act(sso[:, 1, 0:nqt], sso[:, 0, 0:nqt], AF.Ln, bias=EPS, scale=1.0 / 128)
                    k.act(sso[:, 1, 0:nqt], sso[:, 1, 0:nqt], AF.Exp, scale=-0.5)
                    k.tt(V(o_n.ap[:, 0:nqt, :], (o_n.key,)), o3, V(sso.ap[:, 1, 0:nqt].unsqueeze(2).broadcast_to([128, nqt, 128]), (sso.key,)), ALU.mult)
                    for qt in range(nqt):
                        k.tr(V(P3bf.ap[:, qt * 128:(qt + 1) * 128], (B3.key,)), o_n[:, qt, :], identb)
                    k.ts(V(yo.ap[:, hg * 4 + hh, q0:q0 + nq].rearrange("p (a b) -> p a b", a=nqt), (yo.key,)),
                         V(P3bf.ap[:, 0:nq].rearrange("p (a b) -> p a b", a=nqt), (B3.key,)), ccol("sgl"), None, ALU.mult)
        if not SKIP_ATT:
            out_proj(8, pts=[Sbank[0].v(), Sbank[1].v()])
        arena_reset(m0)

    def phase_odd(l, P, pc):
        j = l // 2
        m0 = arena_mark()
        h = alloc([128, 8, 1024], BF16, "h")
        phase_norm(l, gcols_mix[l], 0, 1, P, h)
        nseq, L = P["nseq"], P["L"]
        gg = [alloc([128, 1024], BF16, "gg") for _ in range(8)]
        xr = [alloc([128, 1024], BF16, "xr") for _ in range(8)]
        stg = alloc([128, nseq, L + 3], F32, "stg")
        k.memset(stg.v(), 0.0)
        bd = alloc([128, 4, 8, 128], BF16, "bd")
        k.memset(bd.v(), 0.0)
        for g, (src, dr) in enumerate(((lru_wa, 0), (lru_wx, 0), (lru_wa, 1), (lru_wx, 1))):
            sv = src[j, dr].rearrange("(c two) kk jj -> two kk c jj", two=2)
            for half in range(2):
                k.dma(V(bd.ap[half * 64:(half + 1) * 64, g, :, half * 64:(half + 1) * 64], (bd.key,)), sv[half], chan="bd", q="pool")
        tAll = alloc([128, 1024], F32, "tAll")
        tA = [Tile(tAll.ap[:, i * 512:(i + 1) * 512], tAll.key) for i in range(2)]
        specs = [([(lru_w_in[j][:, g * 512:(g + 1) * 512], 0, 512)], 8, 512) for g in range(4)]
        specs += [([(lru_w_out[j][:, g * 512:(g + 1) * 512], 0, 512)], 8, 512) for g in range(2)]
        wq = WSeq(specs)
        acc = Tile(tAll.ap.rearrange("p (a b) -> p a b", a=nseq), tAll.key)
        for c in range(8):
            w = wq.get(c // 4)
            for blk in range(2):
                pt = ps[blk]
                for kc in range(8):
                    k.mm(pt.v(), w[:, kc, (c % 4) * 128:(c % 4 + 1) * 128], h[:, kc, blk * 512:(blk + 1) * 512], start=(kc == 0), stop=(kc == 7))
                a = tA[blk]
                k.act(a.v(), pt.v(), AF.Square)
                k.ts(a.v(), a.v(), 0.044715, 1.0, ALU.mult, ALU.add)
                k.tt(a.v(), a.v(), pt.v(), ALU.mult)
                k.act(a.v(), a.v(), AF.Sigmoid, scale=2.0 * 0.7978845608028654)
                k.tt(gg[c][:, blk * 512:(blk + 1) * 512], a.v(), pt.v(), ALU.mult)
        for c in range(8):
            w = wq.get(2 + c // 4)
            pl = [ps[2], ps[3]]
            for blk in range(2):
                for kc in range(8):
                    k.mm(pl[blk].v(), w[:, kc, (c % 4) * 128:(c % 4 + 1) * 128], h[:, kc, blk * 512:(blk + 1) * 512], start=(kc == 0), stop=(kc == 7))
            conv_fm(pl, P, stg, 4, 2, pc["cw"] + c, 8, pc["cb"] + c, acc)
            k.copy(V(xr[c].ap.rearrange("p (a b) -> p a b", a=nseq), (xr[c].key,)), acc.v())
            if c == 0 and l == 1:
                dbg("xr0_%d" % P["v"], xr[0].v(), [128, 1024])
                dbg("gg0_%d" % P["v"], gg[0].v(), [128, 1024])
                dbg("h0_%d" % P["v"], h[:, 0, :], [128, 1024])
        rr = alloc([128, 1024], F32, "rr"); ii = alloc([128, 1024], F32, "ii")
        aa = alloc([128, 1024], F32, "aa"); uu = alloc([128, 1024], F32, "uu")
        hhb = [[alloc([128, 1024], F32, "hh") for _ in range(2)] for _ in range(2)]
        lst = alloc([128, 8, 2, NP_SEQ], F32, "lst")
        h0 = alloc([128, 2, 8], F32, "h0")
        USE_H0 = (P["v"] == 1) and not os.environ.get("NOH0")
        if USE_H0:
            for dr, src in enumerate((lru_f0, lru_b0)):
                k.dma(stage[0:8, :], rows(src[j], 8), chan="ld")
                k.tr(ps[7][:, 0:8], stage[0:8, :], V(cst.ap[0:8, 0, 0:8], (cst.key,)))
                k.copy(h0[:, dr, :], ps[7][:, 0:8])
            if debug:
                od = nc.dram_tensor("dbg_h0s", [128, 16], F32, kind="ExternalOutput").ap()
                k.dma(od, V(h0.ap.rearrange("p a b -> p (a b)"), (h0.key,)), chan="dbg")
        aaD = [aa, Tile(stg.ap.rearrange("p a b -> p (a b)")[:, 0:1024], stg.key)]
        uuD = [uu, Tile(tAll.ap, tAll.key)]

        def finish(c):
            hh = hhb[c % 2]
            k.tt(rr.v(), hh[0].v(), hh[1].v(), ALU.add)
            if P["v"] == 0:
                for s_ in range(nseq):
                    for dr_, col_ in ((0, (s_ + 1) * L - 1), (1, s_ * L)):
                        o_ap = lst.ap[:, c, dr_, s_:s_ + 1]
                        i_ap = hh[dr_].ap[:, col_:col_ + 1]
                        k.op("act", (lambda o_ap=o_ap, i_ap=i_ap: nc.scalar.activation(out=o_ap, in_=i_ap, func=AF.Copy)),
                             reads=[hh[dr_].key, rr.key], writes=[lst.key], osize=1)
            k.tt(h[:, c, :], rr.v(), gg[c].v(), ALU.mult)

        for c in range(8):
            hh = hhb[c % 2]
            for dr in range(2):
                for gi, dst in ((0, rr), (1, ii)):
                    g = dr * 2 + gi
                    bcolx = (pc["ba"] if gi == 0 else pc["bx"]) + dr * 8 + c
                    for blk in range(2):
                        pt = ps[(g * 2 + blk) % 4]
                        k.mm(pt.v(), V(bd.ap[:, g, c, :], (bd.key,)), xr[c][:, blk * 512:(blk + 1) * 512])
                        k.act(dst[:, blk * 512:(blk + 1) * 512], pt.v(), AF.Sigmoid, bias=cols[:, bcolx:bcolx + 1])
                lc = pc["nc8"] + dr * 8 + c
                k.act(aaD[dr].v(), rr.v(), AF.Exp, scale=cols[:, lc:lc + 1])
                k.act(uuD[dr].v(), aaD[dr].v(), AF.Square)
                k.act(uuD[dr].v(), uuD[dr].v(), AF.Identity, bias=1.0, scale=-1.0)
                k.act(uuD[dr].v(), uuD[dr].v(), AF.Sqrt)
                k.tt(uuD[dr].v(), uuD[dr].v(), ii.v(), ALU.mult)
                k.tt(uuD[dr].v(), uuD[dr].v(), xr[c].v(), ALU.mult)
                if c == 0 and l == 1 and dr == 0:
                    dbg("rr_%d" % P["v"], rr.v(), [128, 1024]); dbg("ii_%d" % P["v"], ii.v(), [128, 1024])
                    dbg("aa_%d" % P["v"], aaD[dr].v(), [128, 1024]); dbg("uu_%d" % P["v"], uuD[dr].v(), [128, 1024])
                for s in range(nseq):
                    sl = slice(s * L, (s + 1) * L)
                    first = s * L if dr == 0 else (s + 1) * L - 1
                    if USE_H0:
                        k.act(uuD[dr][:, first:first + 1], aaD[dr][:, first:first + 1], AF.Identity, bias=uuD[dr][:, first:first + 1], scale=h0[:, dr, c:c + 1])
                    if dr == 0:
                        k.scan(hh[0][:, sl], aaD[dr][:, sl], uuD[dr][:, sl], 0.0)
                    else:
                        rs = slice((s + 1) * L - 1, s * L - 1 if s > 0 else None, -1)
                        k.scan(V(hh[1].ap[:, rs], (hh[1].key,)), V(aaD[dr].ap[:, rs], (aaD[dr].key,)), V(uuD[dr].ap[:, rs], (uuD[dr].key,)), 0.0)
            if c >= 1:
                finish(c - 1)
        finish(7)
        if P["v"] == 0:
            k.tr(ps[7][0:64, 0:128], V(lst.ap.rearrange("p a b c -> p (a b c)"), (lst.key,)), ident)
            lrow = alloc([64, 128], F32, "lrow")
            k.copy(lrow.v(), ps[7][0:64, 0:128])
            if debug:
                od = nc.dram_tensor("dbg_lst", [128, 64], F32, kind="ExternalOutput").ap()
                k.dma(od, V(lst.ap.rearrange("p a b c -> p (a b c)"), (lst.key,)), chan="dbg")
                od2 = nc.dram_tensor("dbg_lrow", [64, 128], F32, kind="ExternalOutput").ap()
                k.dma(od2, lrow.v(), chan="dbg")
            for dr, dst in enumerate((lf_out, lb_out)):
                for c in range(8):
                    r0 = c * 8 + dr * 4
                    k.dma(dst[:, j, c * 128:(c + 1) * 128], lrow[r0:r0 + 4, :], chan="st")
        if l == 1:
            dbg("yo0_%d" % P["v"], h[:, 0, :], [128, 1024])
            dbg("yo5_%d" % P["v"], h[:, 5, :], [128, 1024])
        for d in range(8):
            w = wq.get(4 + d // 4)
            for blk in range(2):
                pt = ps[4 + (d * 2 + blk) % 2]
                for kc in range(8):
                    k.mm(pt.v(), w[:, kc, (d % 4) * 128:(d % 4 + 1) * 128], h[:, kc, blk * 512:(blk + 1) * 512], start=(kc == 0), stop=(kc == 7))
                resid_add(l, 2, P, d, blk, pt)
        if l == 1:
            dbg("xm0_%d" % P["v"], x[:, 0, P["t0"]:P["t0"] + 1024], [128, 1024])
        arena_reset(m0)

    gcols_mix, gcols_ffn, fcw, fcb = [], [], [], []
    ocols = {}
    ecols = {}
    for l in range(nlayers):
        gcols_mix.append(load_cols(rows(norm_mix_g[l], 8), 8))
        gcols_ffn.append(load_cols(rows(norm_ffn_g[l], 8), 8))
        fcw.append(load_cols(ffn_conv_w[l].rearrange("w (r c) -> (w r) c", c=128), 3 * 2 * NJ))
        fcb.append(load_cols(rows(ffn_conv_b[l], 2 * NJ), 2 * NJ))
        if l % 2 == 0:
            j = l // 2
            ec = {}
            ec["cw"] = load_cols(conv_w_e[j].rearrange("w (r c) -> (w r) c", c=128), 40)
            ec["cb"] = load_cols(rows(conv_b_e[j], 10), 10)
            ec["ng"] = load_cols(rows(ssd_norm_g[j], 8), 8)
            dn = load_cols(rows(diff_norm_g[j], 1), 1)
            base = colstate["n"]
            colstate["n"] += 32 + 32 + 16 + 256 + 32 + 8
            assert colstate["n"] <= NCOL
            ec["dtb"], alg, ec["d"], lp = base, base + 32, base + 64, base + 80
            ec["abc"] = base + 336
            sc = base + 368
            k.dma(cols[:, ec["dtb"]:ec["dtb"] + 32], dt_bias[j:j + 1, :].partition_broadcast(128), chan="ld")
            k.dma(cols[:, alg:alg + 32], a_log[j:j + 1, :].partition_broadcast(128), chan="ld")
            k.dma(cols[:, ec["d"]:ec["d"] + 16], ssd_d[j:j + 1, :].partition_broadcast(128), chan="ld")
            k.dma(cols[:, lp:lp + 256], diff_lambda[j:j + 1, :].partition_broadcast(128), chan="ld")
            k.act(cols[:, ec["abc"]:ec["abc"] + 32], cols[:, alg:alg + 32], AF.Exp)
            k.ts(cols[:, ec["abc"]:ec["abc"] + 32], cols[:, ec["abc"]:ec["abc"] + 32], -1.0, None, ALU.mult)
            lam_init = 0.8 - 0.6 * math.exp(-0.3 * l)
            k.tt(cols[:, lp:lp + 64], cols[:, lp:lp + 64], cols[:, lp + 64:lp + 128], ALU.mult)
            k.tt(cols[:, lp + 128:lp + 192], cols[:, lp + 128:lp + 192], cols[:, lp + 192:lp + 256], ALU.mult)
            k.memset(cols[:, sc:sc + 2], 0.0)
            k.act(cols[:, lp + 64:lp + 128], cols[:, lp:lp + 64], AF.Copy, accum=cols[:, sc:sc + 1])
            k.act(cols[:, lp + 192:lp + 256], cols[:, lp + 128:lp + 192], AF.Copy, accum=cols[:, sc + 1:sc + 2])
            k.act(cols[:, sc + 2:sc + 4], cols[:, sc:sc + 2], AF.Exp)
            k.tt(cols[:, sc + 4:sc + 5], cols[:, sc + 2:sc + 3], cols[:, sc + 3:sc + 4], ALU.subtract)
            k.act(cols[:, sc + 5:sc + 6], cols[:, sc + 4:sc + 5], AF.Identity, bias=-lam_init, scale=-1.0)
            k.act(cols[:, sc + 6:sc + 7], cols[:, dn:dn + 1], AF.Copy, scale=(1.0 - lam_init))
            ec["nlam"], ec["sgl"] = sc + 5, sc + 6
            ecols[l] = ec
        if l % 2 == 1:
            j = l // 2
            pc = {}
            pc["cw"] = load_cols(lru_conv_w[j].rearrange("w (r c) -> (w r) c", c=128), 32)
            pc["cb"] = load_cols(rows(lru_conv_b[j], 8), 8)
            pc["ba"] = load_cols(lru_ba[j].rearrange("d (r c) -> (d r) c", c=128), 16)
            pc["bx"] = load_cols(lru_bx[j].rearrange("d (r c) -> (d r) c", c=128), 16)
            lam = load_cols(lru_lambda[j].rearrange("d (r c) -> (d r) c", c=128), 16)
            pc["nc8"] = colstate["n"]
            colstate["n"] += 16
            dst = cols[:, pc["nc8"]:pc["nc8"] + 16]
            k.act(dst, cols[:, lam:lam + 16], AF.Exp, scale=-1.0)
            k.act(dst, dst, AF.Ln, bias=1.0)
            k.ts(dst, dst, -8.0, None, ALU.mult)
            ocols[l] = pc
    gfin = load_cols(rows(final_norm_g, 8), 8)

    for l in range(nlayers):
        for P in PASSES:
            if l % 2 == 1:
                phase_odd(l, P, ocols[l])
            else:
                if not os.environ.get("DISABLE_EVEN"):
                    phase_even(l, P, ecols[l])
            phase_ffn(l, P, fcw[l], fcb[l])

    for P in PASSES:
        m0 = arena_mark()
        hf = alloc([128, 8, 1024], F32, "hf")
        phase_norm(0, gfin, 0, 0, P, None, final=True, hf=hf)
        ost = [alloc([128, D], F32, "ost") for _ in range(2)]
        for t in range(8):
            o = ost[t % 2]
            for half in range(2):
                pt = ps[2 + half]
                for c4 in range(4):
                    c = half * 4 + c4
                    k.tr(pt[:, c4 * 128:(c4 + 1) * 128], hf[:, c, t * 128:(t + 1) * 128], ident)
                k.copy(o[:, half * 512:(half + 1) * 512], pt.v(), eng=("act" if half else "dve"))
            k.dma(y_out[P["t0"] + t * 128:P["t0"] + (t + 1) * 128, :], o.v(), chan="st")
        arena_reset(m0)

    k.emit()
    return nc, es


def host_consts():
    c = np.zeros((10, 128, 128), np.float32)
    c[0] = np.eye(128)
    R = np.zeros((128, 128), np.float32)
    for m in range(128):
        d = m % 32
        if d < 16:
            R[m + 16, m] = -1.0
        else:
            R[m - 16, m] = 1.0
    c[1] = R
    jj, qq = np.meshgrid(np.arange(128), np.arange(128), indexing="ij")
    c[2] = (jj <= qq)
    c[3] = (jj >= qq)
    c[4] = np.where(qq >= jj, 0.0, -30000.0)
    c[5] = np.where(qq <= jj, 0.0, -30000.0)
    c[6] = 1.0
    t = np.arange(LS)
    row = (t // 64).astype(np.float32)
    col = (t % 64).astype(np.float32)
    freqs = (10000.0 ** (-np.arange(0, 32, 2, dtype=np.float32) / 32.0)).astype(np.float32)
    tab = np.zeros((2, 128, LS), np.float32)
    for p in range(128):
        d = p % 64
        pos = row if d < 32 else col
        f = freqs[(d % 32) % 16]
        ang = (pos * f).astype(np.float32)
        tab[0, p] = np.cos(ang)
        tab[1, p] = np.sin(ang)
    return c, tab


_CACHE = {}


def make_in_maps(inputs):
    consts, tab = host_consts()
    maps = []
    for i in range(8):
        m = {}
        m["xin"] = np.ascontiguousarray(np.concatenate(
            [inputs["x_prompt"][4 * i:4 * i + 4].reshape(4 * LP, D), inputs["x_sample"][i]], axis=0))
        m["cvec"] = np.ascontiguousarray(np.stack([inputs["c_ctx"], inputs["c"][i]], axis=0))
        m["cache_k"] = np.ascontiguousarray(inputs["cache_attn_k"][i].reshape(2, PAST, D))
        m["cache_v"] = np.ascontiguousarray(inputs["cache_attn_v"][i].reshape(2, PAST, D))
        m["ssd_f0"] = np.ascontiguousarray(inputs["state_ssd_fwd"][i])
        m["ssd_b0"] = np.ascontiguousarray(inputs["state_ssd_bwd"][i])
        m["lru_f0"] = np.ascontiguousarray(inputs["state_lru_fwd"][i])
        m["lru_b0"] = np.ascontiguousarray(inputs["state_lru_bwd"][i])
        for nm in ("w_mod", "b_mod", "norm_mix_g", "norm_ffn_g", "ssd_attn_w_in", "ssd_conv_w", "ssd_conv_b",
                   "ssd_norm_g", "diff_norm_g", "ssd_attn_w_out", "lru_w_in", "lru_conv_w", "lru_conv_b", "lru_wa",
                   "lru_ba", "lru_wx", "lru_bx", "lru_lambda", "lru_w_out", "ffn_w_up", "ffn_conv_w", "ffn_conv_b",
                   "ffn_w_down", "final_norm_g"):
            m[nm] = np.ascontiguousarray(inputs[nm])
        m["ssd_a_log"] = np.ascontiguousarray(inputs["ssd_a_log"].reshape(2, 32))
        m["ssd_dt_bias"] = np.ascontiguousarray(inputs["ssd_dt_bias"].reshape(2, 32))
        m["ssd_d"] = np.ascontiguousarray(inputs["ssd_d"])
        m["diff_lambda"] = np.ascontiguousarray(inputs["diff_lambda"].reshape(2, 256))
        m["consts"] = consts
        m["ropetab"] = tab
        maps.append(m)
    return maps


def kernel(**inputs):
    inputs = {k_: np.asarray(v, dtype=np.float32) for k_, v in inputs.items()}
    if "nc" not in _CACHE:
        _CACHE["nc"] = build_program()
    nc, _es = _CACHE["nc"]
    maps = make_in_maps(inputs)
    res = run_bass_kernel_spmd(nc, maps, core_ids=list(range(8)))
    R = res.results
    y = np.stack([r["y_out"] for r in R])
    y_prompt = y[:, :1024].reshape(32, LP, D)
    y_sample = y[:, 1024:].reshape(8, LS, D)
    nk = np.concatenate([r["nk_out"] for r in R], axis=0).reshape(32, 2, LP, 8, 2, 64)
    nv = np.concatenate([r["nv_out"] for r in R], axis=0).reshape(32, 2, LP, 8, 128)
    sf = np.concatenate([r["sf_out"] for r in R], axis=0)
    sb = np.concatenate([r["sb_out"] for r in R], axis=0)
    lf = np.concatenate([r["lf_out"] for r in R], axis=0)
    lb = np.concatenate([r["lb_out"] for r in R], axis=0)
    return (y_prompt, y_sample, nk, nv, sf, sb, lf, lb)
```
